# Optimizing a Trainium2 kernel written in Bass

```python
import math
import jax, jax.numpy as jnp
from jax import lax
import numpy as np

D_MODEL = 1024
BATCH = 8
SEQ = 4096
DEPTH = 1

D_MIX = D_MODEL
D_CONV = D_MIX // 2
D_RNN = D_MIX - D_CONV
N_RNN_HEADS = 8
RNN_HEAD_DIM = D_RNN // N_RNN_HEADS
CONV_WIDTH = 31
RNN_CONV_WIDTH = 4
RG_C = 8.0
D_FF = ((8 * D_MODEL // 3 + 127) // 128) * 128
N_SUBLAYERS = 3
MACARON_W = 0.5
EPS = 1e-6

kernel_name = "hymba_style_conformer_rglru_macaron_adaln_block"


def rmsnorm(x, g):
    xf = x.astype(jnp.float32)
    y = xf * lax.rsqrt(jnp.mean(xf * xf, axis=-1, keepdims=True) + EPS)
    return y.astype(x.dtype) * g


def layernorm(x, g, b):
    xf = x.astype(jnp.float32)
    mu = jnp.mean(xf, axis=-1, keepdims=True)
    var = jnp.mean(jnp.square(xf - mu), axis=-1, keepdims=True)
    y = (xf - mu) * lax.rsqrt(var + EPS)
    return y.astype(x.dtype) * g + b


def ada_rmsnorm(x, g, shift, scale):
    return rmsnorm(x, g) * (1.0 + scale[:, None, :]) + shift[:, None, :]


def causal_depthwise_conv(u, w, b):
    k = w.shape[0]
    y = lax.conv_general_dilated(
        u, w[:, None, :], window_strides=(1,), padding=[(k - 1, 0)],
        dimension_numbers=("NWC", "WIO", "NWC"), feature_group_count=u.shape[-1])
    return y + b


def swiglu_ffn(h, w_in, w_out):
    gate, up = jnp.split(h @ w_in, 2, axis=-1)
    return (jax.nn.silu(gate) * up) @ w_out


def conformer_conv_group(u_val, u_gate, conv_w, conv_b, ln_g, ln_b):
    u = u_val * jax.nn.sigmoid(u_gate)
    u = causal_depthwise_conv(u, conv_w, conv_b)
    u = layernorm(u, ln_g, ln_b)
    return jax.nn.silu(u)


def _lru_combine(left, right):
    a_l, b_l = left
    a_r, b_r = right
    return a_l * a_r, a_r * b_l + b_r


def rglru_group(u_x, u_y, conv_w, conv_b, w_a, b_a, w_i, b_i, lru_lambda):
    bsz, seq, _ = u_x.shape
    xr = causal_depthwise_conv(u_x, conv_w, conv_b)
    xh = xr.reshape(bsz, seq, N_RNN_HEADS, RNN_HEAD_DIM)
    r = jax.nn.sigmoid(jnp.einsum("bshd,hde->bshe", xh, w_a).reshape(bsz, seq, D_RNN) + b_a)
    i = jax.nn.sigmoid(jnp.einsum("bshd,hde->bshe", xh, w_i).reshape(bsz, seq, D_RNN) + b_i)
    log_a = RG_C * r.astype(jnp.float32) * jax.nn.log_sigmoid(lru_lambda.astype(jnp.float32))
    a = jnp.exp(log_a)
    mult = jnp.sqrt(-jnp.expm1(2.0 * log_a))
    bterm = mult * (i.astype(jnp.float32) * xr.astype(jnp.float32))
    _, h = lax.associative_scan(_lru_combine, (a, bterm), axis=1)
    return jax.nn.gelu(u_y) * h.astype(u_x.dtype)


def setup_inputs(seed: int = 0) -> dict:
    key = jax.random.key(seed)
    ks = jax.random.split(key, 32)
    f32 = jnp.float32
    L = DEPTH
    nrm = lambda k, shape, s: jax.random.normal(k, shape, f32) * s
    gain = lambda k, shape: 1.0 + 0.05 * jax.random.normal(k, shape, f32)
    a0 = jax.random.uniform(ks[20], (L, D_RNN), f32, 0.9, 0.999)
    s = a0 ** (1.0 / RG_C)
    lru_lambda = jnp.log(s) - jnp.log1p(-s)
    return {
        "x": jax.random.normal(ks[0], (BATCH, SEQ, D_MODEL), f32),
        "c": jax.random.normal(ks[1], (BATCH, D_MODEL), f32),
        "w_mod": nrm(ks[2], (L, D_MODEL, 3 * N_SUBLAYERS * D_MODEL), 0.5 * D_MODEL ** -0.5),
        "b_mod": nrm(ks[3], (L, 3 * N_SUBLAYERS * D_MODEL), 0.02),
        "g_ffn1": gain(ks[4], (L, D_MODEL)),
        "w_ffn1_in": nrm(ks[5], (L, D_MODEL, 2 * D_FF), D_MODEL ** -0.5),
        "w_ffn1_out": nrm(ks[6], (L, D_FF, D_MODEL), D_FF ** -0.5),
        "g_mix": gain(ks[7], (L, D_MODEL)),
        "w_in": nrm(ks[8], (L, D_MODEL, 2 * D_CONV + 2 * D_RNN), D_MODEL ** -0.5),
        "conv_w": nrm(ks[9], (L, CONV_WIDTH, D_CONV), CONV_WIDTH ** -0.5),
        "conv_b": nrm(ks[10], (L, D_CONV), 0.02),
        "ln_g": gain(ks[11], (L, D_CONV)),
        "ln_b": nrm(ks[12], (L, D_CONV), 0.02),
        "rnn_conv_w": nrm(ks[13], (L, RNN_CONV_WIDTH, D_RNN), RNN_CONV_WIDTH ** -0.5),
        "rnn_conv_b": nrm(ks[14], (L, D_RNN), 0.02),
        "w_a": nrm(ks[15], (L, N_RNN_HEADS, RNN_HEAD_DIM, RNN_HEAD_DIM), RNN_HEAD_DIM ** -0.5),
        "b_a": nrm(ks[16], (L, D_RNN), 0.02),
        "w_i": nrm(ks[17], (L, N_RNN_HEADS, RNN_HEAD_DIM, RNN_HEAD_DIM), RNN_HEAD_DIM ** -0.5),
        "b_i": nrm(ks[18], (L, D_RNN), 0.02),
        "lru_lambda": lru_lambda,
        "w_out": nrm(ks[19], (L, D_MIX, D_MODEL), D_MIX ** -0.5),
        "g_ffn2": gain(ks[21], (L, D_MODEL)),
        "w_ffn2_in": nrm(ks[22], (L, D_MODEL, 2 * D_FF), D_MODEL ** -0.5),
        "w_ffn2_out": nrm(ks[23], (L, D_FF, D_MODEL), D_FF ** -0.5),
        "w_fmod": nrm(ks[24], (D_MODEL, 2 * D_MODEL), 0.5 * D_MODEL ** -0.5),
        "b_fmod": nrm(ks[25], (2 * D_MODEL,), 0.02),
        "g_final": gain(ks[26], (D_MODEL,)),
    }


def reference(x, c, w_mod, b_mod, g_ffn1, w_ffn1_in, w_ffn1_out, g_mix, w_in,
              conv_w, conv_b, ln_g, ln_b, rnn_conv_w, rnn_conv_b, w_a, b_a, w_i, b_i,
              lru_lambda, w_out, g_ffn2, w_ffn2_in, w_ffn2_out, w_fmod, b_fmod, g_final):
    c_act = jax.nn.silu(c)
    for l in range(DEPTH):
        mod = c_act @ w_mod[l] + b_mod[l]
        sh1, sc1, gt1, sh2, sc2, gt2, sh3, sc3, gt3 = jnp.split(mod, 3 * N_SUBLAYERS, axis=-1)

        h = ada_rmsnorm(x, g_ffn1[l], sh1, sc1)
        x = x + MACARON_W * gt1[:, None, :] * swiglu_ffn(h, w_ffn1_in[l], w_ffn1_out[l])

        h = ada_rmsnorm(x, g_mix[l], sh2, sc2)
        proj = h @ w_in[l]
        u_val, u_gate, u_x, u_y = jnp.split(
            proj, [D_CONV, 2 * D_CONV, 2 * D_CONV + D_RNN], axis=-1)
        y_conv = conformer_conv_group(u_val, u_gate, conv_w[l], conv_b[l], ln_g[l], ln_b[l])
        y_rnn = rglru_group(u_x, u_y, rnn_conv_w[l], rnn_conv_b[l], w_a[l], b_a[l],
                            w_i[l], b_i[l], lru_lambda[l])
        y_mix = jnp.concatenate([y_conv, y_rnn], axis=-1) @ w_out[l]
        x = x + gt2[:, None, :] * y_mix

        h = ada_rmsnorm(x, g_ffn2[l], sh3, sc3)
        x = x + MACARON_W * gt3[:, None, :] * swiglu_ffn(h, w_ffn2_in[l], w_ffn2_out[l])

    fmod = c_act @ w_fmod + b_fmod
    f_shift, f_scale = jnp.split(fmod, 2, axis=-1)
    return ada_rmsnorm(x, g_final, f_shift, f_scale)
```

```python
import numpy as np
import concourse.bass as bass
import concourse.mybir as mybir
from concourse.bass_utils import run_bass_kernel_spmd

F32 = mybir.dt.float32
BF16 = mybir.dt.bfloat16
AF = mybir.ActivationFunctionType
ALU = mybir.AluOpType

D = 1024
DFF = 2816
SEQ = 4096
T = 512
KC = 8
FC = 22
NCONV = 31
EPS = 1e-6
NS = 4
N_CORES = 8

V_G1, V_GM, V_G2, V_GF = 0, 8, 16, 24
V_BMOD = 32
V_CONVB = 120
V_LNG = 124
V_LNB = 128
V_RCB = 132
V_BA = 136
V_BI = 140
V_LAM = 144
V_RCW = 148
V_C = 164
NV = 172


class Sem:
    def __init__(self, h):
        self.h = h
        self.val = 0


class Res:
    __slots__ = ("w", "r")

    def __init__(self):
        self.w = None
        self.r = {}


class Eng:
    def __init__(self, name, sem, self_sync):
        self.name = name
        self.sem = sem
        self.prog = []
        self.seen = {}
        self.self_sync = self_sync


def emit(eng, fn, reads=(), writes=(), dma=None, ndma=1):
    waits = {}

    def need(tok):
        if tok is None:
            return
        s, v = tok
        if waits.get(s, 0) < v:
            waits[s] = v

    for r in reads:
        need(r.w)
    for w in writes:
        need(w.w)
        for s, v in w.r.items():
            need((s, v))
    wl = []
    for s, v in waits.items():
        if s is eng.sem and not eng.self_sync:
            continue
        if eng.seen.get(s, 0) >= v:
            continue
        eng.seen[s] = v
        wl.append((s, v))
    if dma is None:
        eng.sem.val += 1
        tok = (eng.sem, eng.sem.val)
        eng.prog.append((wl, fn, eng.sem, 1, False))
    else:
        dma.val += 16 * ndma
        tok = (dma, dma.val)
        eng.prog.append((wl, fn, dma, 16, True))
    for r in reads:
        if r.r.get(tok[0], 0) < tok[1]:
            r.r[tok[0]] = tok[1]
    for w in writes:
        w.w = tok
        w.r = {}
    return tok


def replay(eng, h):
    for wl, fn, sem, inc, is_dma in eng.prog:
        for s, v in wl:
            h.wait_ge(s.h, v)
        ins = fn(h)
        if is_dma:
            for i in ins:
                i.then_inc(sem.h, 16)
        else:
            ins[-1].then_inc(sem.h, 1)


def build_program(n_tiles=SEQ // T):
    import os
    S = n_tiles * T
    nc = bass.Bass("TRN2", target_bir_lowering=False)
    dt_in = lambda name, shape: nc.dram_tensor(name, shape, F32, kind="ExternalInput").ap()
    x_d = dt_in("x", [S, D])
    vecs_d = dt_in("vecs", [128, NV])
    ident_d = dt_in("ident", [128, 128])
    wabd_d = dt_in("wabd", [128, 512])
    wibd_d = dt_in("wibd", [128, 512])
    wmod_d = dt_in("w_mod", [D, 9 * D])
    wfmod_d = dt_in("w_fmod", [D, 2 * D])
    w1i_d = dt_in("w1i", [D, 2 * DFF])
    w1o_d = dt_in("w1o", [DFF, D])
    win_d = dt_in("win", [D, 2 * D])
    wout_d = dt_in("wout", [D, D])
    w2i_d = dt_in("w2i", [D, 2 * DFF])
    w2o_d = dt_in("w2o", [DFF, D])
    dcw_d = dt_in("dcw", [512, NCONV * 128])
    y_d = nc.dram_tensor("y", [S, D], F32, kind="ExternalOutput").ap()
    dt_sc = lambda name, shape: nc.dram_tensor(name, shape, BF16, kind="Internal").ap()
    b1i = dt_sc("b1i", [D, 2 * DFF])
    b1o = dt_sc("b1o", [DFF, D])
    bin_ = dt_sc("bin", [D, 2 * D])
    bout = dt_sc("bout", [D, D])
    b2i = dt_sc("b2i", [D, 2 * DFF])
    b2o = dt_sc("b2o", [DFF, D])
    bdc = dt_sc("bdc", [512, NCONV * 128])

    from contextlib import ExitStack
    es = ExitStack()
    with es:
        sb = lambda name, shape, dt=F32: es.enter_context(nc.sbuf_tensor("sb_" + name, shape, dt))
        newsem = lambda name: Sem(es.enter_context(nc.semaphore(name)))

        PE = Eng("pe", newsem("s_pe"), False)
        ACT = Eng("act", newsem("s_act"), True)
        DVE = Eng("dve", newsem("s_dve"), True)
        SP = Eng("sp", newsem("s_sp"), False)
        PL = Eng("pool", newsem("s_pool"), False)

        ident = sb("ident", [128, 128])
        ones_bf = sb("ones_bf", [128, 128], BF16)
        vecs = sb("vecs", [128, NV])
        modv = sb("modv", [128, 88])
        cact = sb("cact", [128, 8])
        gs1 = sb("gs1", [128, 8]); gth1 = sb("gth1", [128, 8])
        gs2 = sb("gs2", [128, 8])
        gs3 = sb("gs3", [128, 8]); gth3 = sb("gth3", [128, 8])
        gsf = sb("gsf", [128, 8])
        lam8 = sb("lam8", [128, 4]); lam16 = sb("lam16", [128, 4]); lamt = sb("lamt", [128, 4])
        wabd = sb("wabd", [128, 4, 128], BF16)
        wibd = sb("wibd", [128, 4, 128], BF16)
        hstate = sb("hstate", [128, 4])
        slots = [sb(f"slab{i}", [128, 8, 512], BF16) for i in range(NS)]
        xin = sb("xin", [128, 4, D])
        xT = sb("xT", [128, 8, T])
        xsq = [sb(f"xsq{i}", [128, T], BF16) for i in range(3)]
        hb = sb("hb", [128, 8, T], BF16)
        hid = sb("hid", [128, FC, T], BF16)
        sgb = [sb(f"sgb{i}", [128, T]) for i in range(2)]
        ntmp = [sb(f"ntmp{i}", [128, T]) for i in range(2)]
        rstd = sb("rstd", [128, T])
        ub = sb("ub", [128, 4, T + 30], BF16)
        ux = sb("ux", [128, 4, T + 3])
        gy = sb("gy", [128, 4, T], BF16)
        cv = sb("cv", [128, 4, T])
        cvsq = [sb(f"cvsq{i}", [128, T], BF16) for i in range(2)]
        cvb = [sb(f"cvb{i}", [128, T], BF16) for i in range(2)]
        meanb = sb("meanb", [128, T])
        varb = sb("varb", [128, T])
        xr = [sb(f"xr{i}", [128, T]) for i in range(2)]
        xrb = [sb(f"xrb{i}", [128, T], BF16) for i in range(2)]
        rbuf = [sb(f"rbuf{i}", [128, T]) for i in range(2)]
        abuf = [sb(f"abuf{i}", [128, T]) for i in range(2)]
        mbuf = [sb(f"mbuf{i}", [128, T]) for i in range(2)]
        btb = [sb(f"btb{i}", [128, T]) for i in range(2)]
        hs = [sb(f"hs{i}", [128, T]) for i in range(2)]
        oT = [sb(f"oT{i}", [128, T]) for i in range(2)]
        oout = sb("oout", [128, 4, D])
        ps = es.enter_context(nc.psum_tensor("ps", [128, 8, T], F32))

        R = lambda: Res()
        r_const = R()
        r_vecs = R(); r_modv = R(); r_cact = R(); r_derived = R(); r_hstate = [R() for _ in range(4)]
        r_gates = R()
        r_slot = [R() for _ in range(NS)]
        sem_slot = [newsem(f"s_slot{i}") for i in range(NS)]
        r_bank = [R() for _ in range(8)]
        r_xin = R(); sem_xin = newsem("s_xin")
        r_xT = [R() for _ in range(8)]
        r_xsq = [R() for _ in range(3)]
        r_hb = [R() for _ in range(8)]
        r_hid = [R() for _ in range(FC)]
        r_sgb = [R() for _ in range(2)]
        r_ntmp = [R() for _ in range(2)]
        r_rstd = R()
        r_ub = [R() for _ in range(4)]
        r_ux = [R() for _ in range(4)]
        r_gy = [R() for _ in range(4)]
        r_cv = [R() for _ in range(4)]
        r_cvsq = [R() for _ in range(2)]
        r_cvb = [R() for _ in range(2)]
        r_meanb = R(); r_varb = R()
        r_xr = [R() for _ in range(2)]; r_xrb = [R() for _ in range(2)]
        r_rbuf = [R() for _ in range(2)]; r_abuf = [R() for _ in range(2)]
        r_mbuf = [R() for _ in range(2)]; r_btb = [R() for _ in range(2)]
        r_hs = [R() for _ in range(2)]
        r_oT = [R() for _ in range(2)]
        r_oout = [R() for _ in range(8)]; sem_out = newsem("s_out")
        sem_const = newsem("s_const")
        sem_cast = {}
        r_cast = {}

        bank_ctr = [0]

        def next_bank():
            b = bank_ctr[0] % 8
            bank_ctr[0] += 1
            return b

        rot = {}

        def nxt(name, n):
            v = rot.get(name, 0)
            rot[name] = v + 1
            return v % n

        def dma1(eng, out_ap, in_ap, reads, writes, sem):
            return emit(eng, lambda h: [h.dma_start(out=out_ap, in_=in_ap)], reads=reads, writes=writes, dma=sem)

        dma1(SP, vecs[:], vecs_d[:, :], [], [r_vecs], sem_const)
        dma1(SP, ident[:], ident_d[:, :], [], [r_const], newsem("s_ident"))
        def load_x(i):
            src = x_d[i * T:(i + 1) * T, :].rearrange("(b p) d -> p b d", p=128)
            dma1(SP if os.environ.get("KDBG_XQ") == "sp" else PL, xin[:], src, [], [r_xin], sem_xin)
        load_x(0)
        sem_g = newsem("s_gates")
        dma1(PL, wabd[:], wabd_d[:, :].rearrange("p (c n) -> p c n", c=4), [], [r_gates], sem_g)
        dma1(PL, wibd[:], wibd_d[:, :].rearrange("p (c n) -> p c n", c=4), [], [r_gates], sem_g)

        def cast_tensor(name, src, dst, rows):
            sem_cast[name] = newsem("s_c_" + name)
            r_cast[name] = R()
            nrow = src.shape[0]
            pieces = [(r0, min(rows, nrow - r0)) for r0 in range(0, nrow, rows)]

            def fn(h):
                return [h.dma_start(out=dst[r0:r0 + n, :], in_=src[r0:r0 + n, :]) for r0, n in pieces]
            emit(PL, fn, writes=[r_cast[name]], dma=sem_cast[name], ndma=len(pieces))

        cast_tensor("w1i", w1i_d, b1i, 128)
        cast_tensor("w1o", w1o_d, b1o, 256)
        cast_tensor("win", win_d, bin_, 256)
        cast_tensor("dcw", dcw_d, bdc, 128)
        cast_tensor("wout", wout_d, bout, 256)
        cast_tensor("w2i", w2i_d, b2i, 128)
        cast_tensor("w2o", w2o_d, b2o, 256)

        emit(DVE, lambda h: [h.memset(ones_bf[:], 1.0)], writes=[r_const])
        emit(DVE, lambda h: [h.memset(hstate[:], 0.0)], writes=r_hstate)
        emit(DVE, lambda h: [h.memset(ub[:, :, 0:30], 0.0)], writes=r_ub)
        emit(DVE, lambda h: [h.memset(ux[:, :, 0:3], 0.0)], writes=r_ux)
        emit(ACT, lambda h: [h.activation(out=cact[:], in_=vecs[:, V_C:V_C + 8], func=AF.Silu)],
             reads=[r_vecs], writes=[r_cact])
        emit(ACT, lambda h: [h.activation(out=lamt[:], in_=vecs[:, V_LAM:V_LAM + 4], func=AF.Exp, scale=-1.0)],
             reads=[r_vecs], writes=[r_derived])
        emit(ACT, lambda h: [h.activation(out=lamt[:], in_=lamt[:], func=AF.Ln, bias=1.0)],
             reads=[r_derived], writes=[r_derived])
        emit(DVE, lambda h: [h.tensor_scalar(out=lam8[:], in0=lamt[:], scalar1=-8.0, scalar2=None, op0=ALU.mult)],
             reads=[r_derived], writes=[r_derived])
        emit(DVE, lambda h: [h.tensor_scalar(out=lam16[:], in0=lamt[:], scalar1=-16.0, scalar2=None, op0=ALU.mult)],
             reads=[r_derived], writes=[r_derived])

        kview = lambda ap: ap.rearrange("(kc p) n -> p kc n", p=128)
        plan = []

        def add_mod_slabs():
            for g in range(36):
                plan.append((None, lambda si, g=g: [(slots[si][:].bitcast(F32), kview(wmod_d[:, g * 256:(g + 1) * 256]))]))
            for g in range(8):
                plan.append((None, lambda si, g=g: [(slots[si][:].bitcast(F32), kview(wfmod_d[:, g * 256:(g + 1) * 256]))]))

        def add_ffn_slabs(nm_i, wi, nm_o, wo):
            for j in range(11):
                plan.append((nm_i, lambda si, j=j: [
                    (slots[si][:, :, 0:256], kview(wi[:, 256 * j:256 * j + 256])),
                    (slots[si][:, :, 256:512], kview(wi[:, DFF + 256 * j:DFF + 256 * j + 256]))]))
            for ch in range(2):
                for (k0, nk) in ((0, 8), (8, 8), (16, 6)):
                    plan.append((nm_o, lambda si, ch=ch, k0=k0, nk=nk: [
                        (slots[si][:, 0:nk, :], kview(wo[k0 * 128:(k0 + nk) * 128, ch * 512:(ch + 1) * 512]))]))

        def add_mixer_slabs():
            plan.append(("win", lambda si: [(slots[si][:, :, :], kview(bin_[:, 1024:1536]))]))
            for half in range(2):
                plan.append(("win", lambda si, half=half: [
                    (slots[si][:, :, 0:256], kview(bin_[:, 256 * half:256 * half + 256])),
                    (slots[si][:, :, 256:512], kview(bin_[:, 512 + 256 * half:512 + 256 * half + 256]))]))
            plan.append(("win", lambda si: [(slots[si][:, :, :], kview(bin_[:, 1536:2048]))]))
            for c in range(4):
                plan.append(("dcw", lambda si, c=c: [
                    (slots[si][:].rearrange("p k n -> p (k n)")[:, 0:NCONV * 128], bdc[c * 128:(c + 1) * 128, :])]))
            for ch in range(2):
                plan.append(("wout", lambda si, ch=ch: [(slots[si][:, :, :], kview(bout[:, ch * 512:(ch + 1) * 512]))]))

        add_mod_slabs()
        import os
        _DBG_SKIP = os.environ.get("KDBG_SKIP", "").split(",")
        for _ in range(n_tiles):
            if "ffn1" not in _DBG_SKIP:
                add_ffn_slabs("w1i", b1i, "w1o", b1o)
            if "mixer" not in _DBG_SKIP:
                add_mixer_slabs()
            if "ffn2" not in _DBG_SKIP:
                add_ffn_slabs("w2i", b2i, "w2o", b2o)

        st = {"issued": 0, "used": 0}

        def issue_slab():
            k = st["issued"]
            if k >= len(plan):
                return
            st["issued"] += 1
            si = k % NS
            cname, pfn = plan[k]
            pieces = pfn(si)
            reads = [r_cast[cname]] if cname is not None else []

            def fn(h):
                return [h.dma_start(out=o, in_=i) for o, i in pieces]
            emit(SP, fn, reads=reads, writes=[r_slot[si]], dma=sem_slot[si], ndma=len(pieces))

        def next_slab():
            k = st["used"]
            st["used"] += 1
            assert k < st["issued"]
            return k % NS

        def release_slab():
            issue_slab()

        for _ in range(NS):
            issue_slab()

        def mm_group(bank, pairs, reads, first=True, last=True):
            n = len(pairs)

            def fn(h):
                out = []
                for idx, (l, r) in enumerate(pairs):
                    out.append(h.matmul(ps[:, bank, :], lhsT=l, rhs=r,
                                        start=(first and idx == 0), stop=(last and idx == n - 1)))
                return out
            return emit(PE, fn, reads=reads, writes=[r_bank[bank]])

        mod_bank = next_bank()
        for g in range(44):
            si = next_slab()
            sf = slots[si][:].bitcast(F32)

            def fn(h, g=g, sf=sf):
                out = []
                for sub in range(2):
                    j = 2 * g + sub
                    for kc in range(8):
                        out.append(h.matmul(ps[:, mod_bank, j:j + 1], lhsT=sf[:, kc, sub * 128:(sub + 1) * 128],
                                            rhs=cact[:, kc:kc + 1], start=(kc == 0), stop=(kc == 7)))
                return out
            emit(PE, fn, reads=[r_slot[si], r_cact], writes=[r_bank[mod_bank]])
            release_slab()
        emit(DVE, lambda h: [h.tensor_tensor(out=modv[:], in0=ps[:, mod_bank, 0:88], in1=vecs[:, V_BMOD:V_BMOD + 88], op=ALU.add)],
             reads=[r_bank[mod_bank], r_vecs], writes=[r_modv])

        def gs_op(dst, sc_col, g_col):
            emit(DVE, lambda h: [h.scalar_tensor_tensor(out=dst[:], in0=modv[:, sc_col:sc_col + 8], scalar=1.0,
                                                        in1=vecs[:, g_col:g_col + 8], op0=ALU.add, op1=ALU.mult)],
                 reads=[r_modv, r_vecs], writes=[r_derived])
        gs_op(gs1, 8, V_G1)
        gs_op(gs2, 32, V_GM)
        gs_op(gs3, 56, V_G2)
        gs_op(gsf, 80, V_GF)
        emit(DVE, lambda h: [h.tensor_scalar(out=gth1[:], in0=modv[:, 16:24], scalar1=0.5, scalar2=None, op0=ALU.mult)],
             reads=[r_modv], writes=[r_derived])
        emit(DVE, lambda h: [h.tensor_scalar(out=gth3[:], in0=modv[:, 64:72], scalar1=0.5, scalar2=None, op0=ALU.mult)],
             reads=[r_modv], writes=[r_derived])
        sh1 = modv[:, 0:8]; sh2 = modv[:, 24:32]; gt2 = modv[:, 40:48]; sh3 = modv[:, 48:56]; shf = modv[:, 72:80]

        dbg = {"sub": ""}
        def compute_xsq_and_stats(from_psum_banks=None):
            sbank = next_bank()
            for fc in range(8):
                q = nxt("xsq", 3)
                if from_psum_banks is not None:
                    b = from_psum_banks[fc]
                    emit(ACT, lambda h, q=q, b=b: [h.activation(out=xsq[q][:], in_=ps[:, b, :], func=AF.Square)],
                         reads=[r_bank[b]], writes=[r_xsq[q]])
                else:
                    emit(ACT, lambda h, q=q, fc=fc: [h.activation(out=xsq[q][:], in_=xT[:, fc, :], func=AF.Square)],
                         reads=[r_xT[fc]], writes=[r_xsq[q]])
                if dbg["sub"] != "a1":
                    mm_group(sbank, [(ones_bf[:], xsq[q][:])], [r_xsq[q], r_const], first=(fc == 0), last=(fc == 7))
            if dbg["sub"] in ("a", "a1"):
                return
            emit(ACT, lambda h: [h.activation(out=rstd[:], in_=ps[:, sbank, :], func=AF.Sqrt, bias=EPS, scale=1.0 / D)],
                 reads=[r_bank[sbank]], writes=[r_rstd])
            if dbg["sub"] == "b0":
                return
            emit(DVE, lambda h: [h.reciprocal(out=rstd[:], in_=rstd[:])], reads=[r_rstd], writes=[r_rstd])

        def norm_apply(gs, sh, dst_fn, dst_res):
            if dbg["sub"] in ("a", "a1", "b0", "b"):
                return
            for fc in range(8):
                q = nxt("ntmp", 2)
                emit(DVE, lambda h, q=q, fc=fc: [h.scalar_tensor_tensor(
                    out=ntmp[q][:], in0=xT[:, fc, :], scalar=gs[:, fc:fc + 1], in1=rstd[:], op0=ALU.mult, op1=ALU.mult)],
                    reads=[r_xT[fc], r_rstd, r_derived], writes=[r_ntmp[q]])
                emit(ACT, lambda h, q=q, fc=fc: [h.activation(out=dst_fn(fc), in_=ntmp[q][:], func=AF.Identity,
                                                              bias=sh[:, fc:fc + 1], scale=1.0)],
                     reads=[r_ntmp[q], r_modv], writes=[dst_res(fc)])

        def ffn(gth):
            for j in range(11):
                si = next_slab()
                for sub in range(2):
                    cg = 2 * j + sub
                    bg = next_bank()
                    mm_group(bg, [(slots[si][:, kc, sub * 128:(sub + 1) * 128], hb[:, kc, :]) for kc in range(8)],
                             [r_slot[si]] + r_hb)
                    bu = next_bank()
                    mm_group(bu, [(slots[si][:, kc, 256 + sub * 128:256 + (sub + 1) * 128], hb[:, kc, :]) for kc in range(8)],
                             [r_slot[si]] + r_hb)
                    q = nxt("sgb", 2)
                    emit(ACT, lambda h, q=q, bg=bg: [h.activation(out=sgb[q][:], in_=ps[:, bg, :], func=AF.Silu)],
                         reads=[r_bank[bg]], writes=[r_sgb[q]])
                    emit(DVE, lambda h, q=q, bu=bu, cg=cg: [h.tensor_tensor(out=hid[:, cg, :], in0=ps[:, bu, :], in1=sgb[q][:], op=ALU.mult)],
                         reads=[r_bank[bu], r_sgb[q]], writes=[r_hid[cg]])
                release_slab()
            for ch in range(2):
                banks = [next_bank() for _ in range(4)]
                for (k0, nk) in ((0, 8), (8, 8), (16, 6)):
                    si = next_slab()
                    for oc in range(4):
                        mm_group(banks[oc],
                                 [(slots[si][:, kl, oc * 128:(oc + 1) * 128], hid[:, k0 + kl, :]) for kl in range(nk)],
                                 [r_slot[si]] + r_hid[k0:k0 + nk], first=(k0 == 0), last=(k0 == 16))
                    release_slab()
                for oc in range(4):
                    fc = 4 * ch + oc
                    emit(DVE, lambda h, fc=fc, b=banks[oc]: [h.scalar_tensor_tensor(
                        out=xT[:, fc, :], in0=ps[:, b, :], scalar=gth[:, fc:fc + 1], in1=xT[:, fc, :], op0=ALU.mult, op1=ALU.add)],
                        reads=[r_bank[banks[oc]], r_derived, r_modv], writes=[r_xT[fc]])

        def mixer():
            si = next_slab()
            for c in range(4):
                b = next_bank()
                mm_group(b, [(slots[si][:, kc, c * 128:(c + 1) * 128], hb[:, kc, :]) for kc in range(8)], [r_slot[si]] + r_hb)
                emit(ACT, lambda h, c=c, b=b: [h.activation(out=ux[:, c, 3:3 + T], in_=ps[:, b, :], func=AF.Copy)],
                     reads=[r_bank[b]], writes=[r_ux[c]])
            release_slab()
            for half in range(2):
                si = next_slab()
                for sub in range(2):
                    c = 2 * half + sub
                    bv = next_bank()
                    mm_group(bv, [(slots[si][:, kc, sub * 128:(sub + 1) * 128], hb[:, kc, :]) for kc in range(8)], [r_slot[si]] + r_hb)
                    bg = next_bank()
                    mm_group(bg, [(slots[si][:, kc, 256 + sub * 128:256 + (sub + 1) * 128], hb[:, kc, :]) for kc in range(8)], [r_slot[si]] + r_hb)
                    q = nxt("sgb", 2)
                    emit(ACT, lambda h, q=q, bg=bg: [h.activation(out=sgb[q][:], in_=ps[:, bg, :], func=AF.Sigmoid)],
                         reads=[r_bank[bg]], writes=[r_sgb[q]])
                    emit(DVE, lambda h, q=q, bv=bv, c=c: [h.tensor_tensor(out=ub[:, c, 30:30 + T], in0=ps[:, bv, :], in1=sgb[q][:], op=ALU.mult)],
                         reads=[r_bank[bv], r_sgb[q]], writes=[r_ub[c]])
                release_slab()
            si = next_slab()
            for c in range(4):
                b = next_bank()
                mm_group(b, [(slots[si][:, kc, c * 128:(c + 1) * 128], hb[:, kc, :]) for kc in range(8)], [r_slot[si]] + r_hb)
                emit(ACT, lambda h, c=c, b=b: [h.activation(out=gy[:, c, :], in_=ps[:, b, :], func=AF.Gelu_apprx_tanh)],
                     reads=[r_bank[b]], writes=[r_gy[c]])
            release_slab()

            def conv_chunk(c):
                si = next_slab()
                dflat = slots[si][:].rearrange("p k n -> p (k n)")
                b = next_bank()
                mm_group(b, [(dflat[:, k * 128:(k + 1) * 128], ub[:, c, k:k + T]) for k in range(NCONV)], [r_slot[si], r_ub[c]])
                release_slab()
                emit(ACT, lambda h: [h.activation(out=cv[:, c, :], in_=ps[:, b, :], func=AF.Identity,
                                                  bias=vecs[:, V_CONVB + c:V_CONVB + c + 1], scale=1.0)],
                     reads=[r_bank[b], r_vecs], writes=[r_cv[c]])
                emit(DVE, lambda h: [h.tensor_copy(out=ub[:, c, 0:30], in_=ub[:, c, T:T + 30])], reads=[r_ub[c]], writes=[r_ub[c]])

            def rnn_pair(c0):
                cs = (c0, c0 + 1)
                qx = {}
                for c in cs:
                    q = nxt("xr", 2)
                    qx[c] = q
                    emit(DVE, lambda h, c=c, q=q: [h.tensor_scalar(
                        out=xr[q][:], in0=ux[:, c, 0:T], scalar1=vecs[:, V_RCW + 4 * c:V_RCW + 4 * c + 1],
                        scalar2=vecs[:, V_RCB + c:V_RCB + c + 1], op0=ALU.mult, op1=ALU.add)],
                        reads=[r_ux[c], r_vecs], writes=[r_xr[q]])
                    for k in range(1, 4):
                        emit(DVE, lambda h, c=c, q=q, k=k: [h.scalar_tensor_tensor(
                            out=xr[q][:], in0=ux[:, c, k:k + T], scalar=vecs[:, V_RCW + 4 * c + k:V_RCW + 4 * c + k + 1],
                            in1=xr[q][:], op0=ALU.mult, op1=ALU.add)],
                            reads=[r_ux[c], r_vecs, r_xr[q]], writes=[r_xr[q]])
                    emit(DVE, lambda h, c=c: [h.tensor_copy(out=ux[:, c, 0:3], in_=ux[:, c, T:T + 3])], reads=[r_ux[c]], writes=[r_ux[c]])
                    emit(ACT, lambda h, q=q: [h.activation(out=xrb[q][:], in_=xr[q][:], func=AF.Copy)],
                         reads=[r_xr[q]], writes=[r_xrb[q]])
                br = {}; bi = {}
                for c in cs:
                    q = qx[c]
                    br[c] = next_bank()
                    mm_group(br[c], [(wabd[:, c, :], xrb[q][:])], [r_gates, r_xrb[q]])
                    bi[c] = next_bank()
                    mm_group(bi[c], [(wibd[:, c, :], xrb[q][:])], [r_gates, r_xrb[q]])
                for c in cs:
                    q = qx[c]
                    emit(ACT, lambda h, c=c, q=q: [h.activation(out=rbuf[q][:], in_=ps[:, br[c], :], func=AF.Sigmoid,
                                                              bias=vecs[:, V_BA + c:V_BA + c + 1], scale=1.0)],
                         reads=[r_bank[br[c]], r_vecs], writes=[r_rbuf[q]])
                    emit(ACT, lambda h, c=c, q=q: [h.activation(out=btb[q][:], in_=ps[:, bi[c], :], func=AF.Sigmoid,
                                                              bias=vecs[:, V_BI + c:V_BI + c + 1], scale=1.0)],
                         reads=[r_bank[bi[c]], r_vecs], writes=[r_btb[q]])
                for c in cs:
                    q = qx[c]
                    emit(ACT, lambda h, c=c, q=q: [h.activation(out=abuf[q][:], in_=rbuf[q][:], func=AF.Exp, scale=lam8[:, c:c + 1])],
                         reads=[r_rbuf[q], r_derived], writes=[r_abuf[q]])
                    emit(ACT, lambda h, c=c, q=q: [h.activation(out=mbuf[q][:], in_=rbuf[q][:], func=AF.Exp, scale=lam16[:, c:c + 1])],
                         reads=[r_rbuf[q], r_derived], writes=[r_mbuf[q]])
                for c in cs:
                    q = qx[c]
                    emit(DVE, lambda h, q=q: [h.tensor_scalar(out=mbuf[q][:], in0=mbuf[q][:], scalar1=1.0, scalar2=-1.0, op0=ALU.min, op1=ALU.mult)],
                         reads=[r_mbuf[q]], writes=[r_mbuf[q]])
                    emit(DVE, lambda h, q=q: [h.tensor_tensor(out=btb[q][:], in0=btb[q][:], in1=xr[q][:], op=ALU.mult)],
                         reads=[r_btb[q], r_xr[q]], writes=[r_btb[q]])
                for c in cs:
                    q = qx[c]
                    emit(ACT, lambda h, q=q: [h.activation(out=mbuf[q][:], in_=mbuf[q][:], func=AF.Sqrt, bias=1.0, scale=1.0)],
                         reads=[r_mbuf[q]], writes=[r_mbuf[q]])
                for c in cs:
                    q = qx[c]
                    emit(DVE, lambda h, q=q: [h.tensor_tensor(out=btb[q][:], in0=btb[q][:], in1=mbuf[q][:], op=ALU.mult)],
                         reads=[r_btb[q], r_mbuf[q]], writes=[r_btb[q]])
                    emit(DVE, lambda h, c=c, q=q: [h.tensor_tensor_scan(out=hs[q][:], data0=abuf[q][:], data1=btb[q][:],
                                                                        initial=hstate[:, c:c + 1], op0=ALU.mult, op1=ALU.add)],
                         reads=[r_abuf[q], r_btb[q], r_hstate[c]], writes=[r_hs[q]])
                    emit(DVE, lambda h, c=c, q=q: [h.tensor_copy(out=hstate[:, c:c + 1], in_=hs[q][:, T - 1:T])],
                         reads=[r_hs[q]], writes=[r_hstate[c]])
                    emit(DVE, lambda h, c=c, q=q: [h.tensor_tensor(out=hb[:, 4 + c, :], in0=hs[q][:], in1=gy[:, c, :], op=ALU.mult)],
                         reads=[r_hs[q], r_gy[c]], writes=[r_hb[4 + c]])

            conv_chunk(0)
            conv_chunk(1)
            rnn_pair(0)
            conv_chunk(2)
            conv_chunk(3)
            rnn_pair(2)
            bs = next_bank()
            bq = next_bank()
            for c in range(4):
                q = nxt("cvb", 2)
                emit(DVE, lambda h, c=c, q=q: [h.tensor_copy(out=cvb[q][:], in_=cv[:, c, :])], reads=[r_cv[c]], writes=[r_cvb[q]])
                emit(ACT, lambda h, c=c, q=q: [h.activation(out=cvsq[q][:], in_=cv[:, c, :], func=AF.Square)], reads=[r_cv[c]], writes=[r_cvsq[q]])
                mm_group(bs, [(ones_bf[:], cvb[q][:])], [r_cvb[q], r_const], first=(c == 0), last=(c == 3))
                mm_group(bq, [(ones_bf[:], cvsq[q][:])], [r_cvsq[q], r_const], first=(c == 0), last=(c == 3))
            emit(ACT, lambda h: [h.activation(out=meanb[:], in_=ps[:, bs, :], func=AF.Copy, scale=1.0 / 512)],
                 reads=[r_bank[bs]], writes=[r_meanb])
            emit(ACT, lambda h: [h.activation(out=varb[:], in_=ps[:, bs, :], func=AF.Square, scale=1.0 / 512)],
                 reads=[r_bank[bs]], writes=[r_varb])
            emit(DVE, lambda h: [h.scalar_tensor_tensor(out=varb[:], in0=ps[:, bq, :], scalar=1.0 / 512, in1=varb[:],
                                                        op0=ALU.mult, op1=ALU.subtract)],
                 reads=[r_bank[bq], r_varb], writes=[r_varb])
            emit(ACT, lambda h: [h.activation(out=varb[:], in_=varb[:], func=AF.Sqrt, bias=EPS, scale=1.0)],
                 reads=[r_varb], writes=[r_varb])
            emit(DVE, lambda h: [h.reciprocal(out=varb[:], in_=varb[:])], reads=[r_varb], writes=[r_varb])
            for c in range(4):
                q = nxt("ntmp", 2)
                emit(DVE, lambda h, c=c, q=q: [h.tensor_tensor(out=ntmp[q][:], in0=cv[:, c, :], in1=meanb[:], op=ALU.subtract)],
                     reads=[r_cv[c], r_meanb], writes=[r_ntmp[q]])
                emit(DVE, lambda h, q=q: [h.tensor_tensor(out=ntmp[q][:], in0=ntmp[q][:], in1=varb[:], op=ALU.mult)],
                     reads=[r_ntmp[q], r_varb], writes=[r_ntmp[q]])
                emit(ACT, lambda h, c=c, q=q: [h.activation(out=hb[:, c, :], in_=ntmp[q][:], func=AF.Silu,
                                                          bias=vecs[:, V_LNB + c:V_LNB + c + 1], scale=vecs[:, V_LNG + c:V_LNG + c + 1])],
                     reads=[r_ntmp[q], r_vecs], writes=[r_hb[c]])
            for ch in range(2):
                si = next_slab()
                for oc in range(4):
                    fc = 4 * ch + oc
                    b = next_bank()
                    mm_group(b, [(slots[si][:, kc, oc * 128:(oc + 1) * 128], hb[:, kc, :]) for kc in range(8)], [r_slot[si]] + r_hb)
                    emit(DVE, lambda h, fc=fc, b=b: [h.scalar_tensor_tensor(
                        out=xT[:, fc, :], in0=ps[:, b, :], scalar=gt2[:, fc:fc + 1], in1=xT[:, fc, :], op0=ALU.mult, op1=ALU.add)],
                        reads=[r_bank[b], r_modv], writes=[r_xT[fc]])
                release_slab()

        for i in range(n_tiles):
            lvl = int(os.environ.get('KDBG_LVL', '9')) if i >= 1 else 9
            if lvl < 1:
                continue
            tb = []
            for fc in range(8):
                b = next_bank()
                tb.append(b)

                def fn(h, fc=fc, b=b):
                    return [h.transpose(out=ps[:, b, blk * 128:(blk + 1) * 128], in_=xin[:, blk, fc * 128:(fc + 1) * 128],
                                        identity=ident[:]) for blk in range(4)]
                emit(PE, fn, reads=[r_xin, r_const], writes=[r_bank[b]])
                emit(DVE, lambda h, fc=fc, b=b: [h.tensor_copy(out=xT[:, fc, :], in_=ps[:, b, :])],
                     reads=[r_bank[b]], writes=[r_xT[fc]])
            if i + 1 < n_tiles and os.environ.get("KDBG_T1") != "noload":
                load_x(i + 1)
            if lvl < 2:
                continue
            if i >= 1:
                dbg["sub"] = os.environ.get("KDBG_SUB", "")
            compute_xsq_and_stats(from_psum_banks=(tb if os.environ.get('KDBG_PS') == '1' else None))
            norm_apply(gs1, sh1, lambda fc: hb[:, fc, :], lambda fc: r_hb[fc])
            if "ffn1" not in _DBG_SKIP:
                ffn(gth1)
            if lvl < 3:
                continue
            compute_xsq_and_stats()
            norm_apply(gs2, sh2, lambda fc: hb[:, fc, :], lambda fc: r_hb[fc])
            if "mixer" not in _DBG_SKIP:
                mixer()
            compute_xsq_and_stats()
            norm_apply(gs3, sh3, lambda fc: hb[:, fc, :], lambda fc: r_hb[fc])
            if "ffn2" not in _DBG_SKIP:
                ffn(gth3)
            if lvl < 4:
                continue
            compute_xsq_and_stats()
            for fc in range(8):
                q = nxt("ntmp", 2)
                qo = nxt("oT", 2)
                emit(DVE, lambda h, q=q, fc=fc: [h.scalar_tensor_tensor(
                    out=ntmp[q][:], in0=xT[:, fc, :], scalar=gsf[:, fc:fc + 1], in1=rstd[:], op0=ALU.mult, op1=ALU.mult)],
                    reads=[r_xT[fc], r_rstd, r_derived], writes=[r_ntmp[q]])
                emit(ACT, lambda h, q=q, qo=qo, fc=fc: [h.activation(out=oT[qo][:], in_=ntmp[q][:], func=AF.Identity,
                                                                    bias=shf[:, fc:fc + 1], scale=1.0)],
                     reads=[r_ntmp[q], r_modv], writes=[r_oT[qo]])
                b = next_bank()

                def fn(h, qo=qo, b=b):
                    return [h.transpose(out=ps[:, b, blk * 128:(blk + 1) * 128], in_=oT[qo][:, blk * 128:(blk + 1) * 128],
                                        identity=ident[:]) for blk in range(4)]
                emit(PE, fn, reads=[r_oT[qo], r_const], writes=[r_bank[b]])
                src = lambda b=b: ps[:, b, :].rearrange("p (k n) -> p k n", k=4)
                if fc % 2 == 0:
                    emit(DVE, lambda h, fc=fc, b=b: [h.tensor_copy(out=oout[:, :, fc * 128:(fc + 1) * 128], in_=ps[:, b, :].rearrange("p (k n) -> p k n", k=4))],
                         reads=[r_bank[b]], writes=[r_oout[fc]])
                else:
                    emit(ACT, lambda h, fc=fc, b=b: [h.activation(out=oout[:, :, fc * 128:(fc + 1) * 128], in_=ps[:, b, :].rearrange("p (k n) -> p k n", k=4), func=AF.Copy)],
                         reads=[r_bank[b]], writes=[r_oout[fc]])
            dst = y_d[i * T:(i + 1) * T, :].rearrange("(b p) d -> p b d", p=128)
            if not (i >= 1 and os.environ.get("KDBG_T1") == "nostore"):
                dma1(SP if os.environ.get("KDBG_OQ") == "sp" else PL, dst, oout[:], r_oout, [], sem_out)

        final_val = sem_out.val
        assert st["used"] == len(plan), (st, len(plan))

        with nc.Block() as block:
            @block.sync
            def _(h):
                replay(SP, h)

            @block.gpsimd
            def _(h):
                replay(PL, h)
                h.wait_ge(sem_out.h, final_val)

            @block.tensor
            def _(h):
                replay(PE, h)

            @block.scalar
            def _(h):
                replay(ACT, h)

            @block.vector
            def _(h):
                replay(DVE, h)
    return nc


def prep_inputs(inputs, n_tiles=SEQ // T, cores=N_CORES):
    f = lambda a: np.ascontiguousarray(np.asarray(a, dtype=np.float32))
    col = lambda v, n: f(v).reshape(n, 128).T
    x = f(inputs["x"]); c = f(inputs["c"])
    shared = np.zeros((128, NV), np.float32)
    shared[:, V_G1:V_G1 + 8] = col(inputs["g_ffn1"][0], 8)
    shared[:, V_GM:V_GM + 8] = col(inputs["g_mix"][0], 8)
    shared[:, V_G2:V_G2 + 8] = col(inputs["g_ffn2"][0], 8)
    shared[:, V_GF:V_GF + 8] = col(inputs["g_final"], 8)
    shared[:, V_BMOD:V_BMOD + 72] = col(inputs["b_mod"][0], 72)
    shared[:, V_BMOD + 72:V_BMOD + 88] = col(inputs["b_fmod"], 16)
    shared[:, V_CONVB:V_CONVB + 4] = col(inputs["conv_b"][0], 4)
    shared[:, V_LNG:V_LNG + 4] = col(inputs["ln_g"][0], 4)
    shared[:, V_LNB:V_LNB + 4] = col(inputs["ln_b"][0], 4)
    shared[:, V_RCB:V_RCB + 4] = col(inputs["rnn_conv_b"][0], 4)
    shared[:, V_BA:V_BA + 4] = col(inputs["b_a"][0], 4)
    shared[:, V_BI:V_BI + 4] = col(inputs["b_i"][0], 4)
    shared[:, V_LAM:V_LAM + 4] = col(inputs["lru_lambda"][0], 4)
    rcw = f(inputs["rnn_conv_w"][0])
    for cc in range(4):
        for k in range(4):
            shared[:, V_RCW + 4 * cc + k] = rcw[k, cc * 128:(cc + 1) * 128]
    cw = f(inputs["conv_w"][0])
    dcw = np.zeros((4, 128, NCONV, 128), np.float32)
    ar = np.arange(128)
    for cc in range(4):
        for k in range(NCONV):
            dcw[cc, ar, k, ar] = cw[k, cc * 128:(cc + 1) * 128]
    dcw = dcw.reshape(512, NCONV * 128)
    def bd(w):
        w = f(w)
        o = np.zeros((128, 4, 128), np.float32)
        for cc in range(4):
            o[0:64, cc, 0:64] = w[2 * cc]
            o[64:128, cc, 64:128] = w[2 * cc + 1]
        return o.reshape(128, 512)
    wabd = bd(inputs["w_a"][0]); wibd = bd(inputs["w_i"][0])
    common = {
        "ident": np.eye(128, dtype=np.float32), "wabd": wabd, "wibd": wibd,
        "w_mod": f(inputs["w_mod"][0]), "w_fmod": f(inputs["w_fmod"]),
        "w1i": f(inputs["w_ffn1_in"][0]), "w1o": f(inputs["w_ffn1_out"][0]),
        "win": f(inputs["w_in"][0]), "wout": f(inputs["w_out"][0]),
        "w2i": f(inputs["w_ffn2_in"][0]), "w2o": f(inputs["w_ffn2_out"][0]),
        "dcw": dcw,
    }
    in_maps = []
    S = n_tiles * T
    for b in range(cores):
        v = shared.copy()
        v[:, V_C:V_C + 8] = c[b].reshape(8, 128).T
        m = dict(common)
        m["vecs"] = v
        m["x"] = np.ascontiguousarray(x[b, :S, :])
        in_maps.append(m)
    return in_maps


_NC_CACHE = {}


def kernel(**inputs):
    n_tiles = SEQ // T
    if n_tiles not in _NC_CACHE:
        _NC_CACHE[n_tiles] = build_program(n_tiles)
    nc = _NC_CACHE[n_tiles]
    in_maps = prep_inputs(inputs, n_tiles, N_CORES)
    res = run_bass_kernel_spmd(nc, in_maps, core_ids=list(range(N_CORES)))
    out = np.stack([np.asarray(r["y"], dtype=np.float32) for r in res.results], axis=0)
    return out
```

```python
import numpy as np
import concourse.bass as bass
import concourse.mybir as mybir
from concourse.bass_utils import run_bass_kernel_spmd

F32 = mybir.dt.float32
BF16 = mybir.dt.bfloat16
AF = mybir.ActivationFunctionType
ALU = mybir.AluOpType

D = 1024
DFF = 2816
SEQ = 4096
T = 512
KC = 8
FC = 22
NCONV = 31
EPS = 1e-6
NS = 4
N_CORES = 8

V_G1, V_GM, V_G2, V_GF = 0, 8, 16, 24
V_BMOD = 32
V_CONVB = 120
V_LNG = 124
V_LNB = 128
V_RCB = 132
V_BA = 136
V_BI = 140
V_LAM = 144
V_RCW = 148
V_C = 164
NV = 172


class Sem:
    def __init__(self, h):
        self.h = h
        self.val = 0


class Res:
    __slots__ = ("w", "r")

    def __init__(self):
        self.w = None
        self.r = {}


class Eng:
    def __init__(self, name, sem, self_sync):
        self.name = name
        self.sem = sem
        self.prog = []
        self.seen = {}
        self.self_sync = self_sync


def emit(eng, fn, reads=(), writes=(), dma=None, ndma=1):
    waits = {}

    def need(tok):
        if tok is None:
            return
        s, v = tok
        if waits.get(s, 0) < v:
            waits[s] = v

    for r in reads:
        need(r.w)
    for w in writes:
        need(w.w)
        for s, v in w.r.items():
            need((s, v))
    wl = []
    for s, v in waits.items():
        if s is eng.sem and not eng.self_sync:
            continue
        if eng.seen.get(s, 0) >= v:
            continue
        eng.seen[s] = v
        wl.append((s, v))
    if dma is None:
        eng.sem.val += 1
        tok = (eng.sem, eng.sem.val)
        eng.prog.append((wl, fn, eng.sem, 1, False))
    else:
        dma.val += 16 * ndma
        tok = (dma, dma.val)
        eng.prog.append((wl, fn, dma, 16, True))
    for r in reads:
        if r.r.get(tok[0], 0) < tok[1]:
            r.r[tok[0]] = tok[1]
    for w in writes:
        w.w = tok
        w.r = {}
    return tok


def replay(eng, h):
    for wl, fn, sem, inc, is_dma in eng.prog:
        for s, v in wl:
            h.wait_ge(s.h, v)
        ins = fn(h)
        if is_dma:
            for i in ins:
                i.then_inc(sem.h, 16)
        else:
            ins[-1].then_inc(sem.h, 1)


def build_program(n_tiles=SEQ // T):
    S = n_tiles * T
    nc = bass.Bass("TRN2", target_bir_lowering=False)
    dt_in = lambda name, shape: nc.dram_tensor(name, shape, F32, kind="ExternalInput").ap()
    x_d = dt_in("x", [S, D])
    vecs_d = dt_in("vecs", [128, NV])
    ident_d = dt_in("ident", [128, 128])
    wabd_d = dt_in("wabd", [128, 512])
    wibd_d = dt_in("wibd", [128, 512])
    wmod_d = dt_in("w_mod", [D, 9 * D])
    wfmod_d = dt_in("w_fmod", [D, 2 * D])
    w1i_d = dt_in("w1i", [D, 2 * DFF])
    w1o_d = dt_in("w1o", [DFF, D])
    win_d = dt_in("win", [D, 2 * D])
    wout_d = dt_in("wout", [D, D])
    w2i_d = dt_in("w2i", [D, 2 * DFF])
    w2o_d = dt_in("w2o", [DFF, D])
    dcw_d = dt_in("dcw", [512, NCONV * 128])
    y_d = nc.dram_tensor("y", [S, D], F32, kind="ExternalOutput").ap()
    dt_sc = lambda name, shape: nc.dram_tensor(name, shape, BF16, kind="Internal").ap()
    b1i = dt_sc("b1i", [D, 2 * DFF])
    b1o = dt_sc("b1o", [DFF, D])
    bin_ = dt_sc("bin", [D, 2 * D])
    bout = dt_sc("bout", [D, D])
    b2i = dt_sc("b2i", [D, 2 * DFF])
    b2o = dt_sc("b2o", [DFF, D])
    bdc = dt_sc("bdc", [512, NCONV * 128])

    from contextlib import ExitStack
    es = ExitStack()
    with es:
        sb = lambda name, shape, dt=F32: es.enter_context(nc.sbuf_tensor("sb_" + name, shape, dt))
        newsem = lambda name: Sem(es.enter_context(nc.semaphore(name)))

        PE = Eng("pe", newsem("s_pe"), False)
        ACT = Eng("act", newsem("s_act"), True)
        DVE = Eng("dve", newsem("s_dve"), True)
        SP = Eng("sp", newsem("s_sp"), False)
        PL = Eng("pool", newsem("s_pool"), False)

        ident = sb("ident", [128, 128])
        ones_bf = sb("ones_bf", [128, 128], BF16)
        vecs = sb("vecs", [128, NV])
        modv = sb("modv", [128, 88])
        cact = sb("cact", [128, 8])
        gs1 = sb("gs1", [128, 8]); gth1 = sb("gth1", [128, 8])
        gs2 = sb("gs2", [128, 8])
        gs3 = sb("gs3", [128, 8]); gth3 = sb("gth3", [128, 8])
        gsf = sb("gsf", [128, 8])
        lam8 = sb("lam8", [128, 4]); lam16 = sb("lam16", [128, 4]); lamt = sb("lamt", [128, 4])
        wabd = sb("wabd", [128, 4, 128], BF16)
        wibd = sb("wibd", [128, 4, 128], BF16)
        hstate = sb("hstate", [128, 4])
        slots = [sb(f"slab{i}", [128, 8, 512], BF16) for i in range(NS)]
        xin = sb("xin", [128, 4, D])
        xT = sb("xT", [128, 8, T])
        xsq = [sb(f"xsq{i}", [128, T], BF16) for i in range(3)]
        hb = sb("hb", [128, 8, T], BF16)
        hid = sb("hid", [128, FC, T], BF16)
        sgb = [sb(f"sgb{i}", [128, T]) for i in range(2)]
        ntmp = [sb(f"ntmp{i}", [128, T]) for i in range(2)]
        rstd = sb("rstd", [128, T])
        ub = sb("ub", [128, 4, T + 30], BF16)
        ux = sb("ux", [128, 4, T + 3])
        gy = sb("gy", [128, 4, T], BF16)
        cv = sb("cv", [128, 4, T])
        cvsq = [sb(f"cvsq{i}", [128, T], BF16) for i in range(2)]
        cvb = [sb(f"cvb{i}", [128, T], BF16) for i in range(2)]
        meanb = sb("meanb", [128, T])
        varb = sb("varb", [128, T])
        xr = [sb(f"xr{i}", [128, T]) for i in range(2)]
        xrb = [sb(f"xrb{i}", [128, T], BF16) for i in range(2)]
        rbuf = [sb(f"rbuf{i}", [128, T]) for i in range(2)]
        abuf = [sb(f"abuf{i}", [128, T]) for i in range(2)]
        mbuf = [sb(f"mbuf{i}", [128, T]) for i in range(2)]
        btb = [sb(f"btb{i}", [128, T]) for i in range(2)]
        hs = [sb(f"hs{i}", [128, T]) for i in range(4)]
        oT = [sb(f"oT{i}", [128, T]) for i in range(2)]
        oout = sb("oout", [128, 4, D])
        ps = es.enter_context(nc.psum_tensor("ps", [128, 8, T], F32))

        R = lambda: Res()
        r_const = R()
        r_vecs = R(); r_modv = R(); r_cact = R(); r_derived = R(); r_hstate = [R() for _ in range(4)]
        r_gates = R()
        r_slot = [R() for _ in range(NS)]
        sem_slot = [newsem(f"s_slot{i}") for i in range(NS)]
        r_bank = [R() for _ in range(8)]
        r_xin = R(); sem_xin = newsem("s_xin")
        r_xT = [R() for _ in range(8)]
        r_xsq = [R() for _ in range(3)]
        r_hb = [R() for _ in range(8)]
        r_hid = [R() for _ in range(FC)]
        r_sgb = [R() for _ in range(2)]
        r_ntmp = [R() for _ in range(2)]
        r_rstd = R()
        r_ub = [R() for _ in range(4)]
        r_ux = [R() for _ in range(4)]
        r_gy = [R() for _ in range(4)]
        r_cv = [R() for _ in range(4)]
        r_cvsq = [R() for _ in range(2)]
        r_cvb = [R() for _ in range(2)]
        r_meanb = R(); r_varb = R()
        r_xr = [R() for _ in range(2)]; r_xrb = [R() for _ in range(2)]
        r_rbuf = [R() for _ in range(2)]; r_abuf = [R() for _ in range(2)]
        r_mbuf = [R() for _ in range(2)]; r_btb = [R() for _ in range(2)]
        r_hs = [R() for _ in range(4)]
        r_oT = [R() for _ in range(2)]
        r_oout = [R() for _ in range(8)]; sem_out = newsem("s_out")
        sem_const = newsem("s_const")
        sem_cast = {}
        r_cast = {}

        bank_ctr = [0]

        def next_bank():
            b = bank_ctr[0] % 8
            bank_ctr[0] += 1
            return b

        rot = {}

        def nxt(name, n):
            v = rot.get(name, 0)
            rot[name] = v + 1
            return v % n

        def dma1(eng, out_ap, in_ap, reads, writes, sem):
            return emit(eng, lambda h: [h.dma_start(out=out_ap, in_=in_ap)], reads=reads, writes=writes, dma=sem)

        dma1(SP, vecs[:], vecs_d[:, :], [], [r_vecs], sem_const)
        dma1(SP, ident[:], ident_d[:, :], [], [r_const], newsem("s_ident"))
        def load_x(i):
            src = x_d[i * T:(i + 1) * T, :].rearrange("(b p) d -> p b d", p=128)
            dma1(PL, xin[:], src, [], [r_xin], sem_xin)
        load_x(0)
        sem_g = newsem("s_gates")
        dma1(PL, wabd[:], wabd_d[:, :].rearrange("p (c n) -> p c n", c=4), [], [r_gates], sem_g)
        dma1(PL, wibd[:], wibd_d[:, :].rearrange("p (c n) -> p c n", c=4), [], [r_gates], sem_g)

        def cast_tensor(name, src, dst, rows):
            sem_cast[name] = newsem("s_c_" + name)
            r_cast[name] = R()
            nrow = src.shape[0]
            pieces = [(r0, min(rows, nrow - r0)) for r0 in range(0, nrow, rows)]

            def fn(h):
                return [h.dma_start(out=dst[r0:r0 + n, :], in_=src[r0:r0 + n, :]) for r0, n in pieces]
            emit(PL, fn, writes=[r_cast[name]], dma=sem_cast[name], ndma=len(pieces))

        cast_tensor("w1i", w1i_d, b1i, 128)
        cast_tensor("w1o", w1o_d, b1o, 256)
        cast_tensor("win", win_d, bin_, 256)
        cast_tensor("dcw", dcw_d, bdc, 128)
        cast_tensor("wout", wout_d, bout, 256)
        cast_tensor("w2i", w2i_d, b2i, 128)
        cast_tensor("w2o", w2o_d, b2o, 256)

        emit(DVE, lambda h: [h.memset(ones_bf[:], 1.0)], writes=[r_const])
        emit(DVE, lambda h: [h.memset(hstate[:], 0.0)], writes=r_hstate)
        emit(DVE, lambda h: [h.memset(ub[:, :, 0:30], 0.0)], writes=r_ub)
        emit(DVE, lambda h: [h.memset(ux[:, :, 0:3], 0.0)], writes=r_ux)
        emit(ACT, lambda h: [h.activation(out=cact[:], in_=vecs[:, V_C:V_C + 8], func=AF.Silu)],
             reads=[r_vecs], writes=[r_cact])
        emit(ACT, lambda h: [h.activation(out=lamt[:], in_=vecs[:, V_LAM:V_LAM + 4], func=AF.Exp, scale=-1.0)],
             reads=[r_vecs], writes=[r_derived])
        emit(ACT, lambda h: [h.activation(out=lamt[:], in_=lamt[:], func=AF.Ln, bias=1.0)],
             reads=[r_derived], writes=[r_derived])
        emit(DVE, lambda h: [h.tensor_scalar(out=lam8[:], in0=lamt[:], scalar1=-8.0, scalar2=None, op0=ALU.mult)],
             reads=[r_derived], writes=[r_derived])
        emit(DVE, lambda h: [h.tensor_scalar(out=lam16[:], in0=lamt[:], scalar1=-16.0, scalar2=None, op0=ALU.mult)],
             reads=[r_derived], writes=[r_derived])

        kview = lambda ap: ap.rearrange("(kc p) n -> p kc n", p=128)
        plan = []
        st = {"issued": 0, "used": 0, "dry": True}

        def issue_slab():
            k = st["issued"]
            if k >= len(plan):
                return
            st["issued"] += 1
            si = k % NS
            cname, pfn = plan[k]
            pieces = pfn(si)
            reads = [r_cast[cname]] if cname is not None else []

            def fn(h):
                return [h.dma_start(out=o, in_=i) for o, i in pieces]
            emit(SP, fn, reads=reads, writes=[r_slot[si]], dma=sem_slot[si], ndma=len(pieces))

        def next_slab(cname, pfn):
            if st["dry"]:
                plan.append((cname, pfn))
                return 0
            k = st["used"]
            st["used"] += 1
            assert k < st["issued"] and plan[k][0] == cname
            return k % NS

        def release_slab():
            if not st["dry"]:
                issue_slab()

        def E(eng, fn, reads=(), writes=()):
            if st["dry"]:
                return None
            return emit(eng, fn, reads=reads, writes=writes)

        def d_mod(g):
            src = wmod_d if g < 36 else wfmod_d
            gg = g if g < 36 else g - 36
            return (None, lambda si: [(slots[si][:].bitcast(F32), kview(src[:, gg * 256:(gg + 1) * 256]))])

        def d_ffn_in(nm, wi, j):
            return (nm, lambda si: [
                (slots[si][:, :, 0:256], kview(wi[:, 256 * j:256 * j + 256])),
                (slots[si][:, :, 256:512], kview(wi[:, DFF + 256 * j:DFF + 256 * j + 256]))])

        def d_ffn_out(nm, wo, ch, k0, nk):
            return (nm, lambda si: [(slots[si][:, 0:nk, :], kview(wo[k0 * 128:(k0 + nk) * 128, ch * 512:(ch + 1) * 512]))])

        def d_cols(nm, w, c0):
            return (nm, lambda si: [(slots[si][:, :, :], kview(w[:, c0:c0 + 512]))])

        def d_glu(half):
            return ("win", lambda si: [
                (slots[si][:, :, 0:256], kview(bin_[:, 256 * half:256 * half + 256])),
                (slots[si][:, :, 256:512], kview(bin_[:, 512 + 256 * half:512 + 256 * half + 256]))])

        def d_conv(c):
            return ("dcw", lambda si: [
                (slots[si][:].rearrange("p k n -> p (k n)")[:, 0:NCONV * 128], bdc[c * 128:(c + 1) * 128, :])])

        def mm_group(bank, pairs, reads, first=True, last=True):
            n = len(pairs)

            def fn(h):
                out = []
                for idx, (l, r) in enumerate(pairs):
                    out.append(h.matmul(ps[:, bank, :], lhsT=l, rhs=r,
                                        start=(first and idx == 0), stop=(last and idx == n - 1)))
                return out
            return E(PE, fn, reads=reads, writes=[r_bank[bank]])

        sh1 = modv[:, 0:8]; sh2 = modv[:, 24:32]; gt2 = modv[:, 40:48]; sh3 = modv[:, 48:56]; shf = modv[:, 72:80]

        def mod_part(g0, g1):
            bank = next_bank()
            for g in range(g0, g1):
                si = next_slab(*d_mod(g))
                sf = slots[si][:].bitcast(F32)

                def fn(h, g=g, sf=sf):
                    out = []
                    for sub in range(2):
                        j = 2 * g + sub
                        for kc in range(8):
                            out.append(h.matmul(ps[:, bank, j:j + 1], lhsT=sf[:, kc, sub * 128:(sub + 1) * 128],
                                                rhs=cact[:, kc:kc + 1], start=(kc == 0), stop=(kc == 7)))
                    return out
                E(PE, fn, reads=[r_slot[si], r_cact], writes=[r_bank[bank]])
                release_slab()
            c0, c1 = 2 * g0, 2 * g1
            E(DVE, lambda h: [h.tensor_tensor(out=modv[:, c0:c1], in0=ps[:, bank, c0:c1], in1=vecs[:, V_BMOD + c0:V_BMOD + c1], op=ALU.add)],
              reads=[r_bank[bank], r_vecs], writes=[r_modv])

        def gs_op(dst, sc_col, g_col):
            E(DVE, lambda h: [h.scalar_tensor_tensor(out=dst[:], in0=modv[:, sc_col:sc_col + 8], scalar=1.0,
                                                     in1=vecs[:, g_col:g_col + 8], op0=ALU.add, op1=ALU.mult)],
              reads=[r_modv, r_vecs], writes=[r_derived])

        def half_op(dst, col):
            E(DVE, lambda h: [h.tensor_scalar(out=dst[:], in0=modv[:, col:col + 8], scalar1=0.5, scalar2=None, op0=ALU.mult)],
              reads=[r_modv], writes=[r_derived])

        def rsqrt_chain(buf, res, bank, scale):
            E(ACT, lambda h: [h.activation(out=buf[:], in_=ps[:, bank, :], func=AF.Sqrt, bias=EPS, scale=scale)],
              reads=[r_bank[bank]], writes=[res])
            E(DVE, lambda h: [h.reciprocal(out=buf[:], in_=buf[:])], reads=[res], writes=[res])

        def compute_xsq_and_stats():
            sbank = next_bank()
            for fc in range(8):
                q = nxt("xsq", 3)
                E(ACT, lambda h, q=q, fc=fc: [h.activation(out=xsq[q][:], in_=xT[:, fc, :], func=AF.Square)],
                  reads=[r_xT[fc]], writes=[r_xsq[q]])
                mm_group(sbank, [(ones_bf[:], xsq[q][:])], [r_xsq[q], r_const], first=(fc == 0), last=(fc == 7))
            rsqrt_chain(rstd, r_rstd, sbank, 1.0 / D)

        def norm_apply(gs, sh):
            for fc in range(8):
                q = nxt("ntmp", 2)
                E(DVE, lambda h, q=q, fc=fc: [h.scalar_tensor_tensor(
                    out=ntmp[q][:], in0=xT[:, fc, :], scalar=gs[:, fc:fc + 1], in1=rstd[:], op0=ALU.mult, op1=ALU.mult)],
                    reads=[r_xT[fc], r_rstd, r_derived], writes=[r_ntmp[q]])
                E(ACT, lambda h, q=q, fc=fc: [h.activation(out=hb[:, fc, :], in_=ntmp[q][:], func=AF.Identity,
                                                           bias=sh[:, fc:fc + 1], scale=1.0)],
                  reads=[r_ntmp[q], r_modv], writes=[r_hb[fc]])

        def ffn(nm_i, wi, nm_o, wo, gth, hook=None):
            for j in range(11):
                si = next_slab(*d_ffn_in(nm_i, wi, j))
                for sub in range(2):
                    cg = 2 * j + sub
                    bg = next_bank()
                    mm_group(bg, [(slots[si][:, kc, sub * 128:(sub + 1) * 128], hb[:, kc, :]) for kc in range(8)],
                             [r_slot[si]] + r_hb)
                    bu = next_bank()
                    mm_group(bu, [(slots[si][:, kc, 256 + sub * 128:256 + (sub + 1) * 128], hb[:, kc, :]) for kc in range(8)],
                             [r_slot[si]] + r_hb)
                    q = nxt("sgb", 2)
                    E(ACT, lambda h, q=q, bg=bg: [h.activation(out=sgb[q][:], in_=ps[:, bg, :], func=AF.Silu)],
                      reads=[r_bank[bg]], writes=[r_sgb[q]])
                    E(DVE, lambda h, q=q, bu=bu, cg=cg: [h.tensor_tensor(out=hid[:, cg, :], in0=ps[:, bu, :], in1=sgb[q][:], op=ALU.mult)],
                      reads=[r_bank[bu], r_sgb[q]], writes=[r_hid[cg]])
                release_slab()
            if hook is not None:
                hook()
            for ch in range(2):
                banks = [next_bank() for _ in range(4)]
                for (k0, nk) in ((0, 8), (8, 8), (16, 6)):
                    si = next_slab(*d_ffn_out(nm_o, wo, ch, k0, nk))
                    for oc in range(4):
                        mm_group(banks[oc],
                                 [(slots[si][:, kl, oc * 128:(oc + 1) * 128], hid[:, k0 + kl, :]) for kl in range(nk)],
                                 [r_slot[si]] + r_hid[k0:k0 + nk], first=(k0 == 0), last=(k0 == 16))
                    release_slab()
                for oc in range(4):
                    fc = 4 * ch + oc
                    E(DVE, lambda h, fc=fc, b=banks[oc]: [h.scalar_tensor_tensor(
                        out=xT[:, fc, :], in0=ps[:, b, :], scalar=gth[:, fc:fc + 1], in1=xT[:, fc, :], op0=ALU.mult, op1=ALU.add)],
                        reads=[r_bank[banks[oc]], r_derived, r_modv], writes=[r_xT[fc]])

        def mixer(hook=None):
            si = next_slab(*d_cols("win", bin_, 1024))
            for c in range(4):
                b = next_bank()
                mm_group(b, [(slots[si][:, kc, c * 128:(c + 1) * 128], hb[:, kc, :]) for kc in range(8)], [r_slot[si]] + r_hb)
                E(ACT, lambda h, c=c, b=b: [h.activation(out=ux[:, c, 3:3 + T], in_=ps[:, b, :], func=AF.Copy)],
                  reads=[r_bank[b]], writes=[r_ux[c]])
            release_slab()
            for half in range(2):
                si = next_slab(*d_glu(half))
                for sub in range(2):
                    c = 2 * half + sub
                    bv = next_bank()
                    mm_group(bv, [(slots[si][:, kc, sub * 128:(sub + 1) * 128], hb[:, kc, :]) for kc in range(8)], [r_slot[si]] + r_hb)
                    bg = next_bank()
                    mm_group(bg, [(slots[si][:, kc, 256 + sub * 128:256 + (sub + 1) * 128], hb[:, kc, :]) for kc in range(8)], [r_slot[si]] + r_hb)
                    q = nxt("sgb", 2)
                    E(ACT, lambda h, q=q, bg=bg: [h.activation(out=sgb[q][:], in_=ps[:, bg, :], func=AF.Sigmoid)],
                      reads=[r_bank[bg]], writes=[r_sgb[q]])
                    E(DVE, lambda h, q=q, bv=bv, c=c: [h.tensor_tensor(out=ub[:, c, 30:30 + T], in0=ps[:, bv, :], in1=sgb[q][:], op=ALU.mult)],
                      reads=[r_bank[bv], r_sgb[q]], writes=[r_ub[c]])
                release_slab()

            hs_of = {}

            def rnn_pair_a(c0):
                cs = (c0, c0 + 1)
                qx = {}
                for c in cs:
                    q = nxt("xr", 2)
                    qx[c] = q
                    E(DVE, lambda h, c=c, q=q: [h.tensor_scalar(
                        out=xr[q][:], in0=ux[:, c, 0:T], scalar1=vecs[:, V_RCW + 4 * c:V_RCW + 4 * c + 1],
                        scalar2=vecs[:, V_RCB + c:V_RCB + c + 1], op0=ALU.mult, op1=ALU.add)],
                        reads=[r_ux[c], r_vecs], writes=[r_xr[q]])
                    for k in range(1, 4):
                        E(DVE, lambda h, c=c, q=q, k=k: [h.scalar_tensor_tensor(
                            out=xr[q][:], in0=ux[:, c, k:k + T], scalar=vecs[:, V_RCW + 4 * c + k:V_RCW + 4 * c + k + 1],
                            in1=xr[q][:], op0=ALU.mult, op1=ALU.add)],
                            reads=[r_ux[c], r_vecs, r_xr[q]], writes=[r_xr[q]])
                    E(DVE, lambda h, c=c: [h.tensor_copy(out=ux[:, c, 0:3], in_=ux[:, c, T:T + 3])], reads=[r_ux[c]], writes=[r_ux[c]])
                    E(ACT, lambda h, q=q: [h.activation(out=xrb[q][:], in_=xr[q][:], func=AF.Copy)],
                      reads=[r_xr[q]], writes=[r_xrb[q]])
                br = {}; bi = {}
                for c in cs:
                    q = qx[c]
                    br[c] = next_bank()
                    mm_group(br[c], [(wabd[:, c, :], xrb[q][:])], [r_gates, r_xrb[q]])
                    bi[c] = next_bank()
                    mm_group(bi[c], [(wibd[:, c, :], xrb[q][:])], [r_gates, r_xrb[q]])
                for c in cs:
                    q = qx[c]
                    E(ACT, lambda h, c=c, q=q: [h.activation(out=rbuf[q][:], in_=ps[:, br[c], :], func=AF.Sigmoid,
                                                           bias=vecs[:, V_BA + c:V_BA + c + 1], scale=1.0)],
                      reads=[r_bank[br[c]], r_vecs], writes=[r_rbuf[q]])
                    E(ACT, lambda h, c=c, q=q: [h.activation(out=btb[q][:], in_=ps[:, bi[c], :], func=AF.Sigmoid,
                                                           bias=vecs[:, V_BI + c:V_BI + c + 1], scale=1.0)],
                      reads=[r_bank[bi[c]], r_vecs], writes=[r_btb[q]])
                for c in cs:
                    q = qx[c]
                    E(ACT, lambda h, c=c, q=q: [h.activation(out=abuf[q][:], in_=rbuf[q][:], func=AF.Exp, scale=lam8[:, c:c + 1])],
                      reads=[r_rbuf[q], r_derived], writes=[r_abuf[q]])
                    E(ACT, lambda h, c=c, q=q: [h.activation(out=mbuf[q][:], in_=rbuf[q][:], func=AF.Exp, scale=lam16[:, c:c + 1])],
                      reads=[r_rbuf[q], r_derived], writes=[r_mbuf[q]])
                for c in cs:
                    q = qx[c]
                    E(DVE, lambda h, q=q: [h.tensor_scalar(out=mbuf[q][:], in0=mbuf[q][:], scalar1=1.0, scalar2=-1.0, op0=ALU.min, op1=ALU.mult)],
                      reads=[r_mbuf[q]], writes=[r_mbuf[q]])
                    E(DVE, lambda h, q=q: [h.tensor_tensor(out=btb[q][:], in0=btb[q][:], in1=xr[q][:], op=ALU.mult)],
                      reads=[r_btb[q], r_xr[q]], writes=[r_btb[q]])
                for c in cs:
                    q = qx[c]
                    E(ACT, lambda h, q=q: [h.activation(out=mbuf[q][:], in_=mbuf[q][:], func=AF.Sqrt, bias=1.0, scale=1.0)],
                      reads=[r_mbuf[q]], writes=[r_mbuf[q]])
                for c in cs:
                    q = qx[c]
                    E(DVE, lambda h, q=q: [h.tensor_tensor(out=btb[q][:], in0=btb[q][:], in1=mbuf[q][:], op=ALU.mult)],
                      reads=[r_btb[q], r_mbuf[q]], writes=[r_btb[q]])
                    E(DVE, lambda h, c=c, q=q: [h.tensor_tensor_scan(out=hs[c][:], data0=abuf[q][:], data1=btb[q][:],
                                                                     initial=hstate[:, c:c + 1], op0=ALU.mult, op1=ALU.add)],
                      reads=[r_abuf[q], r_btb[q], r_hstate[c]], writes=[r_hs[c]])
                    E(DVE, lambda h, c=c: [h.tensor_copy(out=hstate[:, c:c + 1], in_=hs[c][:, T - 1:T])],
                      reads=[r_hs[c]], writes=[r_hstate[c]])

            def rnn_b(c):
                E(DVE, lambda h, c=c: [h.tensor_tensor(out=hb[:, 4 + c, :], in0=hs[c][:], in1=gy[:, c, :], op=ALU.mult)],
                  reads=[r_hs[c], r_gy[c]], writes=[r_hb[4 + c]])

            def conv_chunk(c):
                si = next_slab(*d_conv(c))
                dflat = slots[si][:].rearrange("p k n -> p (k n)")
                b = next_bank()
                mm_group(b, [(dflat[:, k * 128:(k + 1) * 128], ub[:, c, k:k + T]) for k in range(NCONV)], [r_slot[si], r_ub[c]])
                release_slab()
                E(ACT, lambda h: [h.activation(out=cv[:, c, :], in_=ps[:, b, :], func=AF.Identity,
                                               bias=vecs[:, V_CONVB + c:V_CONVB + c + 1], scale=1.0)],
                  reads=[r_bank[b], r_vecs], writes=[r_cv[c]])
                E(DVE, lambda h: [h.tensor_copy(out=ub[:, c, 0:30], in_=ub[:, c, T:T + 30])], reads=[r_ub[c]], writes=[r_ub[c]])

            rnn_pair_a(0)
            rnn_pair_a(2)
            for c in range(4):
                conv_chunk(c)
            si = next_slab(*d_cols("win", bin_, 1536))
            for c in range(4):
                b = next_bank()
                mm_group(b, [(slots[si][:, kc, c * 128:(c + 1) * 128], hb[:, kc, :]) for kc in range(8)], [r_slot[si]] + r_hb)
                E(ACT, lambda h, c=c, b=b: [h.activation(out=gy[:, c, :], in_=ps[:, b, :], func=AF.Gelu_apprx_tanh)],
                  reads=[r_bank[b]], writes=[r_gy[c]])
            release_slab()
            if hook is not None:
                hook()
            bs = next_bank()
            bq = next_bank()
            for c in range(4):
                q = nxt("cvb", 2)
                E(DVE, lambda h, c=c, q=q: [h.tensor_copy(out=cvb[q][:], in_=cv[:, c, :])], reads=[r_cv[c]], writes=[r_cvb[q]])
                E(ACT, lambda h, c=c, q=q: [h.activation(out=cvsq[q][:], in_=cv[:, c, :], func=AF.Square)], reads=[r_cv[c]], writes=[r_cvsq[q]])
                mm_group(bs, [(ones_bf[:], cvb[q][:])], [r_cvb[q], r_const], first=(c == 0), last=(c == 3))
                mm_group(bq, [(ones_bf[:], cvsq[q][:])], [r_cvsq[q], r_const], first=(c == 0), last=(c == 3))
            E(ACT, lambda h: [h.activation(out=meanb[:], in_=ps[:, bs, :], func=AF.Copy, scale=1.0 / 512)],
              reads=[r_bank[bs]], writes=[r_meanb])
            E(ACT, lambda h: [h.activation(out=varb[:], in_=ps[:, bs, :], func=AF.Square, scale=1.0 / 512)],
              reads=[r_bank[bs]], writes=[r_varb])
            E(DVE, lambda h: [h.scalar_tensor_tensor(out=varb[:], in0=ps[:, bq, :], scalar=1.0 / 512, in1=varb[:],
                                                     op0=ALU.mult, op1=ALU.subtract)],
              reads=[r_bank[bq], r_varb], writes=[r_varb])
            E(ACT, lambda h: [h.activation(out=varb[:], in_=varb[:], func=AF.Sqrt, bias=EPS, scale=1.0)],
              reads=[r_varb], writes=[r_varb])
            E(DVE, lambda h: [h.reciprocal(out=varb[:], in_=varb[:])], reads=[r_varb], writes=[r_varb])
            for c in range(4):
                rnn_b(c)
            for c in range(4):
                q = nxt("ntmp", 2)
                E(DVE, lambda h, c=c, q=q: [h.tensor_tensor(out=ntmp[q][:], in0=cv[:, c, :], in1=meanb[:], op=ALU.subtract)],
                  reads=[r_cv[c], r_meanb], writes=[r_ntmp[q]])
                E(DVE, lambda h, q=q: [h.tensor_tensor(out=ntmp[q][:], in0=ntmp[q][:], in1=varb[:], op=ALU.mult)],
                  reads=[r_ntmp[q], r_varb], writes=[r_ntmp[q]])
                E(ACT, lambda h, c=c, q=q: [h.activation(out=hb[:, c, :], in_=ntmp[q][:], func=AF.Silu,
                                                       bias=vecs[:, V_LNB + c:V_LNB + c + 1], scale=vecs[:, V_LNG + c:V_LNG + c + 1])],
                  reads=[r_ntmp[q], r_vecs], writes=[r_hb[c]])
            for ch in range(2):
                si = next_slab(*d_cols("wout", bout, ch * 512))
                for oc in range(4):
                    fc = 4 * ch + oc
                    b = next_bank()
                    mm_group(b, [(slots[si][:, kc, oc * 128:(oc + 1) * 128], hb[:, kc, :]) for kc in range(8)], [r_slot[si]] + r_hb)
                    E(DVE, lambda h, fc=fc, b=b: [h.scalar_tensor_tensor(
                        out=xT[:, fc, :], in0=ps[:, b, :], scalar=gt2[:, fc:fc + 1], in1=xT[:, fc, :], op0=ALU.mult, op1=ALU.add)],
                        reads=[r_bank[b], r_modv], writes=[r_xT[fc]])
                release_slab()

        def program():
            for i in range(n_tiles):
                first = (i == 0)
                for fc in range(8):
                    b = next_bank()

                    def fn(h, fc=fc, b=b):
                        return [h.transpose(out=ps[:, b, blk * 128:(blk + 1) * 128], in_=xin[:, blk, fc * 128:(fc + 1) * 128],
                                            identity=ident[:]) for blk in range(4)]
                    E(PE, fn, reads=[r_xin, r_const], writes=[r_bank[b]])
                    E(DVE, lambda h, fc=fc, b=b: [h.tensor_copy(out=xT[:, fc, :], in_=ps[:, b, :])],
                      reads=[r_bank[b]], writes=[r_xT[fc]])
                if i + 1 < n_tiles and not st["dry"]:
                    load_x(i + 1)
                if first:
                    mod_part(0, 8); gs_op(gs1, 8, V_G1)
                compute_xsq_and_stats()
                norm_apply(gs1, sh1)
                ffn("w1i", b1i, "w1o", b1o, gth1,
                    hook=(lambda: (mod_part(8, 12), half_op(gth1, 16))) if first else None)
                if first:
                    mod_part(12, 20); gs_op(gs2, 32, V_GM)
                compute_xsq_and_stats()
                norm_apply(gs2, sh2)
                mixer(hook=(lambda: mod_part(20, 24)) if first else None)
                if first:
                    mod_part(24, 32); gs_op(gs3, 56, V_G2)
                compute_xsq_and_stats()
                norm_apply(gs3, sh3)
                ffn("w2i", b2i, "w2o", b2o, gth3,
                    hook=(lambda: (mod_part(32, 36), half_op(gth3, 64))) if first else None)
                if first:
                    mod_part(36, 44); gs_op(gsf, 80, V_GF)
                compute_xsq_and_stats()
                for fc in range(8):
                    q = nxt("ntmp", 2)
                    qo = nxt("oT", 2)
                    E(DVE, lambda h, q=q, fc=fc: [h.scalar_tensor_tensor(
                        out=ntmp[q][:], in0=xT[:, fc, :], scalar=gsf[:, fc:fc + 1], in1=rstd[:], op0=ALU.mult, op1=ALU.mult)],
                        reads=[r_xT[fc], r_rstd, r_derived], writes=[r_ntmp[q]])
                    E(ACT, lambda h, q=q, qo=qo, fc=fc: [h.activation(out=oT[qo][:], in_=ntmp[q][:], func=AF.Identity,
                                                                     bias=shf[:, fc:fc + 1], scale=1.0)],
                      reads=[r_ntmp[q], r_modv], writes=[r_oT[qo]])
                    b = next_bank()

                    def fn(h, qo=qo, b=b):
                        return [h.transpose(out=ps[:, b, blk * 128:(blk + 1) * 128], in_=oT[qo][:, blk * 128:(blk + 1) * 128],
                                            identity=ident[:]) for blk in range(4)]
                    E(PE, fn, reads=[r_oT[qo], r_const], writes=[r_bank[b]])
                    if fc % 2 == 0:
                        E(DVE, lambda h, fc=fc, b=b: [h.tensor_copy(out=oout[:, :, fc * 128:(fc + 1) * 128], in_=ps[:, b, :].rearrange("p (k n) -> p k n", k=4))],
                          reads=[r_bank[b]], writes=[r_oout[fc]])
                    else:
                        E(ACT, lambda h, fc=fc, b=b: [h.activation(out=oout[:, :, fc * 128:(fc + 1) * 128], in_=ps[:, b, :].rearrange("p (k n) -> p k n", k=4), func=AF.Copy)],
                          reads=[r_bank[b]], writes=[r_oout[fc]])
                if not st["dry"]:
                    dst = y_d[i * T:(i + 1) * T, :].rearrange("(b p) d -> p b d", p=128)
                    dma1(PL, dst, oout[:], r_oout, [], sem_out)

        program()
        st["dry"] = False
        bank_ctr[0] = 0
        rot.clear()
        for _ in range(NS):
            issue_slab()
        program()

        final_val = sem_out.val
        assert st["used"] == len(plan), (st, len(plan))

        with nc.Block() as block:
            @block.sync
            def _(h):
                replay(SP, h)

            @block.gpsimd
            def _(h):
                replay(PL, h)
                h.wait_ge(sem_out.h, final_val)

            @block.tensor
            def _(h):
                replay(PE, h)

            @block.scalar
            def _(h):
                replay(ACT, h)

            @block.vector
            def _(h):
                replay(DVE, h)
    return nc


def prep_inputs(inputs, n_tiles=SEQ // T, cores=N_CORES):
    f = lambda a: np.ascontiguousarray(np.asarray(a, dtype=np.float32))
    col = lambda v, n: f(v).reshape(n, 128).T
    x = f(inputs["x"]); c = f(inputs["c"])
    shared = np.zeros((128, NV), np.float32)
    shared[:, V_G1:V_G1 + 8] = col(inputs["g_ffn1"][0], 8)
    shared[:, V_GM:V_GM + 8] = col(inputs["g_mix"][0], 8)
    shared[:, V_G2:V_G2 + 8] = col(inputs["g_ffn2"][0], 8)
    shared[:, V_GF:V_GF + 8] = col(inputs["g_final"], 8)
    shared[:, V_BMOD:V_BMOD + 72] = col(inputs["b_mod"][0], 72)
    shared[:, V_BMOD + 72:V_BMOD + 88] = col(inputs["b_fmod"], 16)
    shared[:, V_CONVB:V_CONVB + 4] = col(inputs["conv_b"][0], 4)
    shared[:, V_LNG:V_LNG + 4] = col(inputs["ln_g"][0], 4)
    shared[:, V_LNB:V_LNB + 4] = col(inputs["ln_b"][0], 4)
    shared[:, V_RCB:V_RCB + 4] = col(inputs["rnn_conv_b"][0], 4)
    shared[:, V_BA:V_BA + 4] = col(inputs["b_a"][0], 4)
    shared[:, V_BI:V_BI + 4] = col(inputs["b_i"][0], 4)
    shared[:, V_LAM:V_LAM + 4] = col(inputs["lru_lambda"][0], 4)
    rcw = f(inputs["rnn_conv_w"][0])
    for cc in range(4):
        for k in range(4):
            shared[:, V_RCW + 4 * cc + k] = rcw[k, cc * 128:(cc + 1) * 128]
    cw = f(inputs["conv_w"][0])
    dcw = np.zeros((4, 128, NCONV, 128), np.float32)
    ar = np.arange(128)
    for cc in range(4):
        for k in range(NCONV):
            dcw[cc, ar, k, ar] = cw[k, cc * 128:(cc + 1) * 128]
    dcw = dcw.reshape(512, NCONV * 128)
    def bd(w):
        w = f(w)
        o = np.zeros((128, 4, 128), np.float32)
        for cc in range(4):
            o[0:64, cc, 0:64] = w[2 * cc]
            o[64:128, cc, 64:128] = w[2 * cc + 1]
        return o.reshape(128, 512)
    wabd = bd(inputs["w_a"][0]); wibd = bd(inputs["w_i"][0])
    common = {
        "ident": np.eye(128, dtype=np.float32), "wabd": wabd, "wibd": wibd,
        "w_mod": f(inputs["w_mod"][0]), "w_fmod": f(inputs["w_fmod"]),
        "w1i": f(inputs["w_ffn1_in"][0]), "w1o": f(inputs["w_ffn1_out"][0]),
        "win": f(inputs["w_in"][0]), "wout": f(inputs["w_out"][0]),
        "w2i": f(inputs["w_ffn2_in"][0]), "w2o": f(inputs["w_ffn2_out"][0]),
        "dcw": dcw,
    }
    in_maps = []
    S = n_tiles * T
    for b in range(cores):
        v = shared.copy()
        v[:, V_C:V_C + 8] = c[b].reshape(8, 128).T
        m = dict(common)
        m["vecs"] = v
        m["x"] = np.ascontiguousarray(x[b, :S, :])
        in_maps.append(m)
    return in_maps


_NC_CACHE = {}


def kernel(**inputs):
    n_tiles = SEQ // T
    if n_tiles not in _NC_CACHE:
        _NC_CACHE[n_tiles] = build_program(n_tiles)
    nc = _NC_CACHE[n_tiles]
    in_maps = prep_inputs(inputs, n_tiles, N_CORES)
    res = run_bass_kernel_spmd(nc, in_maps, core_ids=list(range(N_CORES)))
    out = np.stack([np.asarray(r["y"], dtype=np.float32) for r in res.results], axis=0)
    return out
```

```python
import numpy as np
import concourse.bass as bass
import concourse.mybir as mybir
from concourse.bass_utils import run_bass_kernel_spmd

F32 = mybir.dt.float32
BF16 = mybir.dt.bfloat16
AF = mybir.ActivationFunctionType
ALU = mybir.AluOpType

D = 1024
DFF = 2816
SEQ = 4096
T = 512
KC = 8
FC = 22
NCONV = 31
EPS = 1e-6
NS = 3
N_CORES = 8

V_G1, V_GM, V_G2, V_GF = 0, 8, 16, 24
V_BMOD = 32
V_CONVB = 120
V_LNG = 124
V_LNB = 128
V_RCB = 132
V_BA = 136
V_BI = 140
V_LAM = 144
V_RCW = 148
V_C = 164
NV = 172


class Sem:
    def __init__(self, h):
        self.h = h
        self.val = 0


class Res:
    __slots__ = ("w", "r")

    def __init__(self):
        self.w = None
        self.r = {}


class Eng:
    def __init__(self, name, sem, self_sync):
        self.name = name
        self.sem = sem
        self.prog = []
        self.seen = {}
        self.self_sync = self_sync


def emit(eng, fn, reads=(), writes=(), dma=None, ndma=1):
    waits = {}

    def need(tok):
        if tok is None:
            return
        s, v = tok
        if waits.get(s, 0) < v:
            waits[s] = v

    for r in reads:
        need(r.w)
    for w in writes:
        need(w.w)
        for s, v in w.r.items():
            need((s, v))
    wl = []
    for s, v in waits.items():
        if s is eng.sem and not eng.self_sync:
            continue
        if eng.seen.get(s, 0) >= v:
            continue
        eng.seen[s] = v
        wl.append((s, v))
    if dma is None:
        eng.sem.val += 1
        tok = (eng.sem, eng.sem.val)
        eng.prog.append((wl, fn, eng.sem, 1, False))
    else:
        dma.val += 16 * ndma
        tok = (dma, dma.val)
        eng.prog.append((wl, fn, dma, 16, True))
    for r in reads:
        if r.r.get(tok[0], 0) < tok[1]:
            r.r[tok[0]] = tok[1]
    for w in writes:
        w.w = tok
        w.r = {}
    return tok


def replay(eng, h):
    for wl, fn, sem, inc, is_dma in eng.prog:
        for s, v in wl:
            h.wait_ge(s.h, v)
        ins = fn(h)
        if is_dma:
            for i in ins:
                i.then_inc(sem.h, 16)
        else:
            ins[-1].then_inc(sem.h, 1)


def build_program(n_tiles=SEQ // T):
    S = n_tiles * T
    nc = bass.Bass("TRN2", target_bir_lowering=False)
    dt_in = lambda name, shape: nc.dram_tensor(name, shape, F32, kind="ExternalInput").ap()
    x_d = dt_in("x", [S, D])
    vecs_d = dt_in("vecs", [128, NV])
    ident_d = dt_in("ident", [128, 128])
    wabd_d = dt_in("wabd", [128, 512])
    wibd_d = dt_in("wibd", [128, 512])
    wmod_d = dt_in("w_mod", [D, 9 * D])
    wfmod_d = dt_in("w_fmod", [D, 2 * D])
    w1i_d = dt_in("w1i", [D, 2 * DFF])
    w1o_d = dt_in("w1o", [DFF, D])
    win_d = dt_in("win", [D, 2 * D])
    wout_d = dt_in("wout", [D, D])
    w2i_d = dt_in("w2i", [D, 2 * DFF])
    w2o_d = dt_in("w2o", [DFF, D])
    dcw_d = dt_in("dcw", [512, NCONV * 128])
    y_d = nc.dram_tensor("y", [S, D], F32, kind="ExternalOutput").ap()
    dt_sc = lambda name, shape: nc.dram_tensor(name, shape, BF16, kind="Internal").ap()
    b1i = dt_sc("b1i", [D, 2 * DFF])
    b1o = dt_sc("b1o", [DFF, D])
    bin_ = dt_sc("bin", [D, 2 * D])
    bout = dt_sc("bout", [D, D])
    b2i = dt_sc("b2i", [D, 2 * DFF])
    b2o = dt_sc("b2o", [DFF, D])
    bdc = dt_sc("bdc", [512, NCONV * 128])

    from contextlib import ExitStack
    es = ExitStack()
    with es:
        sb = lambda name, shape, dt=F32: es.enter_context(nc.sbuf_tensor("sb_" + name, shape, dt))
        newsem = lambda name: Sem(es.enter_context(nc.semaphore(name)))

        PE = Eng("pe", newsem("s_pe"), False)
        ACT = Eng("act", newsem("s_act"), True)
        DVE = Eng("dve", newsem("s_dve"), True)
        SP = Eng("sp", newsem("s_sp"), False)
        PL = Eng("pool", newsem("s_pool"), False)

        ident = sb("ident", [128, 128])
        ones_bf = sb("ones_bf", [128, 128], BF16)
        vecs = sb("vecs", [128, NV])
        modv = sb("modv", [128, 88])
        cact = sb("cact", [128, 8])
        gs1 = sb("gs1", [128, 8]); gth1 = sb("gth1", [128, 8])
        gs2 = sb("gs2", [128, 8])
        gs3 = sb("gs3", [128, 8]); gth3 = sb("gth3", [128, 8])
        gsf = sb("gsf", [128, 8])
        lam8 = sb("lam8", [128, 4]); lam16 = sb("lam16", [128, 4]); lamt = sb("lamt", [128, 4])
        wabd = sb("wabd", [128, 4, 128], BF16)
        wibd = sb("wibd", [128, 4, 128], BF16)
        hstate = sb("hstate", [128, 4])
        slots = [sb(f"slab{i}", [128, 8, 512], BF16) for i in range(NS)]
        xin = sb("xin", [128, 4, D])
        xTs = [sb(f"xT{k}", [128, 8, T]) for k in range(2)]
        xsq = [sb(f"xsq{i}", [128, T], BF16) for i in range(3)]
        hbs = [sb(f"hb{k}", [128, 8, T], BF16) for k in range(2)]
        hid = sb("hid", [128, FC, T], BF16)
        sgb = [sb(f"sgb{i}", [128, T]) for i in range(2)]
        ntmp = [sb(f"ntmp{i}", [128, T]) for i in range(2)]
        rstds = [sb(f"rstd{k}", [128, T]) for k in range(2)]
        ub = sb("ub", [128, 4, T + 30], BF16)
        ux = sb("ux", [128, 4, T + 3])
        gy = sb("gy", [128, 4, T], BF16)
        cv = sb("cv", [128, 4, T])
        cvsq = [sb(f"cvsq{i}", [128, T], BF16) for i in range(2)]
        cvb = [sb(f"cvb{i}", [128, T], BF16) for i in range(2)]
        meanb = sb("meanb", [128, T])
        varb = sb("varb", [128, T])
        xr = [sb(f"xr{i}", [128, T]) for i in range(2)]
        xrb = [sb(f"xrb{i}", [128, T], BF16) for i in range(2)]
        rbuf = [sb(f"rbuf{i}", [128, T]) for i in range(2)]
        abuf = [sb(f"abuf{i}", [128, T]) for i in range(2)]
        mbuf = [sb(f"mbuf{i}", [128, T]) for i in range(2)]
        btb = [sb(f"btb{i}", [128, T]) for i in range(2)]
        hs = [sb(f"hs{i}", [128, T]) for i in range(4)]
        oT = [sb(f"oT{i}", [128, T]) for i in range(2)]
        oo = [sb(f"oo{k}", [128, 4, 128]) for k in range(3)]
        ps = es.enter_context(nc.psum_tensor("ps", [128, 8, T], F32))

        R = lambda: Res()
        r_const = R()
        r_vecs = R(); r_modv = R(); r_cact = R(); r_derived = R(); r_hstate = [R() for _ in range(4)]
        r_gates = R()
        r_slot = [R() for _ in range(NS)]
        sem_slot = [newsem(f"s_slot{i}") for i in range(NS)]
        r_bank = [R() for _ in range(8)]
        r_xin = R(); sem_xin = newsem("s_xin")
        r_xTs = [[R() for _ in range(8)] for _ in range(2)]
        r_xsq = [R() for _ in range(3)]
        r_hbs = [[R() for _ in range(8)] for _ in range(2)]
        r_hid = [R() for _ in range(FC)]
        r_sgb = [R() for _ in range(2)]
        r_ntmp = [R() for _ in range(2)]
        r_rstds = [R(), R()]
        r_ub = [R() for _ in range(4)]
        r_ux = [R() for _ in range(4)]
        r_gy = [R() for _ in range(4)]
        r_cv = [R() for _ in range(4)]
        r_cvsq = [R() for _ in range(2)]
        r_cvb = [R() for _ in range(2)]
        r_meanb = R(); r_varb = R()
        r_xr = [R() for _ in range(2)]; r_xrb = [R() for _ in range(2)]
        r_rbuf = [R() for _ in range(2)]; r_abuf = [R() for _ in range(2)]
        r_mbuf = [R() for _ in range(2)]; r_btb = [R() for _ in range(2)]
        r_hs = [R() for _ in range(4)]
        r_oT = [R() for _ in range(2)]
        r_oo = [R() for _ in range(3)]; sem_oo = [newsem(f"s_oo{k}") for k in range(3)]
        sem_const = newsem("s_const")
        sem_cast = {}
        r_cast = {}

        bank_ctr = [0]

        held = set()

        def next_bank(hold=False):
            while True:
                b = bank_ctr[0] % 8
                bank_ctr[0] += 1
                if b not in held:
                    break
            if hold:
                held.add(b)
            return b

        rot = {}

        def nxt(name, n):
            v = rot.get(name, 0)
            rot[name] = v + 1
            return v % n

        def dma1(eng, out_ap, in_ap, reads, writes, sem):
            return emit(eng, lambda h: [h.dma_start(out=out_ap, in_=in_ap)], reads=reads, writes=writes, dma=sem)

        dma1(SP, vecs[:], vecs_d[:, :], [], [r_vecs], sem_const)
        dma1(SP, ident[:], ident_d[:, :], [], [r_const], newsem("s_ident"))
        def load_x(i):
            src = x_d[i * T:(i + 1) * T, :].rearrange("(b p) d -> p b d", p=128)
            dma1(PL, xin[:], src, [], [r_xin], sem_xin)
        load_x(0)
        sem_g = newsem("s_gates")
        dma1(PL, wabd[:], wabd_d[:, :].rearrange("p (c n) -> p c n", c=4), [], [r_gates], sem_g)
        dma1(PL, wibd[:], wibd_d[:, :].rearrange("p (c n) -> p c n", c=4), [], [r_gates], sem_g)

        def cast_tensor(name, src, dst, rows):
            sem_cast[name] = newsem("s_c_" + name)
            r_cast[name] = R()
            nrow = src.shape[0]
            pieces = [(r0, min(rows, nrow - r0)) for r0 in range(0, nrow, rows)]

            def fn(h):
                return [h.dma_start(out=dst[r0:r0 + n, :], in_=src[r0:r0 + n, :]) for r0, n in pieces]
            emit(PL, fn, writes=[r_cast[name]], dma=sem_cast[name], ndma=len(pieces))

        cast_tensor("w1i", w1i_d, b1i, 128)
        cast_tensor("w1o", w1o_d, b1o, 256)
        cast_tensor("win", win_d, bin_, 256)
        cast_tensor("dcw", dcw_d, bdc, 128)
        cast_tensor("wout", wout_d, bout, 256)
        cast_tensor("w2i", w2i_d, b2i, 128)
        cast_tensor("w2o", w2o_d, b2o, 256)

        emit(DVE, lambda h: [h.memset(ones_bf[:], 1.0)], writes=[r_const])
        emit(DVE, lambda h: [h.memset(hstate[:], 0.0)], writes=r_hstate)
        emit(DVE, lambda h: [h.memset(ub[:, :, 0:30], 0.0)], writes=r_ub)
        emit(DVE, lambda h: [h.memset(ux[:, :, 0:3], 0.0)], writes=r_ux)
        emit(ACT, lambda h: [h.activation(out=cact[:], in_=vecs[:, V_C:V_C + 8], func=AF.Silu)],
             reads=[r_vecs], writes=[r_cact])
        emit(ACT, lambda h: [h.activation(out=lamt[:], in_=vecs[:, V_LAM:V_LAM + 4], func=AF.Exp, scale=-1.0)],
             reads=[r_vecs], writes=[r_derived])
        emit(ACT, lambda h: [h.activation(out=lamt[:], in_=lamt[:], func=AF.Ln, bias=1.0)],
             reads=[r_derived], writes=[r_derived])
        emit(DVE, lambda h: [h.tensor_scalar(out=lam8[:], in0=lamt[:], scalar1=-8.0, scalar2=None, op0=ALU.mult)],
             reads=[r_derived], writes=[r_derived])
        emit(DVE, lambda h: [h.tensor_scalar(out=lam16[:], in0=lamt[:], scalar1=-16.0, scalar2=None, op0=ALU.mult)],
             reads=[r_derived], writes=[r_derived])

        kview = lambda ap: ap.rearrange("(kc p) n -> p kc n", p=128)
        plan = []
        st = {"issued": 0, "used": 0, "dry": True}

        def issue_slab():
            k = st["issued"]
            if k >= len(plan):
                return
            st["issued"] += 1
            si = k % NS
            cname, pfn = plan[k]
            pieces = pfn(si)
            reads = [r_cast[cname]] if cname is not None else []

            def fn(h):
                return [h.dma_start(out=o, in_=i) for o, i in pieces]
            emit(SP, fn, reads=reads, writes=[r_slot[si]], dma=sem_slot[si], ndma=len(pieces))

        def next_slab(cname, pfn):
            if st["dry"]:
                plan.append((cname, pfn))
                return 0
            k = st["used"]
            st["used"] += 1
            assert k < st["issued"] and plan[k][0] == cname
            return k % NS

        def release_slab():
            if not st["dry"]:
                issue_slab()

        def E(eng, fn, reads=(), writes=()):
            if st["dry"]:
                return None
            return emit(eng, fn, reads=reads, writes=writes)

        def d_mod(g):
            src = wmod_d if g < 36 else wfmod_d
            gg = g if g < 36 else g - 36
            return (None, lambda si: [(slots[si][:].bitcast(F32), kview(src[:, gg * 256:(gg + 1) * 256]))])

        def d_ffn_in(nm, wi, j):
            return (nm, lambda si: [
                (slots[si][:, :, 0:256], kview(wi[:, 256 * j:256 * j + 256])),
                (slots[si][:, :, 256:512], kview(wi[:, DFF + 256 * j:DFF + 256 * j + 256]))])

        def d_ffn_out(nm, wo, ch, k0, nk):
            return (nm, lambda si: [(slots[si][:, 0:nk, :], kview(wo[k0 * 128:(k0 + nk) * 128, ch * 512:(ch + 1) * 512]))])

        def d_cols(nm, w, c0):
            return (nm, lambda si: [(slots[si][:, :, :], kview(w[:, c0:c0 + 512]))])

        def d_glu(half):
            return ("win", lambda si: [
                (slots[si][:, :, 0:256], kview(bin_[:, 256 * half:256 * half + 256])),
                (slots[si][:, :, 256:512], kview(bin_[:, 512 + 256 * half:512 + 256 * half + 256]))])

        def d_conv(c):
            return ("dcw", lambda si: [
                (slots[si][:].rearrange("p k n -> p (k n)")[:, 0:NCONV * 128], bdc[c * 128:(c + 1) * 128, :])])

        def mm_group(bank, pairs, reads, first=True, last=True):
            n = len(pairs)

            def fn(h):
                out = []
                for idx, (l, r) in enumerate(pairs):
                    out.append(h.matmul(ps[:, bank, :], lhsT=l, rhs=r,
                                        start=(first and idx == 0), stop=(last and idx == n - 1)))
                return out
            return E(PE, fn, reads=reads, writes=[r_bank[bank]])

        sh1 = modv[:, 0:8]; sh2 = modv[:, 24:32]; gt2 = modv[:, 40:48]; sh3 = modv[:, 48:56]; shf = modv[:, 72:80]

        def mod_part(g0, g1):
            bank = next_bank(hold=True)
            for g in range(g0, g1):
                si = next_slab(*d_mod(g))
                sf = slots[si][:].bitcast(F32)

                def fn(h, g=g, sf=sf):
                    out = []
                    for sub in range(2):
                        j = 2 * g + sub
                        for kc in range(8):
                            out.append(h.matmul(ps[:, bank, j:j + 1], lhsT=sf[:, kc, sub * 128:(sub + 1) * 128],
                                                rhs=cact[:, kc:kc + 1], start=(kc == 0), stop=(kc == 7)))
                    return out
                E(PE, fn, reads=[r_slot[si], r_cact], writes=[r_bank[bank]])
                release_slab()
                yield 'M'
            c0, c1 = 2 * g0, 2 * g1
            E(DVE, lambda h: [h.tensor_tensor(out=modv[:, c0:c1], in0=ps[:, bank, c0:c1], in1=vecs[:, V_BMOD + c0:V_BMOD + c1], op=ALU.add)],
              reads=[r_bank[bank], r_vecs], writes=[r_modv])
            held.discard(bank)

        def gs_op(dst, sc_col, g_col):
            E(DVE, lambda h: [h.scalar_tensor_tensor(out=dst[:], in0=modv[:, sc_col:sc_col + 8], scalar=1.0,
                                                     in1=vecs[:, g_col:g_col + 8], op0=ALU.add, op1=ALU.mult)],
              reads=[r_modv, r_vecs], writes=[r_derived])

        def half_op(dst, col):
            E(DVE, lambda h: [h.tensor_scalar(out=dst[:], in0=modv[:, col:col + 8], scalar1=0.5, scalar2=None, op0=ALU.mult)],
              reads=[r_modv], writes=[r_derived])

        def rsqrt_chain(buf, res, bank, scale):
            E(ACT, lambda h: [h.activation(out=buf[:], in_=ps[:, bank, :], func=AF.Sqrt, bias=EPS, scale=scale)],
              reads=[r_bank[bank]], writes=[res])
            E(DVE, lambda h: [h.reciprocal(out=buf[:], in_=buf[:])], reads=[res], writes=[res])

        def compute_xsq_and_stats(s):
            xT = xTs[s]; r_xT = r_xTs[s]; rstd = rstds[s]; r_rstd = r_rstds[s]
            yield 'e'
            sbank = next_bank(hold=True)
            for fc in range(8):
                q = nxt("xsq", 3)
                E(ACT, lambda h, q=q, fc=fc: [h.activation(out=xsq[q][:], in_=xT[:, fc, :], func=AF.Square)],
                  reads=[r_xT[fc]], writes=[r_xsq[q]])
                mm_group(sbank, [(ones_bf[:], xsq[q][:])], [r_xsq[q], r_const], first=(fc == 0), last=(fc == 7))
                if fc % 2 == 1:
                    yield 'e'
            rsqrt_chain(rstd, r_rstd, sbank, 1.0 / D)
            held.discard(sbank)

        def norm_apply(s, gs, sh):
            xT = xTs[s]; r_xT = r_xTs[s]; rstd = rstds[s]; r_rstd = r_rstds[s]; hb = hbs[s]; r_hb = r_hbs[s]
            for fc in range(8):
                if fc % 2 == 0:
                    yield 'e'
                q = nxt("ntmp", 2)
                E(DVE, lambda h, q=q, fc=fc: [h.scalar_tensor_tensor(
                    out=ntmp[q][:], in0=xT[:, fc, :], scalar=gs[:, fc:fc + 1], in1=rstd[:], op0=ALU.mult, op1=ALU.mult)],
                    reads=[r_xT[fc], r_rstd, r_derived], writes=[r_ntmp[q]])
                E(ACT, lambda h, q=q, fc=fc: [h.activation(out=hb[:, fc, :], in_=ntmp[q][:], func=AF.Identity,
                                                           bias=sh[:, fc:fc + 1], scale=1.0)],
                  reads=[r_ntmp[q], r_modv], writes=[r_hb[fc]])

        def ffn(s, nm_i, wi, nm_o, wo, gth, hook=None):
            xT = xTs[s]; r_xT = r_xTs[s]; hb = hbs[s]; r_hb = r_hbs[s]
            for j in range(11):
                yield 'M'
                si = next_slab(*d_ffn_in(nm_i, wi, j))
                for sub in range(2):
                    cg = 2 * j + sub
                    bg = next_bank()
                    mm_group(bg, [(slots[si][:, kc, sub * 128:(sub + 1) * 128], hb[:, kc, :]) for kc in range(8)],
                             [r_slot[si]] + r_hb)
                    bu = next_bank()
                    mm_group(bu, [(slots[si][:, kc, 256 + sub * 128:256 + (sub + 1) * 128], hb[:, kc, :]) for kc in range(8)],
                             [r_slot[si]] + r_hb)
                    q = nxt("sgb", 2)
                    E(ACT, lambda h, q=q, bg=bg: [h.activation(out=sgb[q][:], in_=ps[:, bg, :], func=AF.Silu)],
                      reads=[r_bank[bg]], writes=[r_sgb[q]])
                    E(DVE, lambda h, q=q, bu=bu, cg=cg: [h.tensor_tensor(out=hid[:, cg, :], in0=ps[:, bu, :], in1=sgb[q][:], op=ALU.mult)],
                      reads=[r_bank[bu], r_sgb[q]], writes=[r_hid[cg]])
                release_slab()
            if hook is not None:
                yield from hook()
            for ch in range(2):
                yield 'M'
                banks = [next_bank(hold=True) for _ in range(4)]
                for (k0, nk) in ((0, 8), (8, 8), (16, 6)):
                    si = next_slab(*d_ffn_out(nm_o, wo, ch, k0, nk))
                    for oc in range(4):
                        mm_group(banks[oc],
                                 [(slots[si][:, kl, oc * 128:(oc + 1) * 128], hid[:, k0 + kl, :]) for kl in range(nk)],
                                 [r_slot[si]] + r_hid[k0:k0 + nk], first=(k0 == 0), last=(k0 == 16))
                    release_slab()
                for oc in range(4):
                    fc = 4 * ch + oc
                    E(DVE, lambda h, fc=fc, b=banks[oc]: [h.scalar_tensor_tensor(
                        out=xT[:, fc, :], in0=ps[:, b, :], scalar=gth[:, fc:fc + 1], in1=xT[:, fc, :], op0=ALU.mult, op1=ALU.add)],
                        reads=[r_bank[banks[oc]], r_derived, r_modv], writes=[r_xT[fc]])
                for b in banks:
                    held.discard(b)

        mix_done = [0]

        def mixer(s, hook=None):
            xT = xTs[s]; r_xT = r_xTs[s]; hb = hbs[s]; r_hb = r_hbs[s]
            yield 'MIX'
            si = next_slab(*d_cols("win", bin_, 1024))
            for c in range(4):
                b = next_bank()
                mm_group(b, [(slots[si][:, kc, c * 128:(c + 1) * 128], hb[:, kc, :]) for kc in range(8)], [r_slot[si]] + r_hb)
                E(ACT, lambda h, c=c, b=b: [h.activation(out=ux[:, c, 3:3 + T], in_=ps[:, b, :], func=AF.Copy)],
                  reads=[r_bank[b]], writes=[r_ux[c]])
            release_slab()
            for half in range(2):
                yield 'M'
                si = next_slab(*d_glu(half))
                for sub in range(2):
                    c = 2 * half + sub
                    bv = next_bank()
                    mm_group(bv, [(slots[si][:, kc, sub * 128:(sub + 1) * 128], hb[:, kc, :]) for kc in range(8)], [r_slot[si]] + r_hb)
                    bg = next_bank()
                    mm_group(bg, [(slots[si][:, kc, 256 + sub * 128:256 + (sub + 1) * 128], hb[:, kc, :]) for kc in range(8)], [r_slot[si]] + r_hb)
                    q = nxt("sgb", 2)
                    E(ACT, lambda h, q=q, bg=bg: [h.activation(out=sgb[q][:], in_=ps[:, bg, :], func=AF.Sigmoid)],
                      reads=[r_bank[bg]], writes=[r_sgb[q]])
                    E(DVE, lambda h, q=q, bv=bv, c=c: [h.tensor_tensor(out=ub[:, c, 30:30 + T], in0=ps[:, bv, :], in1=sgb[q][:], op=ALU.mult)],
                      reads=[r_bank[bv], r_sgb[q]], writes=[r_ub[c]])
                release_slab()

            hs_of = {}

            def rnn_pair_a(c0):
                cs = (c0, c0 + 1)
                qx = {}
                for c in cs:
                    q = nxt("xr", 2)
                    qx[c] = q
                    E(DVE, lambda h, c=c, q=q: [h.tensor_scalar(
                        out=xr[q][:], in0=ux[:, c, 0:T], scalar1=vecs[:, V_RCW + 4 * c:V_RCW + 4 * c + 1],
                        scalar2=vecs[:, V_RCB + c:V_RCB + c + 1], op0=ALU.mult, op1=ALU.add)],
                        reads=[r_ux[c], r_vecs], writes=[r_xr[q]])
                    for k in range(1, 4):
                        E(DVE, lambda h, c=c, q=q, k=k: [h.scalar_tensor_tensor(
                            out=xr[q][:], in0=ux[:, c, k:k + T], scalar=vecs[:, V_RCW + 4 * c + k:V_RCW + 4 * c + k + 1],
                            in1=xr[q][:], op0=ALU.mult, op1=ALU.add)],
                            reads=[r_ux[c], r_vecs, r_xr[q]], writes=[r_xr[q]])
                    E(DVE, lambda h, c=c: [h.tensor_copy(out=ux[:, c, 0:3], in_=ux[:, c, T:T + 3])], reads=[r_ux[c]], writes=[r_ux[c]])
                    E(ACT, lambda h, q=q: [h.activation(out=xrb[q][:], in_=xr[q][:], func=AF.Copy)],
                      reads=[r_xr[q]], writes=[r_xrb[q]])
                br = {}; bi = {}
                for c in cs:
                    q = qx[c]
                    br[c] = next_bank()
                    mm_group(br[c], [(wabd[:, c, :], xrb[q][:])], [r_gates, r_xrb[q]])
                    bi[c] = next_bank()
                    mm_group(bi[c], [(wibd[:, c, :], xrb[q][:])], [r_gates, r_xrb[q]])
                for c in cs:
                    q = qx[c]
                    E(ACT, lambda h, c=c, q=q: [h.activation(out=rbuf[q][:], in_=ps[:, br[c], :], func=AF.Sigmoid,
                                                           bias=vecs[:, V_BA + c:V_BA + c + 1], scale=1.0)],
                      reads=[r_bank[br[c]], r_vecs], writes=[r_rbuf[q]])
                    E(ACT, lambda h, c=c, q=q: [h.activation(out=btb[q][:], in_=ps[:, bi[c], :], func=AF.Sigmoid,
                                                           bias=vecs[:, V_BI + c:V_BI + c + 1], scale=1.0)],
                      reads=[r_bank[bi[c]], r_vecs], writes=[r_btb[q]])
                for c in cs:
                    q = qx[c]
                    E(ACT, lambda h, c=c, q=q: [h.activation(out=abuf[q][:], in_=rbuf[q][:], func=AF.Exp, scale=lam8[:, c:c + 1])],
                      reads=[r_rbuf[q], r_derived], writes=[r_abuf[q]])
                    E(ACT, lambda h, c=c, q=q: [h.activation(out=mbuf[q][:], in_=rbuf[q][:], func=AF.Exp, scale=lam16[:, c:c + 1])],
                      reads=[r_rbuf[q], r_derived], writes=[r_mbuf[q]])
                for c in cs:
                    q = qx[c]
                    E(DVE, lambda h, q=q: [h.tensor_scalar(out=mbuf[q][:], in0=mbuf[q][:], scalar1=1.0, scalar2=-1.0, op0=ALU.min, op1=ALU.mult)],
                      reads=[r_mbuf[q]], writes=[r_mbuf[q]])
                    E(DVE, lambda h, q=q: [h.tensor_tensor(out=btb[q][:], in0=btb[q][:], in1=xr[q][:], op=ALU.mult)],
                      reads=[r_btb[q], r_xr[q]], writes=[r_btb[q]])
                for c in cs:
                    q = qx[c]
                    E(ACT, lambda h, q=q: [h.activation(out=mbuf[q][:], in_=mbuf[q][:], func=AF.Sqrt, bias=1.0, scale=1.0)],
                      reads=[r_mbuf[q]], writes=[r_mbuf[q]])
                for c in cs:
                    q = qx[c]
                    E(DVE, lambda h, q=q: [h.tensor_tensor(out=btb[q][:], in0=btb[q][:], in1=mbuf[q][:], op=ALU.mult)],
                      reads=[r_btb[q], r_mbuf[q]], writes=[r_btb[q]])
                    E(DVE, lambda h, c=c, q=q: [h.tensor_tensor_scan(out=hs[c][:], data0=abuf[q][:], data1=btb[q][:],
                                                                     initial=hstate[:, c:c + 1], op0=ALU.mult, op1=ALU.add)],
                      reads=[r_abuf[q], r_btb[q], r_hstate[c]], writes=[r_hs[c]])
                    E(DVE, lambda h, c=c: [h.tensor_copy(out=hstate[:, c:c + 1], in_=hs[c][:, T - 1:T])],
                      reads=[r_hs[c]], writes=[r_hstate[c]])

            def rnn_b(c):
                E(DVE, lambda h, c=c: [h.tensor_tensor(out=hb[:, 4 + c, :], in0=hs[c][:], in1=gy[:, c, :], op=ALU.mult)],
                  reads=[r_hs[c], r_gy[c]], writes=[r_hb[4 + c]])

            def conv_chunk(c):
                si = next_slab(*d_conv(c))
                dflat = slots[si][:].rearrange("p k n -> p (k n)")
                b = next_bank()
                mm_group(b, [(dflat[:, k * 128:(k + 1) * 128], ub[:, c, k:k + T]) for k in range(NCONV)], [r_slot[si], r_ub[c]])
                release_slab()
                E(ACT, lambda h: [h.activation(out=cv[:, c, :], in_=ps[:, b, :], func=AF.Identity,
                                               bias=vecs[:, V_CONVB + c:V_CONVB + c + 1], scale=1.0)],
                  reads=[r_bank[b], r_vecs], writes=[r_cv[c]])
                E(DVE, lambda h: [h.tensor_copy(out=ub[:, c, 0:30], in_=ub[:, c, T:T + 30])], reads=[r_ub[c]], writes=[r_ub[c]])

            yield 'M'
            rnn_pair_a(0)
            yield 'M'
            rnn_pair_a(2)
            for c in range(4):
                yield 'M'
                conv_chunk(c)
            yield 'M'
            si = next_slab(*d_cols("win", bin_, 1536))
            for c in range(4):
                b = next_bank()
                mm_group(b, [(slots[si][:, kc, c * 128:(c + 1) * 128], hb[:, kc, :]) for kc in range(8)], [r_slot[si]] + r_hb)
                E(ACT, lambda h, c=c, b=b: [h.activation(out=gy[:, c, :], in_=ps[:, b, :], func=AF.Gelu_apprx_tanh)],
                  reads=[r_bank[b]], writes=[r_gy[c]])
            release_slab()
            if hook is not None:
                yield from hook()
            yield 'e'
            bs = next_bank(hold=True)
            bq = next_bank(hold=True)
            for c in range(4):
                q = nxt("cvb", 2)
                E(DVE, lambda h, c=c, q=q: [h.tensor_copy(out=cvb[q][:], in_=cv[:, c, :])], reads=[r_cv[c]], writes=[r_cvb[q]])
                E(ACT, lambda h, c=c, q=q: [h.activation(out=cvsq[q][:], in_=cv[:, c, :], func=AF.Square)], reads=[r_cv[c]], writes=[r_cvsq[q]])
                mm_group(bs, [(ones_bf[:], cvb[q][:])], [r_cvb[q], r_const], first=(c == 0), last=(c == 3))
                mm_group(bq, [(ones_bf[:], cvsq[q][:])], [r_cvsq[q], r_const], first=(c == 0), last=(c == 3))
            E(ACT, lambda h: [h.activation(out=meanb[:], in_=ps[:, bs, :], func=AF.Copy, scale=1.0 / 512)],
              reads=[r_bank[bs]], writes=[r_meanb])
            E(ACT, lambda h: [h.activation(out=varb[:], in_=ps[:, bs, :], func=AF.Square, scale=1.0 / 512)],
              reads=[r_bank[bs]], writes=[r_varb])
            E(DVE, lambda h: [h.scalar_tensor_tensor(out=varb[:], in0=ps[:, bq, :], scalar=1.0 / 512, in1=varb[:],
                                                     op0=ALU.mult, op1=ALU.subtract)],
              reads=[r_bank[bq], r_varb], writes=[r_varb])
            E(ACT, lambda h: [h.activation(out=varb[:], in_=varb[:], func=AF.Sqrt, bias=EPS, scale=1.0)],
              reads=[r_varb], writes=[r_varb])
            E(DVE, lambda h: [h.reciprocal(out=varb[:], in_=varb[:])], reads=[r_varb], writes=[r_varb])
            held.discard(bs); held.discard(bq)
            yield 'e'
            for c in range(4):
                rnn_b(c)
            for c in range(4):
                if c % 2 == 0:
                    yield 'e'
                q = nxt("ntmp", 2)
                E(DVE, lambda h, c=c, q=q: [h.tensor_tensor(out=ntmp[q][:], in0=cv[:, c, :], in1=meanb[:], op=ALU.subtract)],
                  reads=[r_cv[c], r_meanb], writes=[r_ntmp[q]])
                E(DVE, lambda h, q=q: [h.tensor_tensor(out=ntmp[q][:], in0=ntmp[q][:], in1=varb[:], op=ALU.mult)],
                  reads=[r_ntmp[q], r_varb], writes=[r_ntmp[q]])
                E(ACT, lambda h, c=c, q=q: [h.activation(out=hb[:, c, :], in_=ntmp[q][:], func=AF.Silu,
                                                       bias=vecs[:, V_LNB + c:V_LNB + c + 1], scale=vecs[:, V_LNG + c:V_LNG + c + 1])],
                  reads=[r_ntmp[q], r_vecs], writes=[r_hb[c]])
            for ch in range(2):
                yield 'M'
                si = next_slab(*d_cols("wout", bout, ch * 512))
                for oc in range(4):
                    fc = 4 * ch + oc
                    b = next_bank()
                    mm_group(b, [(slots[si][:, kc, oc * 128:(oc + 1) * 128], hb[:, kc, :]) for kc in range(8)], [r_slot[si]] + r_hb)
                    E(DVE, lambda h, fc=fc, b=b: [h.scalar_tensor_tensor(
                        out=xT[:, fc, :], in0=ps[:, b, :], scalar=gt2[:, fc:fc + 1], in1=xT[:, fc, :], op0=ALU.mult, op1=ALU.add)],
                        reads=[r_bank[b], r_modv], writes=[r_xT[fc]])
                release_slab()
            mix_done[0] += 1

        def tile_gen(i):
            s = i % 2
            xT = xTs[s]; r_xT = r_xTs[s]; rstd = rstds[s]; r_rstd = r_rstds[s]
            first = (i == 0)
            for fc in range(8):
                if fc % 2 == 0:
                    yield 'e'
                b = next_bank()

                def fn(h, fc=fc, b=b):
                    return [h.transpose(out=ps[:, b, blk * 128:(blk + 1) * 128], in_=xin[:, blk, fc * 128:(fc + 1) * 128],
                                        identity=ident[:]) for blk in range(4)]
                E(PE, fn, reads=[r_xin, r_const], writes=[r_bank[b]])
                E(DVE, lambda h, fc=fc, b=b: [h.tensor_copy(out=xT[:, fc, :], in_=ps[:, b, :])],
                  reads=[r_bank[b]], writes=[r_xT[fc]])
            if i + 1 < n_tiles and not st["dry"]:
                load_x(i + 1)
            if first:
                yield from mod_part(0, 8)
                gs_op(gs1, 8, V_G1)
            yield 'S'
            yield from compute_xsq_and_stats(s)
            yield from norm_apply(s, gs1, sh1)

            def hook1():
                yield from mod_part(8, 12)
                half_op(gth1, 16)
            yield from ffn(s, "w1i", b1i, "w1o", b1o, gth1, hook=hook1 if first else None)
            if first:
                yield from mod_part(12, 20)
                gs_op(gs2, 32, V_GM)
            yield from compute_xsq_and_stats(s)
            yield from norm_apply(s, gs2, sh2)
            yield from mixer(s, hook=(lambda: mod_part(20, 24)) if first else None)
            if first:
                yield from mod_part(24, 32)
                gs_op(gs3, 56, V_G2)
            yield from compute_xsq_and_stats(s)
            yield from norm_apply(s, gs3, sh3)

            def hook2():
                yield from mod_part(32, 36)
                half_op(gth3, 64)
            yield from ffn(s, "w2i", b2i, "w2o", b2o, gth3, hook=hook2 if first else None)
            if first:
                yield from mod_part(36, 44)
                gs_op(gsf, 80, V_GF)
            yield from compute_xsq_and_stats(s)
            for fc in range(8):
                yield 'e'
                q = nxt("ntmp", 2)
                qo = nxt("oT", 2)
                E(DVE, lambda h, q=q, fc=fc: [h.scalar_tensor_tensor(
                    out=ntmp[q][:], in0=xT[:, fc, :], scalar=gsf[:, fc:fc + 1], in1=rstd[:], op0=ALU.mult, op1=ALU.mult)],
                    reads=[r_xT[fc], r_rstd, r_derived], writes=[r_ntmp[q]])
                E(ACT, lambda h, q=q, qo=qo, fc=fc: [h.activation(out=oT[qo][:], in_=ntmp[q][:], func=AF.Identity,
                                                                 bias=shf[:, fc:fc + 1], scale=1.0)],
                  reads=[r_ntmp[q], r_modv], writes=[r_oT[qo]])
                b = next_bank()

                def fn(h, qo=qo, b=b):
                    return [h.transpose(out=ps[:, b, blk * 128:(blk + 1) * 128], in_=oT[qo][:, blk * 128:(blk + 1) * 128],
                                        identity=ident[:]) for blk in range(4)]
                E(PE, fn, reads=[r_oT[qo], r_const], writes=[r_bank[b]])
                k = nxt("oo", 3)
                if fc % 2 == 0:
                    E(DVE, lambda h, k=k, b=b: [h.tensor_copy(out=oo[k][:], in_=ps[:, b, :].rearrange("p (k n) -> p k n", k=4))],
                      reads=[r_bank[b]], writes=[r_oo[k]])
                else:
                    E(ACT, lambda h, k=k, b=b: [h.activation(out=oo[k][:], in_=ps[:, b, :].rearrange("p (k n) -> p k n", k=4), func=AF.Copy)],
                      reads=[r_bank[b]], writes=[r_oo[k]])
                if not st["dry"]:
                    dst = y_d[i * T:(i + 1) * T, fc * 128:(fc + 1) * 128].rearrange("(b p) d -> p b d", p=128)
                    dma1(PL, dst, oo[k][:], [r_oo[k]], [], sem_oo[k])

        def program():
            gens = []
            nxt_tile = [0]

            def step(e):
                e[1] = next(e[0], None)
                while e[1] == 'S':
                    e[2] = True
                    e[1] = next(e[0], None)

            def maybe_spawn():
                while len(gens) < 2 and nxt_tile[0] < n_tiles and (not gens or gens[-1][2]):
                    e = [tile_gen(nxt_tile[0]), None, False, nxt_tile[0]]
                    nxt_tile[0] += 1
                    step(e)
                    gens.append(e)

            def can_own(e):
                return e[1] == 'M' or (e[1] == 'MIX' and mix_done[0] == e[3])

            mix_done[0] = 0
            owner = None
            maybe_spawn()
            while gens:
                if owner is not None and owner[1] in ('M', 'MIX') and owner in gens:
                    step(owner)
                    for o in gens:
                        if o is not owner:
                            for _ in range(2):
                                if o[1] == 'e':
                                    step(o)
                else:
                    owner = None
                    for e in gens:
                        if can_own(e):
                            owner = e
                            break
                    if owner is None:
                        progressed = False
                        for e in gens:
                            if e[1] == 'e':
                                step(e)
                                progressed = True
                        assert progressed, [(e[1], e[3]) for e in gens]
                gens[:] = [e for e in gens if e[1] is not None]
                maybe_spawn()

        program()
        st["dry"] = False
        bank_ctr[0] = 0
        rot.clear()
        held.clear()
        for _ in range(NS):
            issue_slab()
        program()

        final_vals = [(sm.h, sm.val) for sm in sem_oo]
        assert st["used"] == len(plan), (st, len(plan))

        with nc.Block() as block:
            @block.sync
            def _(h):
                replay(SP, h)

            @block.gpsimd
            def _(h):
                replay(PL, h)
                for smh, v in final_vals:
                    if v:
                        h.wait_ge(smh, v)

            @block.tensor
            def _(h):
                replay(PE, h)

            @block.scalar
            def _(h):
                replay(ACT, h)

            @block.vector
            def _(h):
                replay(DVE, h)
    return nc


def prep_inputs(inputs, n_tiles=SEQ // T, cores=N_CORES):
    f = lambda a: np.ascontiguousarray(np.asarray(a, dtype=np.float32))
    col = lambda v, n: f(v).reshape(n, 128).T
    x = f(inputs["x"]); c = f(inputs["c"])
    shared = np.zeros((128, NV), np.float32)
    shared[:, V_G1:V_G1 + 8] = col(inputs["g_ffn1"][0], 8)
    shared[:, V_GM:V_GM + 8] = col(inputs["g_mix"][0], 8)
    shared[:, V_G2:V_G2 + 8] = col(inputs["g_ffn2"][0], 8)
    shared[:, V_GF:V_GF + 8] = col(inputs["g_final"], 8)
    shared[:, V_BMOD:V_BMOD + 72] = col(inputs["b_mod"][0], 72)
    shared[:, V_BMOD + 72:V_BMOD + 88] = col(inputs["b_fmod"], 16)
    shared[:, V_CONVB:V_CONVB + 4] = col(inputs["conv_b"][0], 4)
    shared[:, V_LNG:V_LNG + 4] = col(inputs["ln_g"][0], 4)
    shared[:, V_LNB:V_LNB + 4] = col(inputs["ln_b"][0], 4)
    shared[:, V_RCB:V_RCB + 4] = col(inputs["rnn_conv_b"][0], 4)
    shared[:, V_BA:V_BA + 4] = col(inputs["b_a"][0], 4)
    shared[:, V_BI:V_BI + 4] = col(inputs["b_i"][0], 4)
    shared[:, V_LAM:V_LAM + 4] = col(inputs["lru_lambda"][0], 4)
    rcw = f(inputs["rnn_conv_w"][0])
    for cc in range(4):
        for k in range(4):
            shared[:, V_RCW + 4 * cc + k] = rcw[k, cc * 128:(cc + 1) * 128]
    cw = f(inputs["conv_w"][0])
    dcw = np.zeros((4, 128, NCONV, 128), np.float32)
    ar = np.arange(128)
    for cc in range(4):
        for k in range(NCONV):
            dcw[cc, ar, k, ar] = cw[k, cc * 128:(cc + 1) * 128]
    dcw = dcw.reshape(512, NCONV * 128)
    def bd(w):
        w = f(w)
        o = np.zeros((128, 4, 128), np.float32)
        for cc in range(4):
            o[0:64, cc, 0:64] = w[2 * cc]
            o[64:128, cc, 64:128] = w[2 * cc + 1]
        return o.reshape(128, 512)
    wabd = bd(inputs["w_a"][0]); wibd = bd(inputs["w_i"][0])
    common = {
        "ident": np.eye(128, dtype=np.float32), "wabd": wabd, "wibd": wibd,
        "w_mod": f(inputs["w_mod"][0]), "w_fmod": f(inputs["w_fmod"]),
        "w1i": f(inputs["w_ffn1_in"][0]), "w1o": f(inputs["w_ffn1_out"][0]),
        "win": f(inputs["w_in"][0]), "wout": f(inputs["w_out"][0]),
        "w2i": f(inputs["w_ffn2_in"][0]), "w2o": f(inputs["w_ffn2_out"][0]),
        "dcw": dcw,
    }
    in_maps = []
    S = n_tiles * T
    for b in range(cores):
        v = shared.copy()
        v[:, V_C:V_C + 8] = c[b].reshape(8, 128).T
        m = dict(common)
        m["vecs"] = v
        m["x"] = np.ascontiguousarray(x[b, :S, :])
        in_maps.append(m)
    return in_maps


_NC_CACHE = {}


def kernel(**inputs):
    n_tiles = SEQ // T
    if n_tiles not in _NC_CACHE:
        _NC_CACHE[n_tiles] = build_program(n_tiles)
    nc = _NC_CACHE[n_tiles]
    in_maps = prep_inputs(inputs, n_tiles, N_CORES)
    res = run_bass_kernel_spmd(nc, in_maps, core_ids=list(range(N_CORES)))
    out = np.stack([np.asarray(r["y"], dtype=np.float32) for r in res.results], axis=0)
    return out
```

```python
import numpy as np
import concourse.bass as bass
import concourse.mybir as mybir
from concourse.bass_utils import run_bass_kernel_spmd

F32 = mybir.dt.float32
BF16 = mybir.dt.bfloat16
AF = mybir.ActivationFunctionType
ALU = mybir.AluOpType

D = 1024
DFF = 2816
SEQ = 4096
T = 512
KC = 8
FC = 22
NCONV = 31
EPS = 1e-6
NS = 3
N_CORES = 8

V_G1, V_GM, V_G2, V_GF = 0, 8, 16, 24
V_BMOD = 32
V_CONVB = 120
V_LNG = 124
V_LNB = 128
V_RCB = 132
V_BA = 136
V_BI = 140
V_LAM = 144
V_RCW = 148
V_C = 164
NV = 172


class Sem:
    def __init__(self, h):
        self.h = h
        self.val = 0


class Res:
    __slots__ = ("w", "r")

    def __init__(self):
        self.w = None
        self.r = {}


class Eng:
    def __init__(self, name, sem, self_sync):
        self.name = name
        self.sem = sem
        self.prog = []
        self.seen = {}
        self.self_sync = self_sync


def emit(eng, fn, reads=(), writes=(), dma=None, ndma=1):
    waits = {}

    def need(tok):
        if tok is None:
            return
        s, v = tok
        if waits.get(s, 0) < v:
            waits[s] = v

    for r in reads:
        need(r.w)
    for w in writes:
        need(w.w)
        for s, v in w.r.items():
            need((s, v))
    wl = []
    for s, v in waits.items():
        if s is eng.sem and not eng.self_sync:
            continue
        if eng.seen.get(s, 0) >= v:
            continue
        eng.seen[s] = v
        wl.append((s, v))
    if dma is None:
        eng.sem.val += 1
        tok = (eng.sem, eng.sem.val)
        eng.prog.append((wl, fn, eng.sem, 1, False))
    else:
        dma.val += 16 * ndma
        tok = (dma, dma.val)
        eng.prog.append((wl, fn, dma, 16, True))
    for r in reads:
        if r.r.get(tok[0], 0) < tok[1]:
            r.r[tok[0]] = tok[1]
    for w in writes:
        w.w = tok
        w.r = {}
    return tok


def replay(eng, h):
    for wl, fn, sem, inc, is_dma in eng.prog:
        for s, v in wl:
            h.wait_ge(s.h, v)
        ins = fn(h)
        if is_dma:
            for i in ins:
                i.then_inc(sem.h, 16)
        else:
            ins[-1].then_inc(sem.h, 1)


def build_program(n_tiles=SEQ // T):
    S = n_tiles * T
    nc = bass.Bass("TRN2", target_bir_lowering=False)
    dt_in = lambda name, shape: nc.dram_tensor(name, shape, F32, kind="ExternalInput").ap()
    x_d = dt_in("x", [S, D])
    vecs_d = dt_in("vecs", [128, NV])
    ident_d = dt_in("ident", [128, 128])
    wabd_d = dt_in("wabd", [128, 512])
    wibd_d = dt_in("wibd", [128, 512])
    wmod_d = dt_in("w_mod", [D, 9 * D])
    wfmod_d = dt_in("w_fmod", [D, 2 * D])
    w1i_d = dt_in("w1i", [D, 2 * DFF])
    w1o_d = dt_in("w1o", [DFF, D])
    win_d = dt_in("win", [D, 2 * D])
    wout_d = dt_in("wout", [D, D])
    w2i_d = dt_in("w2i", [D, 2 * DFF])
    w2o_d = dt_in("w2o", [DFF, D])
    dcw_d = dt_in("dcw", [512, NCONV * 128])
    y_d = nc.dram_tensor("y", [S, D], F32, kind="ExternalOutput").ap()
    dt_sc = lambda name, shape: nc.dram_tensor(name, shape, BF16, kind="Internal").ap()
    b1i = dt_sc("b1i", [D, 2 * DFF])
    b1o = dt_sc("b1o", [DFF, D])
    bin_ = dt_sc("bin", [D, 2 * D])
    bout = dt_sc("bout", [D, D])
    b2i = dt_sc("b2i", [D, 2 * DFF])
    b2o = dt_sc("b2o", [DFF, D])
    bdc = dt_sc("bdc", [512, NCONV * 128])

    from contextlib import ExitStack
    es = ExitStack()
    with es:
        sb = lambda name, shape, dt=F32: es.enter_context(nc.sbuf_tensor("sb_" + name, shape, dt))
        newsem = lambda name: Sem(es.enter_context(nc.semaphore(name)))

        PE = Eng("pe", newsem("s_pe"), False)
        ACT = Eng("act", newsem("s_act"), True)
        DVE = Eng("dve", newsem("s_dve"), True)
        SP = Eng("sp", newsem("s_sp"), False)
        PL = Eng("pool", newsem("s_pool"), False)

        ident = sb("ident", [128, 128])
        ones_bf = sb("ones_bf", [128, 128], BF16)
        vecs = sb("vecs", [128, NV])
        modv = sb("modv", [128, 88])
        cact = sb("cact", [128, 8])
        gs1 = sb("gs1", [128, 8]); gth1 = sb("gth1", [128, 8])
        gs2 = sb("gs2", [128, 8])
        gs3 = sb("gs3", [128, 8]); gth3 = sb("gth3", [128, 8])
        gsf = sb("gsf", [128, 8])
        lam8 = sb("lam8", [128, 4]); lam16 = sb("lam16", [128, 4]); lamt = sb("lamt", [128, 4])
        wabd = sb("wabd", [128, 4, 128], BF16)
        wibd = sb("wibd", [128, 4, 128], BF16)
        hstate = sb("hstate", [128, 4])
        slots = [sb(f"slab{i}", [128, 8, 512], BF16) for i in range(NS)]
        xin = sb("xin", [128, 4, D])
        xTs = [sb(f"xT{k}", [128, 8, T]) for k in range(2)]
        xsq = [sb(f"xsq{i}", [128, T], BF16) for i in range(3)]
        hbs = [sb(f"hb{k}", [128, 8, T], BF16) for k in range(2)]
        hid = sb("hid", [128, FC, T], BF16)
        sgb = [sb(f"sgb{i}", [128, T]) for i in range(2)]
        ntmp = [sb(f"ntmp{i}", [128, T]) for i in range(2)]
        rstds = [sb(f"rstd{k}", [128, T]) for k in range(2)]
        ub = sb("ub", [128, 4, T + 30], BF16)
        ux = sb("ux", [128, 4, T + 3])
        gy = sb("gy", [128, 4, T], BF16)
        cv = sb("cv", [128, 4, T])
        cvsq = [sb(f"cvsq{i}", [128, T], BF16) for i in range(2)]
        cvb = [sb(f"cvb{i}", [128, T], BF16) for i in range(2)]
        meanb = sb("meanb", [128, T])
        varb = sb("varb", [128, T])
        xr = [sb(f"xr{i}", [128, T]) for i in range(2)]
        xrb = [sb(f"xrb{i}", [128, T], BF16) for i in range(2)]
        rbuf = [sb(f"rbuf{i}", [128, T]) for i in range(2)]
        abuf = [sb(f"abuf{i}", [128, T]) for i in range(2)]
        mbuf = [sb(f"mbuf{i}", [128, T]) for i in range(2)]
        btb = [sb(f"btb{i}", [128, T]) for i in range(2)]
        hs = [sb(f"hs{i}", [128, T]) for i in range(4)]
        oT = [sb(f"oT{i}", [128, T]) for i in range(2)]
        oo = [sb(f"oo{k}", [128, 4, 128]) for k in range(3)]
        ps = es.enter_context(nc.psum_tensor("ps", [128, 8, T], F32))

        R = lambda: Res()
        r_const = R()
        r_vecs = R(); r_modv = R(); r_cact = R(); r_derived = R(); r_hstate = [R() for _ in range(4)]
        r_gates = R()
        r_slot = [R() for _ in range(NS)]
        sem_slot = [newsem(f"s_slot{i}") for i in range(NS)]
        r_bank = [R() for _ in range(8)]
        r_xin = R(); sem_xin = newsem("s_xin")
        r_xTs = [[R() for _ in range(8)] for _ in range(2)]
        r_xsq = [R() for _ in range(3)]
        r_hbs = [[R() for _ in range(8)] for _ in range(2)]
        r_hid = [R() for _ in range(FC)]
        r_sgb = [R() for _ in range(2)]
        r_ntmp = [R() for _ in range(2)]
        r_rstds = [R(), R()]
        r_ub = [R() for _ in range(4)]
        r_ux = [R() for _ in range(4)]
        r_gy = [R() for _ in range(4)]
        r_cv = [R() for _ in range(4)]
        r_cvsq = [R() for _ in range(2)]
        r_cvb = [R() for _ in range(2)]
        r_meanb = R(); r_varb = R()
        r_xr = [R() for _ in range(2)]; r_xrb = [R() for _ in range(2)]
        r_rbuf = [R() for _ in range(2)]; r_abuf = [R() for _ in range(2)]
        r_mbuf = [R() for _ in range(2)]; r_btb = [R() for _ in range(2)]
        r_hs = [R() for _ in range(4)]
        r_oT = [R() for _ in range(2)]
        r_oo = [R() for _ in range(3)]; sem_oo = [newsem(f"s_oo{k}") for k in range(3)]
        sem_const = newsem("s_const")
        r_scr = {}
        sem_wb = [newsem(f"s_wb{i}") for i in range(NS)]
        sem_slot_pl = [newsem(f"s_slotp{i}") for i in range(NS)]

        bank_ctr = [0]

        held = set()

        def next_bank(hold=False):
            while True:
                b = bank_ctr[0] % 8
                bank_ctr[0] += 1
                if b not in held:
                    break
            if hold:
                held.add(b)
            return b

        rot = {}

        def nxt(name, n):
            v = rot.get(name, 0)
            rot[name] = v + 1
            return v % n

        def dma1(eng, out_ap, in_ap, reads, writes, sem):
            return emit(eng, lambda h: [h.dma_start(out=out_ap, in_=in_ap)], reads=reads, writes=writes, dma=sem)

        dma1(SP, vecs[:], vecs_d[:, :], [], [r_vecs], sem_const)
        dma1(SP, ident[:], ident_d[:, :], [], [r_const], newsem("s_ident"))
        def load_x(i):
            src = x_d[i * T:(i + 1) * T, :].rearrange("(b p) d -> p b d", p=128)
            dma1(PL, xin[:], src, [], [r_xin], sem_xin)
        load_x(0)
        sem_g = newsem("s_gates")
        dma1(PL, wabd[:], wabd_d[:, :].rearrange("p (c n) -> p c n", c=4), [], [r_gates], sem_g)
        dma1(PL, wibd[:], wibd_d[:, :].rearrange("p (c n) -> p c n", c=4), [], [r_gates], sem_g)

        emit(DVE, lambda h: [h.memset(ones_bf[:], 1.0)], writes=[r_const])
        emit(DVE, lambda h: [h.memset(hstate[:], 0.0)], writes=r_hstate)
        emit(DVE, lambda h: [h.memset(ub[:, :, 0:30], 0.0)], writes=r_ub)
        emit(DVE, lambda h: [h.memset(ux[:, :, 0:3], 0.0)], writes=r_ux)
        emit(ACT, lambda h: [h.activation(out=cact[:], in_=vecs[:, V_C:V_C + 8], func=AF.Silu)],
             reads=[r_vecs], writes=[r_cact])
        emit(ACT, lambda h: [h.activation(out=lamt[:], in_=vecs[:, V_LAM:V_LAM + 4], func=AF.Exp, scale=-1.0)],
             reads=[r_vecs], writes=[r_derived])
        emit(ACT, lambda h: [h.activation(out=lamt[:], in_=lamt[:], func=AF.Ln, bias=1.0)],
             reads=[r_derived], writes=[r_derived])
        emit(DVE, lambda h: [h.tensor_scalar(out=lam8[:], in0=lamt[:], scalar1=-8.0, scalar2=None, op0=ALU.mult)],
             reads=[r_derived], writes=[r_derived])
        emit(DVE, lambda h: [h.tensor_scalar(out=lam16[:], in0=lamt[:], scalar1=-16.0, scalar2=None, op0=ALU.mult)],
             reads=[r_derived], writes=[r_derived])

        kview = lambda ap: ap.rearrange("(kc p) n -> p kc n", p=128)
        plan = []
        st = {"issued": 0, "used": 0, "dry": True}

        cur_tile = [0]
        pos_ctr = {}

        def issue_slab():
            k = st["issued"]
            if k >= len(plan):
                return
            st["issued"] += 1
            si = k % NS
            cname, pfn, tile, pos = plan[k]

            def mk(pieces):
                return lambda h: [h.dma_start(out=o, in_=i) for o, i in pieces]
            if cname is None:
                pieces = pfn(si, 0)
                emit(SP, mk(pieces), writes=[r_slot[si]], dma=sem_slot[si], ndma=len(pieces))
            elif tile == 0:
                pieces = pfn(si, 1)
                emit(PL, mk(pieces), writes=[r_slot[si]], dma=sem_slot_pl[si], ndma=len(pieces))
                back = [(d, o) for o, d in pfn(si, 0)]
                r_scr[pos] = Res()
                emit(SP, mk(back), reads=[r_slot[si]], writes=[r_scr[pos]], dma=sem_wb[si], ndma=len(back))
            else:
                pieces = pfn(si, 0)
                emit(SP, mk(pieces), reads=[r_scr[pos]], writes=[r_slot[si]], dma=sem_slot[si], ndma=len(pieces))

        def next_slab(cname, pfn):
            if st["dry"]:
                t = cur_tile[0]
                pos = None
                if cname is not None:
                    pos = pos_ctr.get(t, 0)
                    pos_ctr[t] = pos + 1
                plan.append((cname, pfn, t, pos))
                return 0
            k = st["used"]
            st["used"] += 1
            assert k < st["issued"] and plan[k][0] == cname
            return k % NS

        def release_slab():
            if not st["dry"]:
                issue_slab()

        def E(eng, fn, reads=(), writes=()):
            if st["dry"]:
                return None
            return emit(eng, fn, reads=reads, writes=writes)

        def d_mod(g):
            src = wmod_d if g < 36 else wfmod_d
            gg = g if g < 36 else g - 36
            return (None, lambda si, f: [(slots[si][:].bitcast(F32), kview(src[:, gg * 256:(gg + 1) * 256]))])

        WT = {"w1i": (b1i, w1i_d), "w1o": (b1o, w1o_d), "win": (bin_, win_d), "wout": (bout, wout_d),
              "w2i": (b2i, w2i_d), "w2o": (b2o, w2o_d), "dcw": (bdc, dcw_d)}

        def d_ffn_in(nm, wi_unused, j):
            return (nm, lambda si, f: [
                (slots[si][:, :, 0:256], kview(WT[nm][f][:, 256 * j:256 * j + 256])),
                (slots[si][:, :, 256:512], kview(WT[nm][f][:, DFF + 256 * j:DFF + 256 * j + 256]))])

        def d_ffn_out(nm, wo_unused, ch, k0, nk):
            return (nm, lambda si, f: [(slots[si][:, 0:nk, :], kview(WT[nm][f][k0 * 128:(k0 + nk) * 128, ch * 512:(ch + 1) * 512]))])

        def d_cols(nm, w_unused, c0):
            return (nm, lambda si, f: [(slots[si][:, :, :], kview(WT[nm][f][:, c0:c0 + 512]))])

        def d_glu(half):
            return ("win", lambda si, f: [
                (slots[si][:, :, 0:256], kview(WT["win"][f][:, 256 * half:256 * half + 256])),
                (slots[si][:, :, 256:512], kview(WT["win"][f][:, 512 + 256 * half:512 + 256 * half + 256]))])

        def d_conv(c):
            return ("dcw", lambda si, f: [
                (slots[si][:].rearrange("p k n -> p (k n)")[:, 0:NCONV * 128], WT["dcw"][f][c * 128:(c + 1) * 128, :])])

        def mm_group(bank, pairs, reads, first=True, last=True):
            n = len(pairs)

            def fn(h):
                out = []
                for idx, (l, r) in enumerate(pairs):
                    out.append(h.matmul(ps[:, bank, :], lhsT=l, rhs=r,
                                        start=(first and idx == 0), stop=(last and idx == n - 1)))
                return out
            return E(PE, fn, reads=reads, writes=[r_bank[bank]])

        sh1 = modv[:, 0:8]; sh2 = modv[:, 24:32]; gt2 = modv[:, 40:48]; sh3 = modv[:, 48:56]; shf = modv[:, 72:80]

        def mod_part(g0, g1):
            bank = next_bank(hold=True)
            for g in range(g0, g1):
                si = next_slab(*d_mod(g))
                sf = slots[si][:].bitcast(F32)

                def fn(h, g=g, sf=sf):
                    out = []
                    for sub in range(2):
                        j = 2 * g + sub
                        for kc in range(8):
                            out.append(h.matmul(ps[:, bank, j:j + 1], lhsT=sf[:, kc, sub * 128:(sub + 1) * 128],
                                                rhs=cact[:, kc:kc + 1], start=(kc == 0), stop=(kc == 7)))
                    return out
                E(PE, fn, reads=[r_slot[si], r_cact], writes=[r_bank[bank]])
                release_slab()
                yield 'M'
            c0, c1 = 2 * g0, 2 * g1
            E(DVE, lambda h: [h.tensor_tensor(out=modv[:, c0:c1], in0=ps[:, bank, c0:c1], in1=vecs[:, V_BMOD + c0:V_BMOD + c1], op=ALU.add)],
              reads=[r_bank[bank], r_vecs], writes=[r_modv])
            held.discard(bank)

        def gs_op(dst, sc_col, g_col):
            E(DVE, lambda h: [h.scalar_tensor_tensor(out=dst[:], in0=modv[:, sc_col:sc_col + 8], scalar=1.0,
                                                     in1=vecs[:, g_col:g_col + 8], op0=ALU.add, op1=ALU.mult)],
              reads=[r_modv, r_vecs], writes=[r_derived])

        def half_op(dst, col):
            E(DVE, lambda h: [h.tensor_scalar(out=dst[:], in0=modv[:, col:col + 8], scalar1=0.5, scalar2=None, op0=ALU.mult)],
              reads=[r_modv], writes=[r_derived])

        def rsqrt_chain(buf, res, bank, scale):
            E(ACT, lambda h: [h.activation(out=buf[:], in_=ps[:, bank, :], func=AF.Sqrt, bias=EPS, scale=scale)],
              reads=[r_bank[bank]], writes=[res])
            E(DVE, lambda h: [h.reciprocal(out=buf[:], in_=buf[:])], reads=[res], writes=[res])

        def compute_xsq_and_stats(s):
            xT = xTs[s]; r_xT = r_xTs[s]; rstd = rstds[s]; r_rstd = r_rstds[s]
            yield 'e'
            sbank = next_bank(hold=True)
            for fc in range(8):
                q = nxt("xsq", 3)
                E(ACT, lambda h, q=q, fc=fc: [h.activation(out=xsq[q][:], in_=xT[:, fc, :], func=AF.Square)],
                  reads=[r_xT[fc]], writes=[r_xsq[q]])
                mm_group(sbank, [(ones_bf[:], xsq[q][:])], [r_xsq[q], r_const], first=(fc == 0), last=(fc == 7))
                if fc % 2 == 1:
                    yield 'e'
            rsqrt_chain(rstd, r_rstd, sbank, 1.0 / D)
            held.discard(sbank)

        def norm_apply(s, gs, sh):
            xT = xTs[s]; r_xT = r_xTs[s]; rstd = rstds[s]; r_rstd = r_rstds[s]; hb = hbs[s]; r_hb = r_hbs[s]
            for fc in range(8):
                if fc % 2 == 0:
                    yield 'e'
                q = nxt("ntmp", 2)
                E(DVE, lambda h, q=q, fc=fc: [h.scalar_tensor_tensor(
                    out=ntmp[q][:], in0=xT[:, fc, :], scalar=gs[:, fc:fc + 1], in1=rstd[:], op0=ALU.mult, op1=ALU.mult)],
                    reads=[r_xT[fc], r_rstd, r_derived], writes=[r_ntmp[q]])
                E(ACT, lambda h, q=q, fc=fc: [h.activation(out=hb[:, fc, :], in_=ntmp[q][:], func=AF.Identity,
                                                           bias=sh[:, fc:fc + 1], scale=1.0)],
                  reads=[r_ntmp[q], r_modv], writes=[r_hb[fc]])

        def ffn(s, nm_i, wi, nm_o, wo, gth, hook=None):
            xT = xTs[s]; r_xT = r_xTs[s]; hb = hbs[s]; r_hb = r_hbs[s]
            for j in range(11):
                yield 'M'
                si = next_slab(*d_ffn_in(nm_i, wi, j))
                for sub in range(2):
                    cg = 2 * j + sub
                    bg = next_bank()
                    mm_group(bg, [(slots[si][:, kc, sub * 128:(sub + 1) * 128], hb[:, kc, :]) for kc in range(8)],
                             [r_slot[si]] + r_hb)
                    bu = next_bank()
                    mm_group(bu, [(slots[si][:, kc, 256 + sub * 128:256 + (sub + 1) * 128], hb[:, kc, :]) for kc in range(8)],
                             [r_slot[si]] + r_hb)
                    q = nxt("sgb", 2)
                    E(ACT, lambda h, q=q, bg=bg: [h.activation(out=sgb[q][:], in_=ps[:, bg, :], func=AF.Silu)],
                      reads=[r_bank[bg]], writes=[r_sgb[q]])
                    E(DVE, lambda h, q=q, bu=bu, cg=cg: [h.tensor_tensor(out=hid[:, cg, :], in0=ps[:, bu, :], in1=sgb[q][:], op=ALU.mult)],
                      reads=[r_bank[bu], r_sgb[q]], writes=[r_hid[cg]])
                release_slab()
            if hook is not None:
                yield from hook()
            for ch in range(2):
                yield 'M'
                banks = [next_bank(hold=True) for _ in range(4)]
                for (k0, nk) in ((0, 8), (8, 8), (16, 6)):
                    si = next_slab(*d_ffn_out(nm_o, wo, ch, k0, nk))
                    for oc in range(4):
                        mm_group(banks[oc],
                                 [(slots[si][:, kl, oc * 128:(oc + 1) * 128], hid[:, k0 + kl, :]) for kl in range(nk)],
                                 [r_slot[si]] + r_hid[k0:k0 + nk], first=(k0 == 0), last=(k0 == 16))
                    release_slab()
                for oc in range(4):
                    fc = 4 * ch + oc
                    E(DVE, lambda h, fc=fc, b=banks[oc]: [h.scalar_tensor_tensor(
                        out=xT[:, fc, :], in0=ps[:, b, :], scalar=gth[:, fc:fc + 1], in1=xT[:, fc, :], op0=ALU.mult, op1=ALU.add)],
                        reads=[r_bank[banks[oc]], r_derived, r_modv], writes=[r_xT[fc]])
                for b in banks:
                    held.discard(b)

        mix_done = [0]

        def mixer(s, hook=None):
            xT = xTs[s]; r_xT = r_xTs[s]; hb = hbs[s]; r_hb = r_hbs[s]
            yield 'MIX'
            si = next_slab(*d_cols("win", bin_, 1024))
            for c in range(4):
                b = next_bank()
                mm_group(b, [(slots[si][:, kc, c * 128:(c + 1) * 128], hb[:, kc, :]) for kc in range(8)], [r_slot[si]] + r_hb)
                E(ACT, lambda h, c=c, b=b: [h.activation(out=ux[:, c, 3:3 + T], in_=ps[:, b, :], func=AF.Copy)],
                  reads=[r_bank[b]], writes=[r_ux[c]])
            release_slab()
            for half in range(2):
                yield 'M'
                si = next_slab(*d_glu(half))
                for sub in range(2):
                    c = 2 * half + sub
                    bv = next_bank()
                    mm_group(bv, [(slots[si][:, kc, sub * 128:(sub + 1) * 128], hb[:, kc, :]) for kc in range(8)], [r_slot[si]] + r_hb)
                    bg = next_bank()
                    mm_group(bg, [(slots[si][:, kc, 256 + sub * 128:256 + (sub + 1) * 128], hb[:, kc, :]) for kc in range(8)], [r_slot[si]] + r_hb)
                    q = nxt("sgb", 2)
                    E(ACT, lambda h, q=q, bg=bg: [h.activation(out=sgb[q][:], in_=ps[:, bg, :], func=AF.Sigmoid)],
                      reads=[r_bank[bg]], writes=[r_sgb[q]])
                    E(DVE, lambda h, q=q, bv=bv, c=c: [h.tensor_tensor(out=ub[:, c, 30:30 + T], in0=ps[:, bv, :], in1=sgb[q][:], op=ALU.mult)],
                      reads=[r_bank[bv], r_sgb[q]], writes=[r_ub[c]])
                release_slab()

            hs_of = {}

            def rnn_pair_a(c0):
                cs = (c0, c0 + 1)
                qx = {}
                for c in cs:
                    q = nxt("xr", 2)
                    qx[c] = q
                    E(DVE, lambda h, c=c, q=q: [h.tensor_scalar(
                        out=xr[q][:], in0=ux[:, c, 0:T], scalar1=vecs[:, V_RCW + 4 * c:V_RCW + 4 * c + 1],
                        scalar2=vecs[:, V_RCB + c:V_RCB + c + 1], op0=ALU.mult, op1=ALU.add)],
                        reads=[r_ux[c], r_vecs], writes=[r_xr[q]])
                    for k in range(1, 4):
                        E(DVE, lambda h, c=c, q=q, k=k: [h.scalar_tensor_tensor(
                            out=xr[q][:], in0=ux[:, c, k:k + T], scalar=vecs[:, V_RCW + 4 * c + k:V_RCW + 4 * c + k + 1],
                            in1=xr[q][:], op0=ALU.mult, op1=ALU.add)],
                            reads=[r_ux[c], r_vecs, r_xr[q]], writes=[r_xr[q]])
                    E(DVE, lambda h, c=c: [h.tensor_copy(out=ux[:, c, 0:3], in_=ux[:, c, T:T + 3])], reads=[r_ux[c]], writes=[r_ux[c]])
                    E(ACT, lambda h, q=q: [h.activation(out=xrb[q][:], in_=xr[q][:], func=AF.Copy)],
                      reads=[r_xr[q]], writes=[r_xrb[q]])
                br = {}; bi = {}
                for c in cs:
                    q = qx[c]
                    br[c] = next_bank()
                    mm_group(br[c], [(wabd[:, c, :], xrb[q][:])], [r_gates, r_xrb[q]])
                    bi[c] = next_bank()
                    mm_group(bi[c], [(wibd[:, c, :], xrb[q][:])], [r_gates, r_xrb[q]])
                for c in cs:
                    q = qx[c]
                    E(ACT, lambda h, c=c, q=q: [h.activation(out=rbuf[q][:], in_=ps[:, br[c], :], func=AF.Sigmoid,
                                                           bias=vecs[:, V_BA + c:V_BA + c + 1], scale=1.0)],
                      reads=[r_bank[br[c]], r_vecs], writes=[r_rbuf[q]])
                    E(ACT, lambda h, c=c, q=q: [h.activation(out=btb[q][:], in_=ps[:, bi[c], :], func=AF.Sigmoid,
                                                           bias=vecs[:, V_BI + c:V_BI + c + 1], scale=1.0)],
                      reads=[r_bank[bi[c]], r_vecs], writes=[r_btb[q]])
                for c in cs:
                    q = qx[c]
                    E(ACT, lambda h, c=c, q=q: [h.activation(out=abuf[q][:], in_=rbuf[q][:], func=AF.Exp, scale=lam8[:, c:c + 1])],
                      reads=[r_rbuf[q], r_derived], writes=[r_abuf[q]])
                    E(ACT, lambda h, c=c, q=q: [h.activation(out=mbuf[q][:], in_=rbuf[q][:], func=AF.Exp, scale=lam16[:, c:c + 1])],
                      reads=[r_rbuf[q], r_derived], writes=[r_mbuf[q]])
                for c in cs:
                    q = qx[c]
                    E(DVE, lambda h, q=q: [h.tensor_scalar(out=mbuf[q][:], in0=mbuf[q][:], scalar1=1.0, scalar2=-1.0, op0=ALU.min, op1=ALU.mult)],
                      reads=[r_mbuf[q]], writes=[r_mbuf[q]])
                    E(DVE, lambda h, q=q: [h.tensor_tensor(out=btb[q][:], in0=btb[q][:], in1=xr[q][:], op=ALU.mult)],
                      reads=[r_btb[q], r_xr[q]], writes=[r_btb[q]])
                for c in cs:
                    q = qx[c]
                    E(ACT, lambda h, q=q: [h.activation(out=mbuf[q][:], in_=mbuf[q][:], func=AF.Sqrt, bias=1.0, scale=1.0)],
                      reads=[r_mbuf[q]], writes=[r_mbuf[q]])
                for c in cs:
                    q = qx[c]
                    E(DVE, lambda h, q=q: [h.tensor_tensor(out=btb[q][:], in0=btb[q][:], in1=mbuf[q][:], op=ALU.mult)],
                      reads=[r_btb[q], r_mbuf[q]], writes=[r_btb[q]])
                    E(DVE, lambda h, c=c, q=q: [h.tensor_tensor_scan(out=hs[c][:], data0=abuf[q][:], data1=btb[q][:],
                                                                     initial=hstate[:, c:c + 1], op0=ALU.mult, op1=ALU.add)],
                      reads=[r_abuf[q], r_btb[q], r_hstate[c]], writes=[r_hs[c]])
                    E(DVE, lambda h, c=c: [h.tensor_copy(out=hstate[:, c:c + 1], in_=hs[c][:, T - 1:T])],
                      reads=[r_hs[c]], writes=[r_hstate[c]])

            def rnn_b(c):
                E(DVE, lambda h, c=c: [h.tensor_tensor(out=hb[:, 4 + c, :], in0=hs[c][:], in1=gy[:, c, :], op=ALU.mult)],
                  reads=[r_hs[c], r_gy[c]], writes=[r_hb[4 + c]])

            def conv_chunk(c):
                si = next_slab(*d_conv(c))
                dflat = slots[si][:].rearrange("p k n -> p (k n)")
                b = next_bank()
                mm_group(b, [(dflat[:, k * 128:(k + 1) * 128], ub[:, c, k:k + T]) for k in range(NCONV)], [r_slot[si], r_ub[c]])
                release_slab()
                E(ACT, lambda h: [h.activation(out=cv[:, c, :], in_=ps[:, b, :], func=AF.Identity,
                                               bias=vecs[:, V_CONVB + c:V_CONVB + c + 1], scale=1.0)],
                  reads=[r_bank[b], r_vecs], writes=[r_cv[c]])
                E(DVE, lambda h: [h.tensor_copy(out=ub[:, c, 0:30], in_=ub[:, c, T:T + 30])], reads=[r_ub[c]], writes=[r_ub[c]])

            yield 'M'
            rnn_pair_a(0)
            yield 'M'
            rnn_pair_a(2)
            for c in range(4):
                yield 'M'
                conv_chunk(c)
            yield 'M'
            si = next_slab(*d_cols("win", bin_, 1536))
            for c in range(4):
                b = next_bank()
                mm_group(b, [(slots[si][:, kc, c * 128:(c + 1) * 128], hb[:, kc, :]) for kc in range(8)], [r_slot[si]] + r_hb)
                E(ACT, lambda h, c=c, b=b: [h.activation(out=gy[:, c, :], in_=ps[:, b, :], func=AF.Gelu_apprx_tanh)],
                  reads=[r_bank[b]], writes=[r_gy[c]])
            release_slab()
            if hook is not None:
                yield from hook()
            yield 'e'
            bs = next_bank(hold=True)
            bq = next_bank(hold=True)
            for c in range(4):
                q = nxt("cvb", 2)
                E(DVE, lambda h, c=c, q=q: [h.tensor_copy(out=cvb[q][:], in_=cv[:, c, :])], reads=[r_cv[c]], writes=[r_cvb[q]])
                E(ACT, lambda h, c=c, q=q: [h.activation(out=cvsq[q][:], in_=cv[:, c, :], func=AF.Square)], reads=[r_cv[c]], writes=[r_cvsq[q]])
                mm_group(bs, [(ones_bf[:], cvb[q][:])], [r_cvb[q], r_const], first=(c == 0), last=(c == 3))
                mm_group(bq, [(ones_bf[:], cvsq[q][:])], [r_cvsq[q], r_const], first=(c == 0), last=(c == 3))
            E(ACT, lambda h: [h.activation(out=meanb[:], in_=ps[:, bs, :], func=AF.Copy, scale=1.0 / 512)],
              reads=[r_bank[bs]], writes=[r_meanb])
            E(ACT, lambda h: [h.activation(out=varb[:], in_=ps[:, bs, :], func=AF.Square, scale=1.0 / 512)],
              reads=[r_bank[bs]], writes=[r_varb])
            E(DVE, lambda h: [h.scalar_tensor_tensor(out=varb[:], in0=ps[:, bq, :], scalar=1.0 / 512, in1=varb[:],
                                                     op0=ALU.mult, op1=ALU.subtract)],
              reads=[r_bank[bq], r_varb], writes=[r_varb])
            E(ACT, lambda h: [h.activation(out=varb[:], in_=varb[:], func=AF.Sqrt, bias=EPS, scale=1.0)],
              reads=[r_varb], writes=[r_varb])
            E(DVE, lambda h: [h.reciprocal(out=varb[:], in_=varb[:])], reads=[r_varb], writes=[r_varb])
            held.discard(bs); held.discard(bq)
            yield 'e'
            for c in range(4):
                rnn_b(c)
            for c in range(4):
                if c % 2 == 0:
                    yield 'e'
                q = nxt("ntmp", 2)
                E(DVE, lambda h, c=c, q=q: [h.tensor_tensor(out=ntmp[q][:], in0=cv[:, c, :], in1=meanb[:], op=ALU.subtract)],
                  reads=[r_cv[c], r_meanb], writes=[r_ntmp[q]])
                E(DVE, lambda h, q=q: [h.tensor_tensor(out=ntmp[q][:], in0=ntmp[q][:], in1=varb[:], op=ALU.mult)],
                  reads=[r_ntmp[q], r_varb], writes=[r_ntmp[q]])
                E(ACT, lambda h, c=c, q=q: [h.activation(out=hb[:, c, :], in_=ntmp[q][:], func=AF.Silu,
                                                       bias=vecs[:, V_LNB + c:V_LNB + c + 1], scale=vecs[:, V_LNG + c:V_LNG + c + 1])],
                  reads=[r_ntmp[q], r_vecs], writes=[r_hb[c]])
            for ch in range(2):
                yield 'M'
                si = next_slab(*d_cols("wout", bout, ch * 512))
                for oc in range(4):
                    fc = 4 * ch + oc
                    b = next_bank()
                    mm_group(b, [(slots[si][:, kc, oc * 128:(oc + 1) * 128], hb[:, kc, :]) for kc in range(8)], [r_slot[si]] + r_hb)
                    E(DVE, lambda h, fc=fc, b=b: [h.scalar_tensor_tensor(
                        out=xT[:, fc, :], in0=ps[:, b, :], scalar=gt2[:, fc:fc + 1], in1=xT[:, fc, :], op0=ALU.mult, op1=ALU.add)],
                        reads=[r_bank[b], r_modv], writes=[r_xT[fc]])
                release_slab()
            mix_done[0] += 1

        def tile_gen(i):
            s = i % 2
            xT = xTs[s]; r_xT = r_xTs[s]; rstd = rstds[s]; r_rstd = r_rstds[s]
            first = (i == 0)
            for fc in range(8):
                if fc % 2 == 0:
                    yield 'e'
                b = next_bank()

                def fn(h, fc=fc, b=b):
                    return [h.transpose(out=ps[:, b, blk * 128:(blk + 1) * 128], in_=xin[:, blk, fc * 128:(fc + 1) * 128],
                                        identity=ident[:]) for blk in range(4)]
                E(PE, fn, reads=[r_xin, r_const], writes=[r_bank[b]])
                E(DVE, lambda h, fc=fc, b=b: [h.tensor_copy(out=xT[:, fc, :], in_=ps[:, b, :])],
                  reads=[r_bank[b]], writes=[r_xT[fc]])
            if i + 1 < n_tiles and not st["dry"]:
                load_x(i + 1)
            if first:
                yield from mod_part(0, 8)
                gs_op(gs1, 8, V_G1)
            yield 'S'
            yield from compute_xsq_and_stats(s)
            yield from norm_apply(s, gs1, sh1)

            def hook1():
                yield from mod_part(8, 12)
                half_op(gth1, 16)
            yield from ffn(s, "w1i", b1i, "w1o", b1o, gth1, hook=hook1 if first else None)
            if first:
                yield from mod_part(12, 20)
                gs_op(gs2, 32, V_GM)
            yield from compute_xsq_and_stats(s)
            yield from norm_apply(s, gs2, sh2)
            yield from mixer(s, hook=(lambda: mod_part(20, 24)) if first else None)
            if first:
                yield from mod_part(24, 32)
                gs_op(gs3, 56, V_G2)
            yield from compute_xsq_and_stats(s)
            yield from norm_apply(s, gs3, sh3)

            def hook2():
                yield from mod_part(32, 36)
                half_op(gth3, 64)
            yield from ffn(s, "w2i", b2i, "w2o", b2o, gth3, hook=hook2 if first else None)
            if first:
                yield from mod_part(36, 44)
                gs_op(gsf, 80, V_GF)
            yield from compute_xsq_and_stats(s)
            for fc in range(8):
                yield 'e'
                q = nxt("ntmp", 2)
                qo = nxt("oT", 2)
                E(DVE, lambda h, q=q, fc=fc: [h.scalar_tensor_tensor(
                    out=ntmp[q][:], in0=xT[:, fc, :], scalar=gsf[:, fc:fc + 1], in1=rstd[:], op0=ALU.mult, op1=ALU.mult)],
                    reads=[r_xT[fc], r_rstd, r_derived], writes=[r_ntmp[q]])
                E(ACT, lambda h, q=q, qo=qo, fc=fc: [h.activation(out=oT[qo][:], in_=ntmp[q][:], func=AF.Identity,
                                                                 bias=shf[:, fc:fc + 1], scale=1.0)],
                  reads=[r_ntmp[q], r_modv], writes=[r_oT[qo]])
                b = next_bank()

                def fn(h, qo=qo, b=b):
                    return [h.transpose(out=ps[:, b, blk * 128:(blk + 1) * 128], in_=oT[qo][:, blk * 128:(blk + 1) * 128],
                                        identity=ident[:]) for blk in range(4)]
                E(PE, fn, reads=[r_oT[qo], r_const], writes=[r_bank[b]])
                k = nxt("oo", 3)
                if fc % 2 == 0:
                    E(DVE, lambda h, k=k, b=b: [h.tensor_copy(out=oo[k][:], in_=ps[:, b, :].rearrange("p (k n) -> p k n", k=4))],
                      reads=[r_bank[b]], writes=[r_oo[k]])
                else:
                    E(ACT, lambda h, k=k, b=b: [h.activation(out=oo[k][:], in_=ps[:, b, :].rearrange("p (k n) -> p k n", k=4), func=AF.Copy)],
                      reads=[r_bank[b]], writes=[r_oo[k]])
                if not st["dry"]:
                    dst = y_d[i * T:(i + 1) * T, fc * 128:(fc + 1) * 128].rearrange("(b p) d -> p b d", p=128)
                    dma1(PL, dst, oo[k][:], [r_oo[k]], [], sem_oo[k])

        def program():
            gens = []
            nxt_tile = [0]

            def step(e):
                cur_tile[0] = e[3]
                e[1] = next(e[0], None)
                while e[1] == 'S':
                    e[2] = True
                    e[1] = next(e[0], None)

            def maybe_spawn():
                while len(gens) < 2 and nxt_tile[0] < n_tiles and (not gens or gens[-1][2]):
                    e = [tile_gen(nxt_tile[0]), None, False, nxt_tile[0]]
                    nxt_tile[0] += 1
                    step(e)
                    gens.append(e)

            def can_own(e):
                return e[1] == 'M' or (e[1] == 'MIX' and mix_done[0] == e[3])

            mix_done[0] = 0
            owner = None
            maybe_spawn()
            while gens:
                if owner is not None and owner[1] in ('M', 'MIX') and owner in gens:
                    step(owner)
                    for o in gens:
                        if o is not owner:
                            for _ in range(2):
                                if o[1] == 'e':
                                    step(o)
                else:
                    owner = None
                    for e in gens:
                        if can_own(e):
                            owner = e
                            break
                    if owner is None:
                        progressed = False
                        for e in gens:
                            if e[1] == 'e':
                                step(e)
                                progressed = True
                        assert progressed, [(e[1], e[3]) for e in gens]
                gens[:] = [e for e in gens if e[1] is not None]
                maybe_spawn()

        program()
        st["dry"] = False
        bank_ctr[0] = 0
        rot.clear()
        held.clear()
        for _ in range(NS):
            issue_slab()
        program()

        final_vals = [(sm.h, sm.val) for sm in sem_oo]
        assert st["used"] == len(plan), (st, len(plan))

        with nc.Block() as block:
            @block.sync
            def _(h):
                replay(SP, h)

            @block.gpsimd
            def _(h):
                replay(PL, h)
                for smh, v in final_vals:
                    if v:
                        h.wait_ge(smh, v)

            @block.tensor
            def _(h):
                replay(PE, h)

            @block.scalar
            def _(h):
                replay(ACT, h)

            @block.vector
            def _(h):
                replay(DVE, h)
    return nc


def prep_inputs(inputs, n_tiles=SEQ // T, cores=N_CORES):
    f = lambda a: np.ascontiguousarray(np.asarray(a, dtype=np.float32))
    col = lambda v, n: f(v).reshape(n, 128).T
    x = f(inputs["x"]); c = f(inputs["c"])
    shared = np.zeros((128, NV), np.float32)
    shared[:, V_G1:V_G1 + 8] = col(inputs["g_ffn1"][0], 8)
    shared[:, V_GM:V_GM + 8] = col(inputs["g_mix"][0], 8)
    shared[:, V_G2:V_G2 + 8] = col(inputs["g_ffn2"][0], 8)
    shared[:, V_GF:V_GF + 8] = col(inputs["g_final"], 8)
    shared[:, V_BMOD:V_BMOD + 72] = col(inputs["b_mod"][0], 72)
    shared[:, V_BMOD + 72:V_BMOD + 88] = col(inputs["b_fmod"], 16)
    shared[:, V_CONVB:V_CONVB + 4] = col(inputs["conv_b"][0], 4)
    shared[:, V_LNG:V_LNG + 4] = col(inputs["ln_g"][0], 4)
    shared[:, V_LNB:V_LNB + 4] = col(inputs["ln_b"][0], 4)
    shared[:, V_RCB:V_RCB + 4] = col(inputs["rnn_conv_b"][0], 4)
    shared[:, V_BA:V_BA + 4] = col(inputs["b_a"][0], 4)
    shared[:, V_BI:V_BI + 4] = col(inputs["b_i"][0], 4)
    shared[:, V_LAM:V_LAM + 4] = col(inputs["lru_lambda"][0], 4)
    rcw = f(inputs["rnn_conv_w"][0])
    for cc in range(4):
        for k in range(4):
            shared[:, V_RCW + 4 * cc + k] = rcw[k, cc * 128:(cc + 1) * 128]
    cw = f(inputs["conv_w"][0])
    dcw = np.zeros((4, 128, NCONV, 128), np.float32)
    ar = np.arange(128)
    for cc in range(4):
        for k in range(NCONV):
            dcw[cc, ar, k, ar] = cw[k, cc * 128:(cc + 1) * 128]
    dcw = dcw.reshape(512, NCONV * 128)
    def bd(w):
        w = f(w)
        o = np.zeros((128, 4, 128), np.float32)
        for cc in range(4):
            o[0:64, cc, 0:64] = w[2 * cc]
            o[64:128, cc, 64:128] = w[2 * cc + 1]
        return o.reshape(128, 512)
    wabd = bd(inputs["w_a"][0]); wibd = bd(inputs["w_i"][0])
    common = {
        "ident": np.eye(128, dtype=np.float32), "wabd": wabd, "wibd": wibd,
        "w_mod": f(inputs["w_mod"][0]), "w_fmod": f(inputs["w_fmod"]),
        "w1i": f(inputs["w_ffn1_in"][0]), "w1o": f(inputs["w_ffn1_out"][0]),
        "win": f(inputs["w_in"][0]), "wout": f(inputs["w_out"][0]),
        "w2i": f(inputs["w_ffn2_in"][0]), "w2o": f(inputs["w_ffn2_out"][0]),
        "dcw": dcw,
    }
    in_maps = []
    S = n_tiles * T
    for b in range(cores):
        v = shared.copy()
        v[:, V_C:V_C + 8] = c[b].reshape(8, 128).T
        m = dict(common)
        m["vecs"] = v
        m["x"] = np.ascontiguousarray(x[b, :S, :])
        in_maps.append(m)
    return in_maps


_NC_CACHE = {}


def kernel(**inputs):
    n_tiles = SEQ // T
    if n_tiles not in _NC_CACHE:
        _NC_CACHE[n_tiles] = build_program(n_tiles)
    nc = _NC_CACHE[n_tiles]
    in_maps = prep_inputs(inputs, n_tiles, N_CORES)
    res = run_bass_kernel_spmd(nc, in_maps, core_ids=list(range(N_CORES)))
    out = np.stack([np.asarray(r["y"], dtype=np.float32) for r in res.results], axis=0)
    return out
```

```python
import numpy as np
import concourse.bass as bass
import concourse.mybir as mybir
from concourse.bass_utils import run_bass_kernel_spmd

F32 = mybir.dt.float32
BF16 = mybir.dt.bfloat16
AF = mybir.ActivationFunctionType
ALU = mybir.AluOpType

D = 1024
DFF = 2816
SEQ = 4096
T = 512
KC = 8
FC = 22
NCONV = 31
EPS = 1e-6
NS = 3
N_CORES = 8

V_G1, V_GM, V_G2, V_GF = 0, 8, 16, 24
V_BMOD = 32
V_CONVB = 120
V_LNG = 124
V_LNB = 128
V_RCB = 132
V_BA = 136
V_BI = 140
V_LAM = 144
V_RCW = 148
V_C = 164
NV = 172


class Sem:
    def __init__(self, h):
        self.h = h
        self.val = 0


class Res:
    __slots__ = ("w", "r")

    def __init__(self):
        self.w = None
        self.r = {}


class Eng:
    def __init__(self, name, sem, self_sync):
        self.name = name
        self.sem = sem
        self.prog = []
        self.seen = {}
        self.self_sync = self_sync


def emit(eng, fn, reads=(), writes=(), dma=None, ndma=1):
    waits = {}

    def need(tok):
        if tok is None:
            return
        s, v = tok
        if waits.get(s, 0) < v:
            waits[s] = v

    for r in reads:
        need(r.w)
    for w in writes:
        need(w.w)
        for s, v in w.r.items():
            need((s, v))
    wl = []
    for s, v in waits.items():
        if s is eng.sem and not eng.self_sync:
            continue
        if eng.seen.get(s, 0) >= v:
            continue
        eng.seen[s] = v
        wl.append((s, v))
    if dma is None:
        eng.sem.val += 1
        tok = (eng.sem, eng.sem.val)
        eng.prog.append((wl, fn, eng.sem, 1, False))
    else:
        dma.val += 16 * ndma
        tok = (dma, dma.val)
        eng.prog.append((wl, fn, dma, 16, True))
    for r in reads:
        if r.r.get(tok[0], 0) < tok[1]:
            r.r[tok[0]] = tok[1]
    for w in writes:
        w.w = tok
        w.r = {}
    return tok


def replay(eng, h):
    for wl, fn, sem, inc, is_dma in eng.prog:
        for s, v in wl:
            h.wait_ge(s.h, v)
        ins = fn(h)
        if is_dma:
            for i in ins:
                i.then_inc(sem.h, 16)
        else:
            ins[-1].then_inc(sem.h, 1)


def build_program(n_tiles=SEQ // T):
    S = n_tiles * T
    nc = bass.Bass("TRN2", target_bir_lowering=False)
    dt_in = lambda name, shape: nc.dram_tensor(name, shape, F32, kind="ExternalInput").ap()
    x_d = dt_in("x", [S, D])
    vecs_d = dt_in("vecs", [128, NV])
    ident_d = dt_in("ident", [128, 128])
    wabd_d = dt_in("wabd", [128, 512])
    wibd_d = dt_in("wibd", [128, 512])
    wmod_d = dt_in("w_mod", [D, 9 * D])
    wfmod_d = dt_in("w_fmod", [D, 2 * D])
    w1i_d = dt_in("w1i", [D, 2 * DFF])
    w1o_d = dt_in("w1o", [DFF, D])
    win_d = dt_in("win", [D, 2 * D])
    wout_d = dt_in("wout", [D, D])
    w2i_d = dt_in("w2i", [D, 2 * DFF])
    w2o_d = dt_in("w2o", [DFF, D])
    dcw_d = dt_in("dcw", [512, NCONV * 128])
    y_d = nc.dram_tensor("y", [S, D], F32, kind="ExternalOutput").ap()
    dt_sc = lambda name, shape: nc.dram_tensor(name, shape, BF16, kind="Internal").ap()
    b1i = dt_sc("b1i", [D, 2 * DFF])
    b1o = dt_sc("b1o", [DFF, D])
    bin_ = dt_sc("bin", [D, 2 * D])
    bout = dt_sc("bout", [D, D])
    b2i = dt_sc("b2i", [D, 2 * DFF])
    b2o = dt_sc("b2o", [DFF, D])
    bdc = dt_sc("bdc", [512, NCONV * 128])

    from contextlib import ExitStack
    es = ExitStack()
    with es:
        sb = lambda name, shape, dt=F32: es.enter_context(nc.sbuf_tensor("sb_" + name, shape, dt))
        newsem = lambda name: Sem(es.enter_context(nc.semaphore(name)))

        PE = Eng("pe", newsem("s_pe"), False)
        ACT = Eng("act", newsem("s_act"), True)
        DVE = Eng("dve", newsem("s_dve"), True)
        SP = Eng("sp", newsem("s_sp"), False)
        PL = Eng("pool", newsem("s_pool"), False)

        ident = sb("ident", [128, 128])
        ones_bf = sb("ones_bf", [128, 128], BF16)
        vecs = sb("vecs", [128, NV])
        modv = sb("modv", [128, 88])
        cact = sb("cact", [128, 8])
        gs1 = sb("gs1", [128, 8]); gth1 = sb("gth1", [128, 8])
        gs2 = sb("gs2", [128, 8])
        gs3 = sb("gs3", [128, 8]); gth3 = sb("gth3", [128, 8])
        gsf = sb("gsf", [128, 8])
        lam8 = sb("lam8", [128, 4]); lam16 = sb("lam16", [128, 4]); lamt = sb("lamt", [128, 4])
        wabd = sb("wabd", [128, 4, 128], BF16)
        wibd = sb("wibd", [128, 4, 128], BF16)
        hstate = sb("hstate", [128, 4])
        slots = [sb(f"slab{i}", [128, 8, 512], BF16) for i in range(NS)]
        xin = sb("xin", [128, 4, D])
        xTs = [sb(f"xT{k}", [128, 8, T]) for k in range(2)]
        xsq = [sb(f"xsq{i}", [128, T], BF16) for i in range(4)]
        hbs = [sb(f"hb{k}", [128, 8, T], BF16) for k in range(2)]
        hid = sb("hid", [128, FC, T], BF16)
        sgb = [sb(f"sgb{i}", [128, T]) for i in range(2)]
        ntmp = [sb(f"ntmp{i}", [128, T]) for i in range(2)]
        rstds = [sb(f"rstd{k}", [128, T]) for k in range(2)]
        ub = sb("ub", [128, 4, T + 30], BF16)
        ux = sb("ux", [128, 4, T + 3])
        gy = sb("gy", [128, 4, T], BF16)
        cv = sb("cv", [128, 4, T])
        cvsq = [sb(f"cvsq{i}", [128, T], BF16) for i in range(2)]
        cvb = [sb(f"cvb{i}", [128, T], BF16) for i in range(2)]
        meanb = sb("meanb", [128, T])
        varb = sb("varb", [128, T])
        xr = [sb(f"xr{i}", [128, T]) for i in range(4)]
        xrb = [sb(f"xrb{i}", [128, T], BF16) for i in range(4)]
        abuf = [sb(f"abuf{i}", [128, T]) for i in range(2)]
        mbuf = [sb(f"mbuf{i}", [128, T]) for i in range(2)]
        btb = [sb(f"btb{i}", [128, T]) for i in range(2)]
        hs = [sb(f"hs{i}", [128, T]) for i in range(4)]
        oT = [sb(f"oT{i}", [128, T]) for i in range(3)]
        oo = [sb(f"oo{k}", [128, 4, 128]) for k in range(3)]
        ps = es.enter_context(nc.psum_tensor("ps", [128, 8, T], F32))

        R = lambda: Res()
        r_const = R()
        r_vecs = R(); r_modv = R(); r_cact = R(); r_derived = R(); r_hstate = [R() for _ in range(4)]
        r_gates = R()
        r_slot = [R() for _ in range(NS)]
        sem_slot = [newsem(f"s_slot{i}") for i in range(NS)]
        r_bank = [R() for _ in range(8)]
        r_xin = R(); sem_xin = newsem("s_xin")
        r_xTs = [[R() for _ in range(8)] for _ in range(2)]
        r_xsq = [R() for _ in range(4)]
        r_hbs = [[R() for _ in range(8)] for _ in range(2)]
        r_hid = [R() for _ in range(FC)]
        r_sgb = [R() for _ in range(2)]
        r_ntmp = [R() for _ in range(2)]
        r_rstds = [R(), R()]
        r_ub = [R() for _ in range(4)]
        r_ux = [R() for _ in range(4)]
        r_gy = [R() for _ in range(4)]
        r_cv = [R() for _ in range(4)]
        r_cvsq = [R() for _ in range(2)]
        r_cvb = [R() for _ in range(2)]
        r_meanb = R(); r_varb = R()
        r_xr = [R() for _ in range(4)]; r_xrb = [R() for _ in range(4)]
        r_abuf = [R() for _ in range(2)]
        r_mbuf = [R() for _ in range(2)]; r_btb = [R() for _ in range(2)]
        r_hs = [R() for _ in range(4)]
        r_oT = [R() for _ in range(3)]
        r_oo = [R() for _ in range(3)]; sem_oo = [newsem(f"s_oo{k}") for k in range(3)]
        sem_const = newsem("s_const")
        r_scr = {}
        sem_wb = [newsem(f"s_wb{i}") for i in range(NS)]
        sem_slot_pl = [newsem(f"s_slotp{i}") for i in range(NS)]

        bank_ctr = [0]

        held = set()

        def next_bank(hold=False):
            while True:
                b = bank_ctr[0] % 8
                bank_ctr[0] += 1
                if b not in held:
                    break
            if hold:
                held.add(b)
            return b

        rot = {}

        def nxt(name, n):
            v = rot.get(name, 0)
            rot[name] = v + 1
            return v % n

        def dma1(eng, out_ap, in_ap, reads, writes, sem):
            return emit(eng, lambda h: [h.dma_start(out=out_ap, in_=in_ap)], reads=reads, writes=writes, dma=sem)

        dma1(SP, vecs[:], vecs_d[:, :], [], [r_vecs], sem_const)
        dma1(SP, ident[:], ident_d[:, :], [], [r_const], newsem("s_ident"))
        def load_x(i):
            src = x_d[i * T:(i + 1) * T, :].rearrange("(b p) d -> p b d", p=128)
            dma1(PL, xin[:], src, [], [r_xin], sem_xin)
        load_x(0)
        sem_g = newsem("s_gates")
        dma1(PL, wabd[:], wabd_d[:, :].rearrange("p (c n) -> p c n", c=4), [], [r_gates], sem_g)
        dma1(PL, wibd[:], wibd_d[:, :].rearrange("p (c n) -> p c n", c=4), [], [r_gates], sem_g)

        emit(DVE, lambda h: [h.memset(ones_bf[:], 1.0)], writes=[r_const])
        emit(DVE, lambda h: [h.memset(hstate[:], 0.0)], writes=r_hstate)
        emit(DVE, lambda h: [h.memset(ub[:, :, 0:30], 0.0)], writes=r_ub)
        emit(DVE, lambda h: [h.memset(ux[:, :, 0:3], 0.0)], writes=r_ux)
        emit(ACT, lambda h: [h.activation(out=cact[:], in_=vecs[:, V_C:V_C + 8], func=AF.Silu)],
             reads=[r_vecs], writes=[r_cact])
        emit(ACT, lambda h: [h.activation(out=lamt[:], in_=vecs[:, V_LAM:V_LAM + 4], func=AF.Exp, scale=-1.0)],
             reads=[r_vecs], writes=[r_derived])
        emit(ACT, lambda h: [h.activation(out=lamt[:], in_=lamt[:], func=AF.Ln, bias=1.0)],
             reads=[r_derived], writes=[r_derived])
        emit(DVE, lambda h: [h.tensor_scalar(out=lam8[:], in0=lamt[:], scalar1=-8.0, scalar2=None, op0=ALU.mult)],
             reads=[r_derived], writes=[r_derived])
        emit(DVE, lambda h: [h.tensor_scalar(out=lam16[:], in0=lamt[:], scalar1=-16.0, scalar2=None, op0=ALU.mult)],
             reads=[r_derived], writes=[r_derived])

        kview = lambda ap: ap.rearrange("(kc p) n -> p kc n", p=128)
        plan = []
        st = {"issued": 0, "used": 0, "dry": True}

        cur_tile = [0]
        pos_ctr = {}

        def issue_slab():
            k = st["issued"]
            if k >= len(plan):
                return
            st["issued"] += 1
            si = k % NS
            cname, pfn, tile, pos = plan[k]

            def mk(pieces):
                return lambda h: [h.dma_start(out=o, in_=i) for o, i in pieces]
            if cname is None:
                pieces = pfn(si, 0)
                emit(SP, mk(pieces), writes=[r_slot[si]], dma=sem_slot[si], ndma=len(pieces))
            elif tile == 0:
                pieces = pfn(si, 1)
                emit(PL, mk(pieces), writes=[r_slot[si]], dma=sem_slot_pl[si], ndma=len(pieces))
                back = [(d, o) for o, d in pfn(si, 0)]
                r_scr[pos] = Res()
                emit(SP, mk(back), reads=[r_slot[si]], writes=[r_scr[pos]], dma=sem_wb[si], ndma=len(back))
            else:
                pieces = pfn(si, 0)
                emit(SP, mk(pieces), reads=[r_scr[pos]], writes=[r_slot[si]], dma=sem_slot[si], ndma=len(pieces))

        def next_slab(cname, pfn):
            if st["dry"]:
                t = cur_tile[0]
                pos = None
                if cname is not None:
                    pos = pos_ctr.get(t, 0)
                    pos_ctr[t] = pos + 1
                plan.append((cname, pfn, t, pos))
                return 0
            k = st["used"]
            st["used"] += 1
            assert k < st["issued"] and plan[k][0] == cname
            return k % NS

        def release_slab():
            if not st["dry"]:
                issue_slab()

        def E(eng, fn, reads=(), writes=()):
            if st["dry"]:
                return None
            return emit(eng, fn, reads=reads, writes=writes)

        def d_mod(g):
            src = wmod_d if g < 36 else wfmod_d
            gg = g if g < 36 else g - 36
            return (None, lambda si, f: [(slots[si][:].bitcast(F32), kview(src[:, gg * 256:(gg + 1) * 256]))])

        WT = {"w1i": (b1i, w1i_d), "w1o": (b1o, w1o_d), "win": (bin_, win_d), "wout": (bout, wout_d),
              "w2i": (b2i, w2i_d), "w2o": (b2o, w2o_d), "dcw": (bdc, dcw_d)}

        def d_ffn_in(nm, wi_unused, j):
            return (nm, lambda si, f: [
                (slots[si][:, :, 0:256], kview(WT[nm][f][:, 256 * j:256 * j + 256])),
                (slots[si][:, :, 256:512], kview(WT[nm][f][:, DFF + 256 * j:DFF + 256 * j + 256]))])

        def d_ffn_out(nm, wo_unused, ch, k0, nk):
            return (nm, lambda si, f: [(slots[si][:, 0:nk, :], kview(WT[nm][f][k0 * 128:(k0 + nk) * 128, ch * 512:(ch + 1) * 512]))])

        def d_cols(nm, w_unused, c0):
            return (nm, lambda si, f: [(slots[si][:, :, :], kview(WT[nm][f][:, c0:c0 + 512]))])

        def d_glu(half):
            return ("win", lambda si, f: [
                (slots[si][:, :, 0:256], kview(WT["win"][f][:, 256 * half:256 * half + 256])),
                (slots[si][:, :, 256:512], kview(WT["win"][f][:, 512 + 256 * half:512 + 256 * half + 256]))])

        def d_conv(c):
            return ("dcw", lambda si, f: [
                (slots[si][:].rearrange("p k n -> p (k n)")[:, 0:NCONV * 128], WT["dcw"][f][c * 128:(c + 1) * 128, :])])

        def mm_group(bank, pairs, reads, first=True, last=True):
            n = len(pairs)

            def fn(h):
                out = []
                for idx, (l, r) in enumerate(pairs):
                    out.append(h.matmul(ps[:, bank, :], lhsT=l, rhs=r,
                                        start=(first and idx == 0), stop=(last and idx == n - 1)))
                return out
            return E(PE, fn, reads=reads, writes=[r_bank[bank]])

        sh1 = modv[:, 0:8]; sh2 = modv[:, 24:32]; gt2 = modv[:, 40:48]; sh3 = modv[:, 48:56]; shf = modv[:, 72:80]

        def mod_part(g0, g1):
            bank = next_bank(hold=True)
            for g in range(g0, g1):
                si = next_slab(*d_mod(g))
                sf = slots[si][:].bitcast(F32)

                def fn(h, g=g, sf=sf):
                    out = []
                    for sub in range(2):
                        j = 2 * g + sub
                        for kc in range(8):
                            out.append(h.matmul(ps[:, bank, j:j + 1], lhsT=sf[:, kc, sub * 128:(sub + 1) * 128],
                                                rhs=cact[:, kc:kc + 1], start=(kc == 0), stop=(kc == 7)))
                    return out
                E(PE, fn, reads=[r_slot[si], r_cact], writes=[r_bank[bank]])
                release_slab()
                yield 'M'
            c0, c1 = 2 * g0, 2 * g1
            E(DVE, lambda h: [h.tensor_tensor(out=modv[:, c0:c1], in0=ps[:, bank, c0:c1], in1=vecs[:, V_BMOD + c0:V_BMOD + c1], op=ALU.add)],
              reads=[r_bank[bank], r_vecs], writes=[r_modv])
            held.discard(bank)

        def gs_op(dst, sc_col, g_col):
            E(DVE, lambda h: [h.scalar_tensor_tensor(out=dst[:], in0=modv[:, sc_col:sc_col + 8], scalar=1.0,
                                                     in1=vecs[:, g_col:g_col + 8], op0=ALU.add, op1=ALU.mult)],
              reads=[r_modv, r_vecs], writes=[r_derived])

        def half_op(dst, col):
            E(DVE, lambda h: [h.tensor_scalar(out=dst[:], in0=modv[:, col:col + 8], scalar1=0.5, scalar2=None, op0=ALU.mult)],
              reads=[r_modv], writes=[r_derived])

        def rsqrt_chain(buf, res, bank, scale):
            E(ACT, lambda h: [h.activation(out=buf[:], in_=ps[:, bank, :], func=AF.Sqrt, bias=EPS, scale=scale)],
              reads=[r_bank[bank]], writes=[res])
            E(DVE, lambda h: [h.reciprocal(out=buf[:], in_=buf[:])], reads=[res], writes=[res])

        def compute_xsq_and_stats(s):
            xT = xTs[s]; r_xT = r_xTs[s]; rstd = rstds[s]; r_rstd = r_rstds[s]
            yield 'e'
            sbank = next_bank(hold=True)
            pend = []

            def flush():
                for (q, fc) in pend:
                    mm_group(sbank, [(ones_bf[:], xsq[q][:])], [r_xsq[q], r_const], first=(fc == 0), last=(fc == 7))
                del pend[:]
            for fc in range(8):
                if fc % 2 == 0:
                    flush()
                q = nxt("xsq", 4)
                E(ACT, lambda h, q=q, fc=fc: [h.activation(out=xsq[q][:], in_=xT[:, fc, :], func=AF.Square)],
                  reads=[r_xT[fc]], writes=[r_xsq[q]])
                pend.append((q, fc))
                if fc % 2 == 1:
                    yield 'e'
            flush()
            rsqrt_chain(rstd, r_rstd, sbank, 1.0 / D)
            held.discard(sbank)

        def norm_apply(s, gs, sh):
            xT = xTs[s]; r_xT = r_xTs[s]; rstd = rstds[s]; r_rstd = r_rstds[s]; hb = hbs[s]; r_hb = r_hbs[s]
            for fc in range(8):
                if fc % 2 == 0:
                    yield 'e'
                q = nxt("ntmp", 2)
                E(DVE, lambda h, q=q, fc=fc: [h.scalar_tensor_tensor(
                    out=ntmp[q][:], in0=xT[:, fc, :], scalar=gs[:, fc:fc + 1], in1=rstd[:], op0=ALU.mult, op1=ALU.mult)],
                    reads=[r_xT[fc], r_rstd, r_derived], writes=[r_ntmp[q]])
                E(ACT, lambda h, q=q, fc=fc: [h.activation(out=hb[:, fc, :], in_=ntmp[q][:], func=AF.Identity,
                                                           bias=sh[:, fc:fc + 1], scale=1.0)],
                  reads=[r_ntmp[q], r_modv], writes=[r_hb[fc]])

        def ffn(s, nm_i, wi, nm_o, wo, gth, hook=None):
            xT = xTs[s]; r_xT = r_xTs[s]; hb = hbs[s]; r_hb = r_hbs[s]
            for j in range(11):
                yield 'M'
                si = next_slab(*d_ffn_in(nm_i, wi, j))
                for sub in range(2):
                    cg = 2 * j + sub
                    bg = next_bank()
                    mm_group(bg, [(slots[si][:, kc, sub * 128:(sub + 1) * 128], hb[:, kc, :]) for kc in range(8)],
                             [r_slot[si]] + r_hb)
                    bu = next_bank()
                    mm_group(bu, [(slots[si][:, kc, 256 + sub * 128:256 + (sub + 1) * 128], hb[:, kc, :]) for kc in range(8)],
                             [r_slot[si]] + r_hb)
                    q = nxt("sgb", 2)
                    E(ACT, lambda h, q=q, bg=bg: [h.activation(out=sgb[q][:], in_=ps[:, bg, :], func=AF.Silu)],
                      reads=[r_bank[bg]], writes=[r_sgb[q]])
                    E(DVE, lambda h, q=q, bu=bu, cg=cg: [h.tensor_tensor(out=hid[:, cg, :], in0=ps[:, bu, :], in1=sgb[q][:], op=ALU.mult)],
                      reads=[r_bank[bu], r_sgb[q]], writes=[r_hid[cg]])
                release_slab()
            if hook is not None:
                yield from hook()
            for ch in range(2):
                yield 'M'
                banks = [next_bank(hold=True) for _ in range(4)]
                for (k0, nk) in ((0, 8), (8, 8), (16, 6)):
                    si = next_slab(*d_ffn_out(nm_o, wo, ch, k0, nk))
                    for oc in range(4):
                        mm_group(banks[oc],
                                 [(slots[si][:, kl, oc * 128:(oc + 1) * 128], hid[:, k0 + kl, :]) for kl in range(nk)],
                                 [r_slot[si]] + r_hid[k0:k0 + nk], first=(k0 == 0), last=(k0 == 16))
                    release_slab()
                for oc in range(4):
                    fc = 4 * ch + oc
                    E(DVE, lambda h, fc=fc, b=banks[oc]: [h.scalar_tensor_tensor(
                        out=xT[:, fc, :], in0=ps[:, b, :], scalar=gth[:, fc:fc + 1], in1=xT[:, fc, :], op0=ALU.mult, op1=ALU.add)],
                        reads=[r_bank[banks[oc]], r_derived, r_modv], writes=[r_xT[fc]])
                for b in banks:
                    held.discard(b)

        mix_done = [0]

        def mixer(s, hook=None):
            xT = xTs[s]; r_xT = r_xTs[s]; hb = hbs[s]; r_hb = r_hbs[s]
            yield 'MIX'
            si = next_slab(*d_cols("win", bin_, 1024))
            for c in range(4):
                b = next_bank()
                mm_group(b, [(slots[si][:, kc, c * 128:(c + 1) * 128], hb[:, kc, :]) for kc in range(8)], [r_slot[si]] + r_hb)
                E(ACT, lambda h, c=c, b=b: [h.activation(out=ux[:, c, 3:3 + T], in_=ps[:, b, :], func=AF.Copy)],
                  reads=[r_bank[b]], writes=[r_ux[c]])
            release_slab()
            for c in range(4):
                E(DVE, lambda h, c=c: [h.tensor_scalar(
                    out=xr[c][:], in0=ux[:, c, 0:T], scalar1=vecs[:, V_RCW + 4 * c:V_RCW + 4 * c + 1],
                    scalar2=vecs[:, V_RCB + c:V_RCB + c + 1], op0=ALU.mult, op1=ALU.add)],
                    reads=[r_ux[c], r_vecs], writes=[r_xr[c]])
                for k in range(1, 4):
                    E(DVE, lambda h, c=c, k=k: [h.scalar_tensor_tensor(
                        out=xr[c][:], in0=ux[:, c, k:k + T], scalar=vecs[:, V_RCW + 4 * c + k:V_RCW + 4 * c + k + 1],
                        in1=xr[c][:], op0=ALU.mult, op1=ALU.add)],
                        reads=[r_ux[c], r_vecs, r_xr[c]], writes=[r_xr[c]])
                E(DVE, lambda h, c=c: [h.tensor_copy(out=ux[:, c, 0:3], in_=ux[:, c, T:T + 3])], reads=[r_ux[c]], writes=[r_ux[c]])
                E(ACT, lambda h, c=c: [h.activation(out=xrb[c][:], in_=xr[c][:], func=AF.Copy)],
                  reads=[r_xr[c]], writes=[r_xrb[c]])
            for half in range(2):
                yield 'M'
                si = next_slab(*d_glu(half))
                for sub in range(2):
                    c = 2 * half + sub
                    bv = next_bank()
                    mm_group(bv, [(slots[si][:, kc, sub * 128:(sub + 1) * 128], hb[:, kc, :]) for kc in range(8)], [r_slot[si]] + r_hb)
                    bg = next_bank()
                    mm_group(bg, [(slots[si][:, kc, 256 + sub * 128:256 + (sub + 1) * 128], hb[:, kc, :]) for kc in range(8)], [r_slot[si]] + r_hb)
                    q = nxt("sgb", 2)
                    E(ACT, lambda h, q=q, bg=bg: [h.activation(out=sgb[q][:], in_=ps[:, bg, :], func=AF.Sigmoid)],
                      reads=[r_bank[bg]], writes=[r_sgb[q]])
                    E(DVE, lambda h, q=q, bv=bv, c=c: [h.tensor_tensor(out=ub[:, c, 30:30 + T], in0=ps[:, bv, :], in1=sgb[q][:], op=ALU.mult)],
                      reads=[r_bank[bv], r_sgb[q]], writes=[r_ub[c]])
                release_slab()

            hs_of = {}

            def rnn_pair_a(c0):
                cs = (c0, c0 + 1)
                qx = {c: c for c in cs}
                br = {}; bi = {}
                for c in cs:
                    br[c] = next_bank()
                    mm_group(br[c], [(wabd[:, c, :], xrb[c][:])], [r_gates, r_xrb[c]])
                    bi[c] = next_bank()
                    mm_group(bi[c], [(wibd[:, c, :], xrb[c][:])], [r_gates, r_xrb[c]])
                qx = {c: c % 2 for c in cs}
                for c in cs:
                    q = qx[c]
                    E(ACT, lambda h, c=c, q=q: [h.activation(out=abuf[q][:], in_=ps[:, br[c], :], func=AF.Sigmoid,
                                                           bias=vecs[:, V_BA + c:V_BA + c + 1], scale=1.0)],
                      reads=[r_bank[br[c]], r_vecs], writes=[r_abuf[q]])
                    E(ACT, lambda h, c=c, q=q: [h.activation(out=btb[q][:], in_=ps[:, bi[c], :], func=AF.Sigmoid,
                                                           bias=vecs[:, V_BI + c:V_BI + c + 1], scale=1.0)],
                      reads=[r_bank[bi[c]], r_vecs], writes=[r_btb[q]])
                for c in cs:
                    q = qx[c]
                    E(ACT, lambda h, c=c, q=q: [h.activation(out=mbuf[q][:], in_=abuf[q][:], func=AF.Exp, scale=lam16[:, c:c + 1])],
                      reads=[r_abuf[q], r_derived], writes=[r_mbuf[q]])
                    E(ACT, lambda h, c=c, q=q: [h.activation(out=abuf[q][:], in_=abuf[q][:], func=AF.Exp, scale=lam8[:, c:c + 1])],
                      reads=[r_abuf[q], r_derived], writes=[r_abuf[q]])
                for c in cs:
                    q = qx[c]
                    E(DVE, lambda h, q=q: [h.tensor_scalar(out=mbuf[q][:], in0=mbuf[q][:], scalar1=1.0, scalar2=-1.0, op0=ALU.min, op1=ALU.mult)],
                      reads=[r_mbuf[q]], writes=[r_mbuf[q]])
                    E(DVE, lambda h, q=q, c=c: [h.tensor_tensor(out=btb[q][:], in0=btb[q][:], in1=xr[c][:], op=ALU.mult)],
                      reads=[r_btb[q], r_xr[c]], writes=[r_btb[q]])
                for c in cs:
                    q = qx[c]
                    E(ACT, lambda h, q=q: [h.activation(out=mbuf[q][:], in_=mbuf[q][:], func=AF.Sqrt, bias=1.0, scale=1.0)],
                      reads=[r_mbuf[q]], writes=[r_mbuf[q]])
                for c in cs:
                    q = qx[c]
                    E(DVE, lambda h, q=q: [h.tensor_tensor(out=btb[q][:], in0=btb[q][:], in1=mbuf[q][:], op=ALU.mult)],
                      reads=[r_btb[q], r_mbuf[q]], writes=[r_btb[q]])
                    E(DVE, lambda h, c=c, q=q: [h.tensor_tensor_scan(out=hs[c][:], data0=abuf[q][:], data1=btb[q][:],
                                                                     initial=hstate[:, c:c + 1], op0=ALU.mult, op1=ALU.add)],
                      reads=[r_abuf[q], r_btb[q], r_hstate[c]], writes=[r_hs[c]])
                    E(DVE, lambda h, c=c: [h.tensor_copy(out=hstate[:, c:c + 1], in_=hs[c][:, T - 1:T])],
                      reads=[r_hs[c]], writes=[r_hstate[c]])

            def rnn_b(c):
                E(DVE, lambda h, c=c: [h.tensor_tensor(out=hb[:, 4 + c, :], in0=hs[c][:], in1=gy[:, c, :], op=ALU.mult)],
                  reads=[r_hs[c], r_gy[c]], writes=[r_hb[4 + c]])

            def conv_chunk(c):
                si = next_slab(*d_conv(c))
                dflat = slots[si][:].rearrange("p k n -> p (k n)")
                b = next_bank()
                mm_group(b, [(dflat[:, k * 128:(k + 1) * 128], ub[:, c, k:k + T]) for k in range(NCONV)], [r_slot[si], r_ub[c]])
                release_slab()
                E(ACT, lambda h: [h.activation(out=cv[:, c, :], in_=ps[:, b, :], func=AF.Identity,
                                               bias=vecs[:, V_CONVB + c:V_CONVB + c + 1], scale=1.0)],
                  reads=[r_bank[b], r_vecs], writes=[r_cv[c]])
                E(DVE, lambda h: [h.tensor_copy(out=ub[:, c, 0:30], in_=ub[:, c, T:T + 30])], reads=[r_ub[c]], writes=[r_ub[c]])
                q = nxt("cvb", 2)
                E(DVE, lambda h, q=q: [h.tensor_copy(out=cvb[q][:], in_=cv[:, c, :])], reads=[r_cv[c]], writes=[r_cvb[q]])
                E(ACT, lambda h, q=q: [h.activation(out=cvsq[q][:], in_=cv[:, c, :], func=AF.Square)], reads=[r_cv[c]], writes=[r_cvsq[q]])
                mm_group(ln_banks[0], [(ones_bf[:], cvb[q][:])], [r_cvb[q], r_const], first=(c == 0), last=(c == 3))
                mm_group(ln_banks[1], [(ones_bf[:], cvsq[q][:])], [r_cvsq[q], r_const], first=(c == 0), last=(c == 3))

            yield 'M'
            rnn_pair_a(0)
            yield 'M'
            rnn_pair_a(2)
            ln_banks = [next_bank(hold=True), next_bank(hold=True)]
            for c in range(4):
                yield 'M'
                conv_chunk(c)
            bs, bq = ln_banks
            E(ACT, lambda h: [h.activation(out=meanb[:], in_=ps[:, bs, :], func=AF.Copy, scale=1.0 / 512)],
              reads=[r_bank[bs]], writes=[r_meanb])
            E(ACT, lambda h: [h.activation(out=varb[:], in_=ps[:, bs, :], func=AF.Square, scale=1.0 / 512)],
              reads=[r_bank[bs]], writes=[r_varb])
            E(DVE, lambda h: [h.scalar_tensor_tensor(out=varb[:], in0=ps[:, bq, :], scalar=1.0 / 512, in1=varb[:],
                                                     op0=ALU.mult, op1=ALU.subtract)],
              reads=[r_bank[bq], r_varb], writes=[r_varb])
            E(ACT, lambda h: [h.activation(out=varb[:], in_=varb[:], func=AF.Sqrt, bias=EPS, scale=1.0)],
              reads=[r_varb], writes=[r_varb])
            E(DVE, lambda h: [h.reciprocal(out=varb[:], in_=varb[:])], reads=[r_varb], writes=[r_varb])
            held.discard(bs); held.discard(bq)
            yield 'M'
            si = next_slab(*d_cols("win", bin_, 1536))
            for c in range(4):
                b = next_bank()
                mm_group(b, [(slots[si][:, kc, c * 128:(c + 1) * 128], hb[:, kc, :]) for kc in range(8)], [r_slot[si]] + r_hb)
                E(ACT, lambda h, c=c, b=b: [h.activation(out=gy[:, c, :], in_=ps[:, b, :], func=AF.Gelu_apprx_tanh)],
                  reads=[r_bank[b]], writes=[r_gy[c]])
            release_slab()
            if hook is not None:
                yield from hook()
            yield 'e'
            for c in range(4):
                rnn_b(c)
            for c in range(4):
                if c % 2 == 0:
                    yield 'e'
                q = nxt("ntmp", 2)
                E(DVE, lambda h, c=c, q=q: [h.tensor_tensor(out=ntmp[q][:], in0=cv[:, c, :], in1=meanb[:], op=ALU.subtract)],
                  reads=[r_cv[c], r_meanb], writes=[r_ntmp[q]])
                E(DVE, lambda h, q=q: [h.tensor_tensor(out=ntmp[q][:], in0=ntmp[q][:], in1=varb[:], op=ALU.mult)],
                  reads=[r_ntmp[q], r_varb], writes=[r_ntmp[q]])
                E(ACT, lambda h, c=c, q=q: [h.activation(out=hb[:, c, :], in_=ntmp[q][:], func=AF.Silu,
                                                       bias=vecs[:, V_LNB + c:V_LNB + c + 1], scale=vecs[:, V_LNG + c:V_LNG + c + 1])],
                  reads=[r_ntmp[q], r_vecs], writes=[r_hb[c]])
            for ch in range(2):
                yield 'M'
                si = next_slab(*d_cols("wout", bout, ch * 512))
                for oc in range(4):
                    fc = 4 * ch + oc
                    b = next_bank()
                    mm_group(b, [(slots[si][:, kc, oc * 128:(oc + 1) * 128], hb[:, kc, :]) for kc in range(8)], [r_slot[si]] + r_hb)
                    E(DVE, lambda h, fc=fc, b=b: [h.scalar_tensor_tensor(
                        out=xT[:, fc, :], in0=ps[:, b, :], scalar=gt2[:, fc:fc + 1], in1=xT[:, fc, :], op0=ALU.mult, op1=ALU.add)],
                        reads=[r_bank[b], r_modv], writes=[r_xT[fc]])
                release_slab()
            mix_done[0] += 1

        def tile_gen(i):
            s = i % 2
            xT = xTs[s]; r_xT = r_xTs[s]; rstd = rstds[s]; r_rstd = r_rstds[s]
            first = (i == 0)
            for fc in range(8):
                if fc % 2 == 0:
                    yield 'e'
                b = next_bank()

                def fn(h, fc=fc, b=b):
                    return [h.transpose(out=ps[:, b, blk * 128:(blk + 1) * 128], in_=xin[:, blk, fc * 128:(fc + 1) * 128],
                                        identity=ident[:]) for blk in range(4)]
                E(PE, fn, reads=[r_xin, r_const], writes=[r_bank[b]])
                E(DVE, lambda h, fc=fc, b=b: [h.tensor_copy(out=xT[:, fc, :], in_=ps[:, b, :])],
                  reads=[r_bank[b]], writes=[r_xT[fc]])
            if i + 1 < n_tiles and not st["dry"]:
                load_x(i + 1)
            if first:
                yield from mod_part(0, 8)
                gs_op(gs1, 8, V_G1)
            yield 'S'
            yield from compute_xsq_and_stats(s)
            yield from norm_apply(s, gs1, sh1)

            def hook1():
                yield from mod_part(8, 12)
                half_op(gth1, 16)
            yield from ffn(s, "w1i", b1i, "w1o", b1o, gth1, hook=hook1 if first else None)
            if first:
                yield from mod_part(12, 20)
                gs_op(gs2, 32, V_GM)
            yield from compute_xsq_and_stats(s)
            yield from norm_apply(s, gs2, sh2)
            yield from mixer(s, hook=(lambda: mod_part(20, 24)) if first else None)
            if first:
                yield from mod_part(24, 32)
                gs_op(gs3, 56, V_G2)
            yield from compute_xsq_and_stats(s)
            yield from norm_apply(s, gs3, sh3)

            def hook2():
                yield from mod_part(32, 36)
                half_op(gth3, 64)
            yield from ffn(s, "w2i", b2i, "w2o", b2o, gth3, hook=hook2 if first else None)
            if first:
                yield from mod_part(36, 44)
                gs_op(gsf, 80, V_GF)
            yield from compute_xsq_and_stats(s)
            def out_block(fc, qo):
                b = next_bank()

                def fn(h, qo=qo, b=b):
                    return [h.transpose(out=ps[:, b, blk * 128:(blk + 1) * 128], in_=oT[qo][:, blk * 128:(blk + 1) * 128],
                                        identity=ident[:]) for blk in range(4)]
                E(PE, fn, reads=[r_oT[qo], r_const], writes=[r_bank[b]])
                k = nxt("oo", 3)
                if fc % 2 == 0:
                    E(DVE, lambda h, k=k, b=b: [h.tensor_copy(out=oo[k][:], in_=ps[:, b, :].rearrange("p (k n) -> p k n", k=4))],
                      reads=[r_bank[b]], writes=[r_oo[k]])
                else:
                    E(ACT, lambda h, k=k, b=b: [h.activation(out=oo[k][:], in_=ps[:, b, :].rearrange("p (k n) -> p k n", k=4), func=AF.Copy)],
                      reads=[r_bank[b]], writes=[r_oo[k]])
                if not st["dry"]:
                    dst = y_d[i * T:(i + 1) * T, fc * 128:(fc + 1) * 128].rearrange("(b p) d -> p b d", p=128)
                    dma1(PL, dst, oo[k][:], [r_oo[k]], [], sem_oo[k])

            prev = None
            for fc in range(8):
                yield 'e'
                q = nxt("ntmp", 2)
                qo = nxt("oT", 3)
                E(DVE, lambda h, q=q, fc=fc: [h.scalar_tensor_tensor(
                    out=ntmp[q][:], in0=xT[:, fc, :], scalar=gsf[:, fc:fc + 1], in1=rstd[:], op0=ALU.mult, op1=ALU.mult)],
                    reads=[r_xT[fc], r_rstd, r_derived], writes=[r_ntmp[q]])
                E(ACT, lambda h, q=q, qo=qo, fc=fc: [h.activation(out=oT[qo][:], in_=ntmp[q][:], func=AF.Identity,
                                                                 bias=shf[:, fc:fc + 1], scale=1.0)],
                  reads=[r_ntmp[q], r_modv], writes=[r_oT[qo]])
                if prev is not None:
                    out_block(*prev)
                prev = (fc, qo)
            yield 'e'
            out_block(*prev)

        def program():
            gens = []
            nxt_tile = [0]

            def step(e):
                cur_tile[0] = e[3]
                e[1] = next(e[0], None)
                while e[1] == 'S':
                    e[2] = True
                    e[1] = next(e[0], None)

            def maybe_spawn():
                while len(gens) < 2 and nxt_tile[0] < n_tiles and (not gens or gens[-1][2]):
                    e = [tile_gen(nxt_tile[0]), None, False, nxt_tile[0]]
                    nxt_tile[0] += 1
                    step(e)
                    gens.append(e)

            def can_own(e):
                return e[1] == 'M' or (e[1] == 'MIX' and mix_done[0] == e[3])

            mix_done[0] = 0
            owner = None
            maybe_spawn()
            while gens:
                if owner is not None and owner[1] in ('M', 'MIX') and owner in gens:
                    step(owner)
                    for o in gens:
                        if o is not owner:
                            for _ in range(2):
                                if o[1] == 'e':
                                    step(o)
                else:
                    owner = None
                    for e in gens:
                        if can_own(e):
                            owner = e
                            break
                    if owner is None:
                        progressed = False
                        for e in gens:
                            if e[1] == 'e':
                                step(e)
                                progressed = True
                        assert progressed, [(e[1], e[3]) for e in gens]
                gens[:] = [e for e in gens if e[1] is not None]
                maybe_spawn()

        program()
        st["dry"] = False
        bank_ctr[0] = 0
        rot.clear()
        held.clear()
        for _ in range(NS):
            issue_slab()
        program()

        final_vals = [(sm.h, sm.val) for sm in sem_oo]
        assert st["used"] == len(plan), (st, len(plan))

        with nc.Block() as block:
            @block.sync
            def _(h):
                replay(SP, h)

            @block.gpsimd
            def _(h):
                replay(PL, h)
                for smh, v in final_vals:
                    if v:
                        h.wait_ge(smh, v)

            @block.tensor
            def _(h):
                replay(PE, h)

            @block.scalar
            def _(h):
                replay(ACT, h)

            @block.vector
            def _(h):
                replay(DVE, h)
    return nc


def prep_inputs(inputs, n_tiles=SEQ // T, cores=N_CORES):
    f = lambda a: np.ascontiguousarray(np.asarray(a, dtype=np.float32))
    col = lambda v, n: f(v).reshape(n, 128).T
    x = f(inputs["x"]); c = f(inputs["c"])
    shared = np.zeros((128, NV), np.float32)
    shared[:, V_G1:V_G1 + 8] = col(inputs["g_ffn1"][0], 8)
    shared[:, V_GM:V_GM + 8] = col(inputs["g_mix"][0], 8)
    shared[:, V_G2:V_G2 + 8] = col(inputs["g_ffn2"][0], 8)
    shared[:, V_GF:V_GF + 8] = col(inputs["g_final"], 8)
    shared[:, V_BMOD:V_BMOD + 72] = col(inputs["b_mod"][0], 72)
    shared[:, V_BMOD + 72:V_BMOD + 88] = col(inputs["b_fmod"], 16)
    shared[:, V_CONVB:V_CONVB + 4] = col(inputs["conv_b"][0], 4)
    shared[:, V_LNG:V_LNG + 4] = col(inputs["ln_g"][0], 4)
    shared[:, V_LNB:V_LNB + 4] = col(inputs["ln_b"][0], 4)
    shared[:, V_RCB:V_RCB + 4] = col(inputs["rnn_conv_b"][0], 4)
    shared[:, V_BA:V_BA + 4] = col(inputs["b_a"][0], 4)
    shared[:, V_BI:V_BI + 4] = col(inputs["b_i"][0], 4)
    shared[:, V_LAM:V_LAM + 4] = col(inputs["lru_lambda"][0], 4)
    rcw = f(inputs["rnn_conv_w"][0])
    for cc in range(4):
        for k in range(4):
            shared[:, V_RCW + 4 * cc + k] = rcw[k, cc * 128:(cc + 1) * 128]
    cw = f(inputs["conv_w"][0])
    dcw = np.zeros((4, 128, NCONV, 128), np.float32)
    ar = np.arange(128)
    for cc in range(4):
        for k in range(NCONV):
            dcw[cc, ar, k, ar] = cw[k, cc * 128:(cc + 1) * 128]
    dcw = dcw.reshape(512, NCONV * 128)
    def bd(w):
        w = f(w)
        o = np.zeros((128, 4, 128), np.float32)
        for cc in range(4):
            o[0:64, cc, 0:64] = w[2 * cc]
            o[64:128, cc, 64:128] = w[2 * cc + 1]
        return o.reshape(128, 512)
    wabd = bd(inputs["w_a"][0]); wibd = bd(inputs["w_i"][0])
    common = {
        "ident": np.eye(128, dtype=np.float32), "wabd": wabd, "wibd": wibd,
        "w_mod": f(inputs["w_mod"][0]), "w_fmod": f(inputs["w_fmod"]),
        "w1i": f(inputs["w_ffn1_in"][0]), "w1o": f(inputs["w_ffn1_out"][0]),
        "win": f(inputs["w_in"][0]), "wout": f(inputs["w_out"][0]),
        "w2i": f(inputs["w_ffn2_in"][0]), "w2o": f(inputs["w_ffn2_out"][0]),
        "dcw": dcw,
    }
    in_maps = []
    S = n_tiles * T
    for b in range(cores):
        v = shared.copy()
        v[:, V_C:V_C + 8] = c[b].reshape(8, 128).T
        m = dict(common)
        m["vecs"] = v
        m["x"] = np.ascontiguousarray(x[b, :S, :])
        in_maps.append(m)
    return in_maps


_NC_CACHE = {}


def kernel(**inputs):
    n_tiles = SEQ // T
    if n_tiles not in _NC_CACHE:
        _NC_CACHE[n_tiles] = build_program(n_tiles)
    nc = _NC_CACHE[n_tiles]
    in_maps = prep_inputs(inputs, n_tiles, N_CORES)
    res = run_bass_kernel_spmd(nc, in_maps, core_ids=list(range(N_CORES)))
    out = np.stack([np.asarray(r["y"], dtype=np.float32) for r in res.results], axis=0)
    return out
```

```python
import numpy as np
import concourse.bass as bass
import concourse.mybir as mybir
from concourse.bass_utils import run_bass_kernel_spmd

F32 = mybir.dt.float32
BF16 = mybir.dt.bfloat16
AF = mybir.ActivationFunctionType
ALU = mybir.AluOpType

D = 1024
DFF = 2816
SEQ = 4096
T = 512
KC = 8
FC = 22
NCONV = 31
EPS = 1e-6
NS = 3
N_CORES = 8

V_G1, V_GM, V_G2, V_GF = 0, 8, 16, 24
V_BMOD = 32
V_CONVB = 120
V_LNG = 124
V_LNB = 128
V_RCB = 132
V_BA = 136
V_BI = 140
V_LAM = 144
V_RCW = 148
V_C = 164
NV = 172


class Sem:
    def __init__(self, h):
        self.h = h
        self.val = 0


class Res:
    __slots__ = ("w", "r")

    def __init__(self):
        self.w = None
        self.r = {}


class Eng:
    def __init__(self, name, sem, self_sync):
        self.name = name
        self.sem = sem
        self.prog = []
        self.seen = {}
        self.self_sync = self_sync


def emit(eng, fn, reads=(), writes=(), dma=None, ndma=1):
    waits = {}

    def need(tok):
        if tok is None:
            return
        s, v = tok
        if waits.get(s, 0) < v:
            waits[s] = v

    for r in reads:
        need(r.w)
    for w in writes:
        need(w.w)
        for s, v in w.r.items():
            need((s, v))
    wl = []
    for s, v in waits.items():
        if s is eng.sem and not eng.self_sync:
            continue
        if eng.seen.get(s, 0) >= v:
            continue
        eng.seen[s] = v
        wl.append((s, v))
    if dma is None:
        eng.sem.val += 1
        tok = (eng.sem, eng.sem.val)
        eng.prog.append((wl, fn, eng.sem, 1, False))
    else:
        dma.val += 16 * ndma
        tok = (dma, dma.val)
        eng.prog.append((wl, fn, dma, 16, True))
    for r in reads:
        if r.r.get(tok[0], 0) < tok[1]:
            r.r[tok[0]] = tok[1]
    for w in writes:
        w.w = tok
        w.r = {}
    return tok


def replay(eng, h):
    for wl, fn, sem, inc, is_dma in eng.prog:
        for s, v in wl:
            h.wait_ge(s.h, v)
        ins = fn(h)
        if is_dma:
            for i in ins:
                i.then_inc(sem.h, 16)
        else:
            ins[-1].then_inc(sem.h, 1)


def build_program(n_tiles=SEQ // T):
    S = n_tiles * T
    nc = bass.Bass("TRN2", target_bir_lowering=False)
    dt_in = lambda name, shape: nc.dram_tensor(name, shape, F32, kind="ExternalInput").ap()
    x_d = dt_in("x", [S, D])
    vecs_d = dt_in("vecs", [128, NV])
    ident_d = dt_in("ident", [128, 128])
    wabd_d = dt_in("wabd", [128, 512])
    wibd_d = dt_in("wibd", [128, 512])
    wmod_d = dt_in("w_mod", [D, 9 * D])
    wfmod_d = dt_in("w_fmod", [D, 2 * D])
    w1i_d = dt_in("w1i", [D, 2 * DFF])
    w1o_d = dt_in("w1o", [DFF, D])
    win_d = dt_in("win", [D, 2 * D])
    wout_d = dt_in("wout", [D, D])
    w2i_d = dt_in("w2i", [D, 2 * DFF])
    w2o_d = dt_in("w2o", [DFF, D])
    dcw_d = dt_in("dcw", [512, NCONV * 128])
    y_d = nc.dram_tensor("y", [S, D], F32, kind="ExternalOutput").ap()
    dt_sc = lambda name, shape: nc.dram_tensor(name, shape, BF16, kind="Internal").ap()
    b1i = dt_sc("b1i", [D, 2 * DFF])
    b1o = dt_sc("b1o", [DFF, D])
    bin_ = dt_sc("bin", [D, 2 * D])
    bout = dt_sc("bout", [D, D])
    b2i = dt_sc("b2i", [D, 2 * DFF])
    b2o = dt_sc("b2o", [DFF, D])
    bdc = dt_sc("bdc", [512, NCONV * 128])

    from contextlib import ExitStack
    es = ExitStack()
    with es:
        sb = lambda name, shape, dt=F32: es.enter_context(nc.sbuf_tensor("sb_" + name, shape, dt))
        newsem = lambda name: Sem(es.enter_context(nc.semaphore(name)))

        PE = Eng("pe", newsem("s_pe"), False)
        ACT = Eng("act", newsem("s_act"), True)
        DVE = Eng("dve", newsem("s_dve"), True)
        SP = Eng("sp", newsem("s_sp"), False)
        PL = Eng("pool", newsem("s_pool"), False)

        ident = sb("ident", [128, 128])
        ones_bf = sb("ones_bf", [128, 128], BF16)
        vecs = sb("vecs", [128, NV])
        modv = sb("modv", [128, 88])
        cact = sb("cact", [128, 8])
        gs1 = sb("gs1", [128, 8]); gth1 = sb("gth1", [128, 8])
        gs2 = sb("gs2", [128, 8])
        gs3 = sb("gs3", [128, 8]); gth3 = sb("gth3", [128, 8])
        gsf = sb("gsf", [128, 8])
        lam8 = sb("lam8", [128, 4]); lam16 = sb("lam16", [128, 4]); lamt = sb("lamt", [128, 4])
        wabd = sb("wabd", [128, 4, 128], BF16)
        wibd = sb("wibd", [128, 4, 128], BF16)
        hstate = sb("hstate", [128, 4])
        slots = [sb(f"slab{i}", [128, 8, 512], BF16) for i in range(NS)]
        xin = sb("xin", [128, 4, D])
        xTs = [sb(f"xT{k}", [128, 8, T]) for k in range(2)]
        xsq = [sb(f"xsq{i}", [128, T], BF16) for i in range(4)]
        hbs = [sb(f"hb{k}", [128, 8, T], BF16) for k in range(2)]
        hid = sb("hid", [128, FC, T], BF16)
        sgb = [sb(f"sgb{i}", [128, T]) for i in range(2)]
        ntmp = [sb(f"ntmp{i}", [128, T]) for i in range(2)]
        rstds = [sb(f"rstd{k}", [128, T]) for k in range(2)]
        ub = sb("ub", [128, 4, T + 30], BF16)
        ux = sb("ux", [128, 4, T + 3])
        gy = sb("gy", [128, 4, T], BF16)
        cv = sb("cv", [128, 4, T])
        cvsq = [sb(f"cvsq{i}", [128, T], BF16) for i in range(2)]
        cvb = [sb(f"cvb{i}", [128, T], BF16) for i in range(2)]
        meanb = sb("meanb", [128, T])
        varb = sb("varb", [128, T])
        xr = [sb(f"xr{i}", [128, T]) for i in range(4)]
        xrb = [sb(f"xrb{i}", [128, T], BF16) for i in range(4)]
        abuf = [sb(f"abuf{i}", [128, T]) for i in range(2)]
        mbuf = [sb(f"mbuf{i}", [128, T]) for i in range(2)]
        btb = [sb(f"btb{i}", [128, T]) for i in range(2)]
        hs = [sb(f"hs{i}", [128, T]) for i in range(4)]
        oT = [sb(f"oT{i}", [128, T]) for i in range(3)]
        oo = [sb(f"oo{k}", [128, 4, 128]) for k in range(3)]
        ps = es.enter_context(nc.psum_tensor("ps", [128, 8, T], F32))

        R = lambda: Res()
        r_const = R()
        r_vecs = R(); r_modv = R(); r_cact = R(); r_derived = R(); r_hstate = [R() for _ in range(4)]
        r_gates = R()
        r_slot = [R() for _ in range(NS)]
        sem_slot = [newsem(f"s_slot{i}") for i in range(NS)]
        r_bank = [R() for _ in range(8)]
        r_xin = R(); sem_xin = newsem("s_xin")
        r_xTs = [[R() for _ in range(8)] for _ in range(2)]
        r_xsq = [R() for _ in range(4)]
        r_hbs = [[R() for _ in range(8)] for _ in range(2)]
        r_hid = [R() for _ in range(FC)]
        r_sgb = [R() for _ in range(2)]
        r_ntmp = [R() for _ in range(2)]
        r_rstds = [R(), R()]
        r_ub = [R() for _ in range(4)]
        r_ux = [R() for _ in range(4)]
        r_gy = [R() for _ in range(4)]
        r_cv = [R() for _ in range(4)]
        r_cvsq = [R() for _ in range(2)]
        r_cvb = [R() for _ in range(2)]
        r_meanb = R(); r_varb = R()
        r_xr = [R() for _ in range(4)]; r_xrb = [R() for _ in range(4)]
        r_abuf = [R() for _ in range(2)]
        r_mbuf = [R() for _ in range(2)]; r_btb = [R() for _ in range(2)]
        r_hs = [R() for _ in range(4)]
        r_oT = [R() for _ in range(3)]
        r_oo = [R() for _ in range(3)]; sem_oo = [newsem(f"s_oo{k}") for k in range(3)]
        sem_const = newsem("s_const")
        r_scr = {}
        sem_wb = [newsem(f"s_wb{i}") for i in range(NS)]
        sem_slot_pl = [newsem(f"s_slotp{i}") for i in range(NS)]

        bank_ctr = [0]

        held = set()

        def next_bank(hold=False):
            while True:
                b = bank_ctr[0] % 8
                bank_ctr[0] += 1
                if b not in held:
                    break
            if hold:
                held.add(b)
            return b

        rot = {}

        def nxt(name, n):
            v = rot.get(name, 0)
            rot[name] = v + 1
            return v % n

        def dma1(eng, out_ap, in_ap, reads, writes, sem):
            return emit(eng, lambda h: [h.dma_start(out=out_ap, in_=in_ap)], reads=reads, writes=writes, dma=sem)

        dma1(SP, vecs[:], vecs_d[:, :], [], [r_vecs], sem_const)
        dma1(SP, ident[:], ident_d[:, :], [], [r_const], newsem("s_ident"))
        def load_x(i):
            src = x_d[i * T:(i + 1) * T, :].rearrange("(b p) d -> p b d", p=128)
            dma1(PL, xin[:], src, [], [r_xin], sem_xin)
        load_x(0)
        sem_g = newsem("s_gates")
        dma1(PL, wabd[:], wabd_d[:, :].rearrange("p (c n) -> p c n", c=4), [], [r_gates], sem_g)
        dma1(PL, wibd[:], wibd_d[:, :].rearrange("p (c n) -> p c n", c=4), [], [r_gates], sem_g)

        emit(DVE, lambda h: [h.memset(ones_bf[:], 1.0)], writes=[r_const])
        emit(DVE, lambda h: [h.memset(hstate[:], 0.0)], writes=r_hstate)
        emit(DVE, lambda h: [h.memset(ub[:, :, 0:30], 0.0)], writes=r_ub)
        emit(DVE, lambda h: [h.memset(ux[:, :, 0:3], 0.0)], writes=r_ux)
        emit(ACT, lambda h: [h.activation(out=cact[:], in_=vecs[:, V_C:V_C + 8], func=AF.Silu)],
             reads=[r_vecs], writes=[r_cact])
        emit(ACT, lambda h: [h.activation(out=lamt[:], in_=vecs[:, V_LAM:V_LAM + 4], func=AF.Exp, scale=-1.0)],
             reads=[r_vecs], writes=[r_derived])
        emit(ACT, lambda h: [h.activation(out=lamt[:], in_=lamt[:], func=AF.Ln, bias=1.0)],
             reads=[r_derived], writes=[r_derived])
        emit(DVE, lambda h: [h.tensor_scalar(out=lam8[:], in0=lamt[:], scalar1=-8.0, scalar2=None, op0=ALU.mult)],
             reads=[r_derived], writes=[r_derived])
        emit(DVE, lambda h: [h.tensor_scalar(out=lam16[:], in0=lamt[:], scalar1=-16.0, scalar2=None, op0=ALU.mult)],
             reads=[r_derived], writes=[r_derived])

        kview = lambda ap: ap.rearrange("(kc p) n -> p kc n", p=128)
        plan = []
        st = {"issued": 0, "used": 0, "dry": True}

        cur_tile = [0]
        pos_ctr = {}

        def issue_slab():
            k = st["issued"]
            if k >= len(plan):
                return
            st["issued"] += 1
            si = k % NS
            cname, pfn, tile, pos = plan[k]

            def mk(pieces):
                return lambda h: [h.dma_start(out=o, in_=i) for o, i in pieces]
            if cname is None:
                pieces = pfn(si, 0)
                emit(SP, mk(pieces), writes=[r_slot[si]], dma=sem_slot[si], ndma=len(pieces))
            elif tile == 0:
                pieces = pfn(si, 1)
                emit(PL, mk(pieces), writes=[r_slot[si]], dma=sem_slot_pl[si], ndma=len(pieces))
                back = [(d, o) for o, d in pfn(si, 0)]
                r_scr[pos] = Res()
                emit(SP, mk(back), reads=[r_slot[si]], writes=[r_scr[pos]], dma=sem_wb[si], ndma=len(back))
            else:
                pieces = pfn(si, 0)
                emit(SP, mk(pieces), reads=[r_scr[pos]], writes=[r_slot[si]], dma=sem_slot[si], ndma=len(pieces))

        def next_slab(cname, pfn):
            if st["dry"]:
                t = cur_tile[0]
                pos = None
                if cname is not None:
                    pos = pos_ctr.get(t, 0)
                    pos_ctr[t] = pos + 1
                plan.append((cname, pfn, t, pos))
                return 0
            k = st["used"]
            st["used"] += 1
            assert k < st["issued"] and plan[k][0] == cname
            return k % NS

        def release_slab():
            if not st["dry"]:
                issue_slab()

        def E(eng, fn, reads=(), writes=()):
            if st["dry"]:
                return None
            return emit(eng, fn, reads=reads, writes=writes)

        def d_mod(g):
            src = wmod_d if g < 36 else wfmod_d
            gg = g if g < 36 else g - 36
            return (None, lambda si, f: [(slots[si][:].bitcast(F32), kview(src[:, gg * 256:(gg + 1) * 256]))])

        WT = {"w1i": (b1i, w1i_d), "w1o": (b1o, w1o_d), "win": (bin_, win_d), "wout": (bout, wout_d),
              "w2i": (b2i, w2i_d), "w2o": (b2o, w2o_d), "dcw": (bdc, dcw_d)}

        def d_ffn_in(nm, wi_unused, j):
            return (nm, lambda si, f: [
                (slots[si][:, :, 0:256], kview(WT[nm][f][:, 256 * j:256 * j + 256])),
                (slots[si][:, :, 256:512], kview(WT[nm][f][:, DFF + 256 * j:DFF + 256 * j + 256]))])

        def d_ffn_out(nm, wo_unused, ch, k0, nk):
            return (nm, lambda si, f: [(slots[si][:, 0:nk, :], kview(WT[nm][f][k0 * 128:(k0 + nk) * 128, ch * 512:(ch + 1) * 512]))])

        def d_cols(nm, w_unused, c0):
            return (nm, lambda si, f: [(slots[si][:, :, :], kview(WT[nm][f][:, c0:c0 + 512]))])

        def d_glu(half):
            return ("win", lambda si, f: [
                (slots[si][:, :, 0:256], kview(WT["win"][f][:, 256 * half:256 * half + 256])),
                (slots[si][:, :, 256:512], kview(WT["win"][f][:, 512 + 256 * half:512 + 256 * half + 256]))])

        def d_conv(c):
            return ("dcw", lambda si, f: [
                (slots[si][:].rearrange("p k n -> p (k n)")[:, 0:NCONV * 128], WT["dcw"][f][c * 128:(c + 1) * 128, :])])

        def mm_group(bank, pairs, reads, first=True, last=True):
            n = len(pairs)

            def fn(h):
                out = []
                for idx, (l, r) in enumerate(pairs):
                    out.append(h.matmul(ps[:, bank, :], lhsT=l, rhs=r,
                                        start=(first and idx == 0), stop=(last and idx == n - 1)))
                return out
            return E(PE, fn, reads=reads, writes=[r_bank[bank]])

        sh1 = modv[:, 0:8]; sh2 = modv[:, 24:32]; gt2 = modv[:, 40:48]; sh3 = modv[:, 48:56]; shf = modv[:, 72:80]

        def mod_part(g0, g1):
            bank = next_bank(hold=True)
            for g in range(g0, g1):
                si = next_slab(*d_mod(g))
                sf = slots[si][:].bitcast(F32)

                def fn(h, g=g, sf=sf):
                    out = []
                    for sub in range(2):
                        j = 2 * g + sub
                        for kc in range(8):
                            out.append(h.matmul(ps[:, bank, j:j + 1], lhsT=sf[:, kc, sub * 128:(sub + 1) * 128],
                                                rhs=cact[:, kc:kc + 1], start=(kc == 0), stop=(kc == 7)))
                    return out
                E(PE, fn, reads=[r_slot[si], r_cact], writes=[r_bank[bank]])
                release_slab()
                yield 'M'
            c0, c1 = 2 * g0, 2 * g1
            E(DVE, lambda h: [h.tensor_tensor(out=modv[:, c0:c1], in0=ps[:, bank, c0:c1], in1=vecs[:, V_BMOD + c0:V_BMOD + c1], op=ALU.add)],
              reads=[r_bank[bank], r_vecs], writes=[r_modv])
            held.discard(bank)

        def gs_op(dst, sc_col, g_col):
            E(DVE, lambda h: [h.scalar_tensor_tensor(out=dst[:], in0=modv[:, sc_col:sc_col + 8], scalar=1.0,
                                                     in1=vecs[:, g_col:g_col + 8], op0=ALU.add, op1=ALU.mult)],
              reads=[r_modv, r_vecs], writes=[r_derived])

        def half_op(dst, col):
            E(DVE, lambda h: [h.tensor_scalar(out=dst[:], in0=modv[:, col:col + 8], scalar1=0.5, scalar2=None, op0=ALU.mult)],
              reads=[r_modv], writes=[r_derived])

        def rsqrt_chain(buf, res, bank, scale):
            E(ACT, lambda h: [h.activation(out=buf[:], in_=ps[:, bank, :], func=AF.Sqrt, bias=EPS, scale=scale)],
              reads=[r_bank[bank]], writes=[res])
            E(DVE, lambda h: [h.reciprocal(out=buf[:], in_=buf[:])], reads=[res], writes=[res])

        def compute_xsq_and_stats(s):
            xT = xTs[s]; r_xT = r_xTs[s]; rstd = rstds[s]; r_rstd = r_rstds[s]
            yield 'e'
            sbank = next_bank(hold=True)
            pend = []

            def flush():
                for (q, fc) in pend:
                    mm_group(sbank, [(ones_bf[:], xsq[q][:])], [r_xsq[q], r_const], first=(fc == 0), last=(fc == 7))
                del pend[:]
            for fc in range(8):
                if fc % 2 == 0:
                    flush()
                q = nxt("xsq", 4)
                E(ACT, lambda h, q=q, fc=fc: [h.activation(out=xsq[q][:], in_=xT[:, fc, :], func=AF.Square)],
                  reads=[r_xT[fc]], writes=[r_xsq[q]])
                pend.append((q, fc))
                if fc % 2 == 1:
                    yield 'e'
            flush()
            rsqrt_chain(rstd, r_rstd, sbank, 1.0 / D)
            held.discard(sbank)

        def norm_apply(s, gs, sh):
            xT = xTs[s]; r_xT = r_xTs[s]; rstd = rstds[s]; r_rstd = r_rstds[s]; hb = hbs[s]; r_hb = r_hbs[s]
            for fc in range(8):
                if fc % 2 == 0:
                    yield 'e'
                q = nxt("ntmp", 2)
                E(DVE, lambda h, q=q, fc=fc: [h.scalar_tensor_tensor(
                    out=ntmp[q][:], in0=xT[:, fc, :], scalar=gs[:, fc:fc + 1], in1=rstd[:], op0=ALU.mult, op1=ALU.mult)],
                    reads=[r_xT[fc], r_rstd, r_derived], writes=[r_ntmp[q]])
                E(ACT, lambda h, q=q, fc=fc: [h.activation(out=hb[:, fc, :], in_=ntmp[q][:], func=AF.Identity,
                                                           bias=sh[:, fc:fc + 1], scale=1.0)],
                  reads=[r_ntmp[q], r_modv], writes=[r_hb[fc]])

        def ffn(s, nm_i, wi, nm_o, wo, gth, hook=None):
            xT = xTs[s]; r_xT = r_xTs[s]; hb = hbs[s]; r_hb = r_hbs[s]
            for j in range(11):
                yield 'M'
                si = next_slab(*d_ffn_in(nm_i, wi, j))
                for sub in range(2):
                    cg = 2 * j + sub
                    bg = next_bank()
                    mm_group(bg, [(slots[si][:, kc, sub * 128:(sub + 1) * 128], hb[:, kc, :]) for kc in range(8)],
                             [r_slot[si]] + r_hb)
                    bu = next_bank()
                    mm_group(bu, [(slots[si][:, kc, 256 + sub * 128:256 + (sub + 1) * 128], hb[:, kc, :]) for kc in range(8)],
                             [r_slot[si]] + r_hb)
                    q = nxt("sgb", 2)
                    E(ACT, lambda h, q=q, bg=bg: [h.activation(out=sgb[q][:], in_=ps[:, bg, :], func=AF.Silu)],
                      reads=[r_bank[bg]], writes=[r_sgb[q]])
                    E(DVE, lambda h, q=q, bu=bu, cg=cg: [h.tensor_tensor(out=hid[:, cg, :], in0=ps[:, bu, :], in1=sgb[q][:], op=ALU.mult)],
                      reads=[r_bank[bu], r_sgb[q]], writes=[r_hid[cg]])
                release_slab()
            if hook is not None:
                yield from hook()
            for ch in range(2):
                yield 'M'
                banks = [next_bank(hold=True) for _ in range(4)]
                for (k0, nk) in ((0, 8), (8, 8), (16, 6)):
                    si = next_slab(*d_ffn_out(nm_o, wo, ch, k0, nk))
                    for oc in range(4):
                        mm_group(banks[oc],
                                 [(slots[si][:, kl, oc * 128:(oc + 1) * 128], hid[:, k0 + kl, :]) for kl in range(nk)],
                                 [r_slot[si]] + r_hid[k0:k0 + nk], first=(k0 == 0), last=(k0 == 16))
                    release_slab()
                for oc in range(4):
                    fc = 4 * ch + oc
                    E(DVE, lambda h, fc=fc, b=banks[oc]: [h.scalar_tensor_tensor(
                        out=xT[:, fc, :], in0=ps[:, b, :], scalar=gth[:, fc:fc + 1], in1=xT[:, fc, :], op0=ALU.mult, op1=ALU.add)],
                        reads=[r_bank[banks[oc]], r_derived, r_modv], writes=[r_xT[fc]])
                for b in banks:
                    held.discard(b)

        mix_done = [0]

        def mixer(s, hook=None):
            xT = xTs[s]; r_xT = r_xTs[s]; hb = hbs[s]; r_hb = r_hbs[s]
            yield 'MIX'
            si = next_slab(*d_cols("win", bin_, 1024))
            for c in range(4):
                b = next_bank()
                mm_group(b, [(slots[si][:, kc, c * 128:(c + 1) * 128], hb[:, kc, :]) for kc in range(8)], [r_slot[si]] + r_hb)
                E(ACT, lambda h, c=c, b=b: [h.activation(out=ux[:, c, 3:3 + T], in_=ps[:, b, :], func=AF.Copy)],
                  reads=[r_bank[b]], writes=[r_ux[c]])
            release_slab()
            for c in range(4):
                E(DVE, lambda h, c=c: [h.tensor_scalar(
                    out=xr[c][:], in0=ux[:, c, 0:T], scalar1=vecs[:, V_RCW + 4 * c:V_RCW + 4 * c + 1],
                    scalar2=vecs[:, V_RCB + c:V_RCB + c + 1], op0=ALU.mult, op1=ALU.add)],
                    reads=[r_ux[c], r_vecs], writes=[r_xr[c]])
                for k in range(1, 4):
                    E(DVE, lambda h, c=c, k=k: [h.scalar_tensor_tensor(
                        out=xr[c][:], in0=ux[:, c, k:k + T], scalar=vecs[:, V_RCW + 4 * c + k:V_RCW + 4 * c + k + 1],
                        in1=xr[c][:], op0=ALU.mult, op1=ALU.add)],
                        reads=[r_ux[c], r_vecs, r_xr[c]], writes=[r_xr[c]])
                E(DVE, lambda h, c=c: [h.tensor_copy(out=ux[:, c, 0:3], in_=ux[:, c, T:T + 3])], reads=[r_ux[c]], writes=[r_ux[c]])
                E(DVE, lambda h, c=c: [h.tensor_copy(out=xrb[c][:], in_=xr[c][:])],
                  reads=[r_xr[c]], writes=[r_xrb[c]])
            for half in range(2):
                yield 'M'
                si = next_slab(*d_glu(half))
                for sub in range(2):
                    c = 2 * half + sub
                    bv = next_bank()
                    mm_group(bv, [(slots[si][:, kc, sub * 128:(sub + 1) * 128], hb[:, kc, :]) for kc in range(8)], [r_slot[si]] + r_hb)
                    bg = next_bank()
                    mm_group(bg, [(slots[si][:, kc, 256 + sub * 128:256 + (sub + 1) * 128], hb[:, kc, :]) for kc in range(8)], [r_slot[si]] + r_hb)
                    q = nxt("sgb", 2)
                    E(ACT, lambda h, q=q, bg=bg: [h.activation(out=sgb[q][:], in_=ps[:, bg, :], func=AF.Sigmoid)],
                      reads=[r_bank[bg]], writes=[r_sgb[q]])
                    E(DVE, lambda h, q=q, bv=bv, c=c: [h.tensor_tensor(out=ub[:, c, 30:30 + T], in0=ps[:, bv, :], in1=sgb[q][:], op=ALU.mult)],
                      reads=[r_bank[bv], r_sgb[q]], writes=[r_ub[c]])
                release_slab()

            hs_of = {}

            def rnn_pair_a(c0):
                cs = (c0, c0 + 1)
                qx = {c: c for c in cs}
                br = {}; bi = {}
                for c in cs:
                    br[c] = next_bank()
                    mm_group(br[c], [(wabd[:, c, :], xrb[c][:])], [r_gates, r_xrb[c]])
                    bi[c] = next_bank()
                    mm_group(bi[c], [(wibd[:, c, :], xrb[c][:])], [r_gates, r_xrb[c]])
                qx = {c: c % 2 for c in cs}
                for c in cs:
                    q = qx[c]
                    E(ACT, lambda h, c=c, q=q: [h.activation(out=abuf[q][:], in_=ps[:, br[c], :], func=AF.Sigmoid,
                                                           bias=vecs[:, V_BA + c:V_BA + c + 1], scale=1.0)],
                      reads=[r_bank[br[c]], r_vecs], writes=[r_abuf[q]])
                    E(ACT, lambda h, c=c, q=q: [h.activation(out=btb[q][:], in_=ps[:, bi[c], :], func=AF.Sigmoid,
                                                           bias=vecs[:, V_BI + c:V_BI + c + 1], scale=1.0)],
                      reads=[r_bank[bi[c]], r_vecs], writes=[r_btb[q]])
                for c in cs:
                    q = qx[c]
                    E(ACT, lambda h, c=c, q=q: [h.activation(out=mbuf[q][:], in_=abuf[q][:], func=AF.Exp, scale=lam16[:, c:c + 1])],
                      reads=[r_abuf[q], r_derived], writes=[r_mbuf[q]])
                    E(ACT, lambda h, c=c, q=q: [h.activation(out=abuf[q][:], in_=abuf[q][:], func=AF.Exp, scale=lam8[:, c:c + 1])],
                      reads=[r_abuf[q], r_derived], writes=[r_abuf[q]])
                for c in cs:
                    q = qx[c]
                    E(DVE, lambda h, q=q: [h.tensor_scalar(out=mbuf[q][:], in0=mbuf[q][:], scalar1=1.0, scalar2=-1.0, op0=ALU.min, op1=ALU.mult)],
                      reads=[r_mbuf[q]], writes=[r_mbuf[q]])
                    E(DVE, lambda h, q=q, c=c: [h.tensor_tensor(out=btb[q][:], in0=btb[q][:], in1=xr[c][:], op=ALU.mult)],
                      reads=[r_btb[q], r_xr[c]], writes=[r_btb[q]])
                for c in cs:
                    q = qx[c]
                    E(ACT, lambda h, q=q: [h.activation(out=mbuf[q][:], in_=mbuf[q][:], func=AF.Sqrt, bias=1.0, scale=1.0)],
                      reads=[r_mbuf[q]], writes=[r_mbuf[q]])
                for c in cs:
                    q = qx[c]
                    E(DVE, lambda h, q=q: [h.tensor_tensor(out=btb[q][:], in0=btb[q][:], in1=mbuf[q][:], op=ALU.mult)],
                      reads=[r_btb[q], r_mbuf[q]], writes=[r_btb[q]])
                    E(DVE, lambda h, c=c, q=q: [h.tensor_tensor_scan(out=hs[c][:], data0=abuf[q][:], data1=btb[q][:],
                                                                     initial=hstate[:, c:c + 1], op0=ALU.mult, op1=ALU.add)],
                      reads=[r_abuf[q], r_btb[q], r_hstate[c]], writes=[r_hs[c]])
                    E(DVE, lambda h, c=c: [h.tensor_copy(out=hstate[:, c:c + 1], in_=hs[c][:, T - 1:T])],
                      reads=[r_hs[c]], writes=[r_hstate[c]])

            def rnn_b(c):
                E(DVE, lambda h, c=c: [h.tensor_tensor(out=hb[:, 4 + c, :], in0=hs[c][:], in1=gy[:, c, :], op=ALU.mult)],
                  reads=[r_hs[c], r_gy[c]], writes=[r_hb[4 + c]])

            def conv_chunk(c):
                si = next_slab(*d_conv(c))
                dflat = slots[si][:].rearrange("p k n -> p (k n)")
                b = next_bank()
                mm_group(b, [(dflat[:, k * 128:(k + 1) * 128], ub[:, c, k:k + T]) for k in range(NCONV)], [r_slot[si], r_ub[c]])
                release_slab()
                E(ACT, lambda h: [h.activation(out=cv[:, c, :], in_=ps[:, b, :], func=AF.Identity,
                                               bias=vecs[:, V_CONVB + c:V_CONVB + c + 1], scale=1.0)],
                  reads=[r_bank[b], r_vecs], writes=[r_cv[c]])
                E(DVE, lambda h: [h.tensor_copy(out=ub[:, c, 0:30], in_=ub[:, c, T:T + 30])], reads=[r_ub[c]], writes=[r_ub[c]])
                q = nxt("cvb", 2)
                E(DVE, lambda h, q=q: [h.tensor_copy(out=cvb[q][:], in_=cv[:, c, :])], reads=[r_cv[c]], writes=[r_cvb[q]])
                E(ACT, lambda h, q=q: [h.activation(out=cvsq[q][:], in_=cv[:, c, :], func=AF.Square)], reads=[r_cv[c]], writes=[r_cvsq[q]])
                mm_group(ln_banks[0], [(ones_bf[:], cvb[q][:])], [r_cvb[q], r_const], first=(c == 0), last=(c == 3))
                mm_group(ln_banks[1], [(ones_bf[:], cvsq[q][:])], [r_cvsq[q], r_const], first=(c == 0), last=(c == 3))

            yield 'M'
            rnn_pair_a(0)
            ln_banks = [next_bank(hold=True), next_bank(hold=True)]
            for c in range(4):
                yield 'M'
                if c == 2:
                    rnn_pair_a(2)
                conv_chunk(c)
            bs, bq = ln_banks
            E(ACT, lambda h: [h.activation(out=meanb[:], in_=ps[:, bs, :], func=AF.Copy, scale=1.0 / 512)],
              reads=[r_bank[bs]], writes=[r_meanb])
            E(ACT, lambda h: [h.activation(out=varb[:], in_=ps[:, bs, :], func=AF.Square, scale=1.0 / 512)],
              reads=[r_bank[bs]], writes=[r_varb])
            E(DVE, lambda h: [h.scalar_tensor_tensor(out=varb[:], in0=ps[:, bq, :], scalar=1.0 / 512, in1=varb[:],
                                                     op0=ALU.mult, op1=ALU.subtract)],
              reads=[r_bank[bq], r_varb], writes=[r_varb])
            E(ACT, lambda h: [h.activation(out=varb[:], in_=varb[:], func=AF.Sqrt, bias=EPS, scale=1.0)],
              reads=[r_varb], writes=[r_varb])
            E(DVE, lambda h: [h.reciprocal(out=varb[:], in_=varb[:])], reads=[r_varb], writes=[r_varb])
            held.discard(bs); held.discard(bq)
            yield 'M'
            si = next_slab(*d_cols("win", bin_, 1536))
            for c in range(4):
                b = next_bank()
                mm_group(b, [(slots[si][:, kc, c * 128:(c + 1) * 128], hb[:, kc, :]) for kc in range(8)], [r_slot[si]] + r_hb)
                E(ACT, lambda h, c=c, b=b: [h.activation(out=gy[:, c, :], in_=ps[:, b, :], func=AF.Gelu_apprx_tanh)],
                  reads=[r_bank[b]], writes=[r_gy[c]])
            release_slab()
            if hook is not None:
                yield from hook()
            yield 'e'
            for c in range(4):
                rnn_b(c)
            for c in range(4):
                if c % 2 == 0:
                    yield 'e'
                q = nxt("ntmp", 2)
                E(DVE, lambda h, c=c, q=q: [h.tensor_tensor(out=ntmp[q][:], in0=cv[:, c, :], in1=meanb[:], op=ALU.subtract)],
                  reads=[r_cv[c], r_meanb], writes=[r_ntmp[q]])
                E(DVE, lambda h, q=q: [h.tensor_tensor(out=ntmp[q][:], in0=ntmp[q][:], in1=varb[:], op=ALU.mult)],
                  reads=[r_ntmp[q], r_varb], writes=[r_ntmp[q]])
                E(ACT, lambda h, c=c, q=q: [h.activation(out=hb[:, c, :], in_=ntmp[q][:], func=AF.Silu,
                                                       bias=vecs[:, V_LNB + c:V_LNB + c + 1], scale=vecs[:, V_LNG + c:V_LNG + c + 1])],
                  reads=[r_ntmp[q], r_vecs], writes=[r_hb[c]])
            for ch in range(2):
                yield 'M'
                si = next_slab(*d_cols("wout", bout, ch * 512))
                for oc in range(4):
                    fc = 4 * ch + oc
                    b = next_bank()
                    mm_group(b, [(slots[si][:, kc, oc * 128:(oc + 1) * 128], hb[:, kc, :]) for kc in range(8)], [r_slot[si]] + r_hb)
                    E(DVE, lambda h, fc=fc, b=b: [h.scalar_tensor_tensor(
                        out=xT[:, fc, :], in0=ps[:, b, :], scalar=gt2[:, fc:fc + 1], in1=xT[:, fc, :], op0=ALU.mult, op1=ALU.add)],
                        reads=[r_bank[b], r_modv], writes=[r_xT[fc]])
                release_slab()
            mix_done[0] += 1

        def tile_gen(i):
            s = i % 2
            xT = xTs[s]; r_xT = r_xTs[s]; rstd = rstds[s]; r_rstd = r_rstds[s]
            first = (i == 0)
            for fc in range(8):
                if fc % 2 == 0:
                    yield 'e'
                b = next_bank()

                def fn(h, fc=fc, b=b):
                    return [h.transpose(out=ps[:, b, blk * 128:(blk + 1) * 128], in_=xin[:, blk, fc * 128:(fc + 1) * 128],
                                        identity=ident[:]) for blk in range(4)]
                E(PE, fn, reads=[r_xin, r_const], writes=[r_bank[b]])
                E(DVE, lambda h, fc=fc, b=b: [h.tensor_copy(out=xT[:, fc, :], in_=ps[:, b, :])],
                  reads=[r_bank[b]], writes=[r_xT[fc]])
            if i + 1 < n_tiles and not st["dry"]:
                load_x(i + 1)
            if first:
                yield from mod_part(0, 8)
                gs_op(gs1, 8, V_G1)
            yield 'S'
            yield from compute_xsq_and_stats(s)
            yield from norm_apply(s, gs1, sh1)

            def hook1():
                yield from mod_part(8, 12)
                half_op(gth1, 16)
            yield from ffn(s, "w1i", b1i, "w1o", b1o, gth1, hook=hook1 if first else None)
            if first:
                yield from mod_part(12, 20)
                gs_op(gs2, 32, V_GM)
            yield from compute_xsq_and_stats(s)
            yield from norm_apply(s, gs2, sh2)
            yield from mixer(s, hook=(lambda: mod_part(20, 24)) if first else None)
            if first:
                yield from mod_part(24, 32)
                gs_op(gs3, 56, V_G2)
            yield from compute_xsq_and_stats(s)
            yield from norm_apply(s, gs3, sh3)

            def hook2():
                yield from mod_part(32, 36)
                half_op(gth3, 64)
            yield from ffn(s, "w2i", b2i, "w2o", b2o, gth3, hook=hook2 if first else None)
            if first:
                yield from mod_part(36, 44)
                gs_op(gsf, 80, V_GF)
            yield from compute_xsq_and_stats(s)
            def out_block(fc, qo):
                b = next_bank()

                def fn(h, qo=qo, b=b):
                    return [h.transpose(out=ps[:, b, blk * 128:(blk + 1) * 128], in_=oT[qo][:, blk * 128:(blk + 1) * 128],
                                        identity=ident[:]) for blk in range(4)]
                E(PE, fn, reads=[r_oT[qo], r_const], writes=[r_bank[b]])
                k = nxt("oo", 3)
                if fc % 2 == 0:
                    E(DVE, lambda h, k=k, b=b: [h.tensor_copy(out=oo[k][:], in_=ps[:, b, :].rearrange("p (k n) -> p k n", k=4))],
                      reads=[r_bank[b]], writes=[r_oo[k]])
                else:
                    E(ACT, lambda h, k=k, b=b: [h.activation(out=oo[k][:], in_=ps[:, b, :].rearrange("p (k n) -> p k n", k=4), func=AF.Copy)],
                      reads=[r_bank[b]], writes=[r_oo[k]])
                if not st["dry"]:
                    dst = y_d[i * T:(i + 1) * T, fc * 128:(fc + 1) * 128].rearrange("(b p) d -> p b d", p=128)
                    dma1(PL, dst, oo[k][:], [r_oo[k]], [], sem_oo[k])

            prev = None
            for fc in range(8):
                yield 'e'
                q = nxt("ntmp", 2)
                qo = nxt("oT", 3)
                E(DVE, lambda h, q=q, fc=fc: [h.scalar_tensor_tensor(
                    out=ntmp[q][:], in0=xT[:, fc, :], scalar=gsf[:, fc:fc + 1], in1=rstd[:], op0=ALU.mult, op1=ALU.mult)],
                    reads=[r_xT[fc], r_rstd, r_derived], writes=[r_ntmp[q]])
                E(ACT, lambda h, q=q, qo=qo, fc=fc: [h.activation(out=oT[qo][:], in_=ntmp[q][:], func=AF.Identity,
                                                                 bias=shf[:, fc:fc + 1], scale=1.0)],
                  reads=[r_ntmp[q], r_modv], writes=[r_oT[qo]])
                if prev is not None:
                    out_block(*prev)
                prev = (fc, qo)
            yield 'e'
            out_block(*prev)

        def program():
            gens = []
            nxt_tile = [0]

            def step(e):
                cur_tile[0] = e[3]
                e[1] = next(e[0], None)
                while e[1] == 'S':
                    e[2] = True
                    e[1] = next(e[0], None)

            def maybe_spawn():
                while len(gens) < 2 and nxt_tile[0] < n_tiles and (not gens or gens[-1][2]):
                    e = [tile_gen(nxt_tile[0]), None, False, nxt_tile[0]]
                    nxt_tile[0] += 1
                    step(e)
                    gens.append(e)

            def can_own(e):
                return e[1] == 'M' or (e[1] == 'MIX' and mix_done[0] == e[3])

            mix_done[0] = 0
            owner = None
            maybe_spawn()
            while gens:
                if owner is not None and owner[1] in ('M', 'MIX') and owner in gens:
                    step(owner)
                    for o in gens:
                        if o is not owner:
                            for _ in range(2):
                                if o[1] == 'e':
                                    step(o)
                else:
                    owner = None
                    for e in gens:
                        if can_own(e):
                            owner = e
                            break
                    if owner is None:
                        progressed = False
                        for e in gens:
                            if e[1] == 'e':
                                step(e)
                                progressed = True
                        assert progressed, [(e[1], e[3]) for e in gens]
                gens[:] = [e for e in gens if e[1] is not None]
                maybe_spawn()

        program()
        st["dry"] = False
        bank_ctr[0] = 0
        rot.clear()
        held.clear()
        for _ in range(NS):
            issue_slab()
        program()

        final_vals = [(sm.h, sm.val) for sm in sem_oo]
        assert st["used"] == len(plan), (st, len(plan))

        with nc.Block() as block:
            @block.sync
            def _(h):
                replay(SP, h)

            @block.gpsimd
            def _(h):
                replay(PL, h)
                for smh, v in final_vals:
                    if v:
                        h.wait_ge(smh, v)

            @block.tensor
            def _(h):
                replay(PE, h)

            @block.scalar
            def _(h):
                replay(ACT, h)

            @block.vector
            def _(h):
                replay(DVE, h)
    return nc


def prep_inputs(inputs, n_tiles=SEQ // T, cores=N_CORES):
    f = lambda a: np.ascontiguousarray(np.asarray(a, dtype=np.float32))
    col = lambda v, n: f(v).reshape(n, 128).T
    x = f(inputs["x"]); c = f(inputs["c"])
    shared = np.zeros((128, NV), np.float32)
    shared[:, V_G1:V_G1 + 8] = col(inputs["g_ffn1"][0], 8)
    shared[:, V_GM:V_GM + 8] = col(inputs["g_mix"][0], 8)
    shared[:, V_G2:V_G2 + 8] = col(inputs["g_ffn2"][0], 8)
    shared[:, V_GF:V_GF + 8] = col(inputs["g_final"], 8)
    shared[:, V_BMOD:V_BMOD + 72] = col(inputs["b_mod"][0], 72)
    shared[:, V_BMOD + 72:V_BMOD + 88] = col(inputs["b_fmod"], 16)
    shared[:, V_CONVB:V_CONVB + 4] = col(inputs["conv_b"][0], 4)
    shared[:, V_LNG:V_LNG + 4] = col(inputs["ln_g"][0], 4)
    shared[:, V_LNB:V_LNB + 4] = col(inputs["ln_b"][0], 4)
    shared[:, V_RCB:V_RCB + 4] = col(inputs["rnn_conv_b"][0], 4)
    shared[:, V_BA:V_BA + 4] = col(inputs["b_a"][0], 4)
    shared[:, V_BI:V_BI + 4] = col(inputs["b_i"][0], 4)
    shared[:, V_LAM:V_LAM + 4] = col(inputs["lru_lambda"][0], 4)
    rcw = f(inputs["rnn_conv_w"][0])
    for cc in range(4):
        for k in range(4):
            shared[:, V_RCW + 4 * cc + k] = rcw[k, cc * 128:(cc + 1) * 128]
    cw = f(inputs["conv_w"][0])
    dcw = np.zeros((4, 128, NCONV, 128), np.float32)
    ar = np.arange(128)
    for cc in range(4):
        for k in range(NCONV):
            dcw[cc, ar, k, ar] = cw[k, cc * 128:(cc + 1) * 128]
    dcw = dcw.reshape(512, NCONV * 128)
    def bd(w):
        w = f(w)
        o = np.zeros((128, 4, 128), np.float32)
        for cc in range(4):
            o[0:64, cc, 0:64] = w[2 * cc]
            o[64:128, cc, 64:128] = w[2 * cc + 1]
        return o.reshape(128, 512)
    wabd = bd(inputs["w_a"][0]); wibd = bd(inputs["w_i"][0])
    common = {
        "ident": np.eye(128, dtype=np.float32), "wabd": wabd, "wibd": wibd,
        "w_mod": f(inputs["w_mod"][0]), "w_fmod": f(inputs["w_fmod"]),
        "w1i": f(inputs["w_ffn1_in"][0]), "w1o": f(inputs["w_ffn1_out"][0]),
        "win": f(inputs["w_in"][0]), "wout": f(inputs["w_out"][0]),
        "w2i": f(inputs["w_ffn2_in"][0]), "w2o": f(inputs["w_ffn2_out"][0]),
        "dcw": dcw,
    }
    in_maps = []
    S = n_tiles * T
    for b in range(cores):
        v = shared.copy()
        v[:, V_C:V_C + 8] = c[b].reshape(8, 128).T
        m = dict(common)
        m["vecs"] = v
        m["x"] = np.ascontiguousarray(x[b, :S, :])
        in_maps.append(m)
    return in_maps


_NC_CACHE = {}


def kernel(**inputs):
    n_tiles = SEQ // T
    if n_tiles not in _NC_CACHE:
        _NC_CACHE[n_tiles] = build_program(n_tiles)
    nc = _NC_CACHE[n_tiles]
    in_maps = prep_inputs(inputs, n_tiles, N_CORES)
    res = run_bass_kernel_spmd(nc, in_maps, core_ids=list(range(N_CORES)))
    out = np.stack([np.asarray(r["y"], dtype=np.float32) for r in res.results], axis=0)
    return out
```

```python
import numpy as np
import concourse.bass as bass
import concourse.mybir as mybir
from concourse.bass_utils import run_bass_kernel_spmd

F32 = mybir.dt.float32
BF16 = mybir.dt.bfloat16
AF = mybir.ActivationFunctionType
ALU = mybir.AluOpType

D = 1024
DFF = 2816
SEQ = 4096
T = 512
KC = 8
FC = 22
NCONV = 31
EPS = 1e-6
NS = 3
N_CORES = 8

V_G1, V_GM, V_G2, V_GF = 0, 8, 16, 24
V_BMOD = 32
V_CONVB = 120
V_LNG = 124
V_LNB = 128
V_RCB = 132
V_BA = 136
V_BI = 140
V_LAM = 144
V_RCW = 148
V_C = 164
NV = 172


class Sem:
    def __init__(self, h):
        self.h = h
        self.val = 0


class Res:
    __slots__ = ("w", "r")

    def __init__(self):
        self.w = None
        self.r = {}


class Eng:
    def __init__(self, name, sem, self_sync):
        self.name = name
        self.sem = sem
        self.prog = []
        self.seen = {}
        self.self_sync = self_sync


def emit(eng, fn, reads=(), writes=(), dma=None, ndma=1):
    waits = {}

    def need(tok):
        if tok is None:
            return
        s, v = tok
        if waits.get(s, 0) < v:
            waits[s] = v

    for r in reads:
        need(r.w)
    for w in writes:
        need(w.w)
        for s, v in w.r.items():
            need((s, v))
    wl = []
    for s, v in waits.items():
        if s is eng.sem and not eng.self_sync:
            continue
        if eng.seen.get(s, 0) >= v:
            continue
        eng.seen[s] = v
        wl.append((s, v))
    if dma is None:
        eng.sem.val += 1
        tok = (eng.sem, eng.sem.val)
        eng.prog.append((wl, fn, eng.sem, 1, False))
    else:
        dma.val += 16 * ndma
        tok = (dma, dma.val)
        eng.prog.append((wl, fn, dma, 16, True))
    for r in reads:
        if r.r.get(tok[0], 0) < tok[1]:
            r.r[tok[0]] = tok[1]
    for w in writes:
        w.w = tok
        w.r = {}
    return tok


def replay(eng, h):
    for wl, fn, sem, inc, is_dma in eng.prog:
        for s, v in wl:
            h.wait_ge(s.h, v)
        ins = fn(h)
        if is_dma:
            for i in ins:
                i.then_inc(sem.h, 16)
        else:
            ins[-1].then_inc(sem.h, 1)


def build_program(n_tiles=SEQ // T):
    S = n_tiles * T
    nc = bass.Bass("TRN2", target_bir_lowering=False)
    dt_in = lambda name, shape: nc.dram_tensor(name, shape, F32, kind="ExternalInput").ap()
    x_d = dt_in("x", [S, D])
    vecs_d = dt_in("vecs", [128, NV])
    ident_d = dt_in("ident", [128, 128])
    wabd_d = dt_in("wabd", [128, 512])
    wibd_d = dt_in("wibd", [128, 512])
    wmod_d = dt_in("w_mod", [D, 9 * D])
    wfmod_d = dt_in("w_fmod", [D, 2 * D])
    w1i_d = dt_in("w1i", [D, 2 * DFF])
    w1o_d = dt_in("w1o", [DFF, D])
    win_d = dt_in("win", [D, 2 * D])
    wout_d = dt_in("wout", [D, D])
    w2i_d = dt_in("w2i", [D, 2 * DFF])
    w2o_d = dt_in("w2o", [DFF, D])
    dcw_d = dt_in("dcw", [512, NCONV * 128])
    y_d = nc.dram_tensor("y", [S, D], F32, kind="ExternalOutput").ap()
    dt_sc = lambda name, shape: nc.dram_tensor(name, shape, BF16, kind="Internal").ap()
    b1i = dt_sc("b1i", [D, 2 * DFF])
    b1o = dt_sc("b1o", [DFF, D])
    bin_ = dt_sc("bin", [D, 2 * D])
    bout = dt_sc("bout", [D, D])
    b2i = dt_sc("b2i", [D, 2 * DFF])
    b2o = dt_sc("b2o", [DFF, D])
    bdc = dt_sc("bdc", [512, NCONV * 128])

    from contextlib import ExitStack
    es = ExitStack()
    with es:
        sb = lambda name, shape, dt=F32: es.enter_context(nc.sbuf_tensor("sb_" + name, shape, dt))
        newsem = lambda name: Sem(es.enter_context(nc.semaphore(name)))

        PE = Eng("pe", newsem("s_pe"), False)
        ACT = Eng("act", newsem("s_act"), True)
        DVE = Eng("dve", newsem("s_dve"), True)
        SP = Eng("sp", newsem("s_sp"), False)
        PL = Eng("pool", newsem("s_pool"), False)

        ident = sb("ident", [128, 128])
        ones_bf = sb("ones_bf", [128, 128], BF16)
        vecs = sb("vecs", [128, NV])
        modv = sb("modv", [128, 88])
        cact = sb("cact", [128, 8])
        gs1 = sb("gs1", [128, 8]); gth1 = sb("gth1", [128, 8])
        gs2 = sb("gs2", [128, 8])
        gs3 = sb("gs3", [128, 8]); gth3 = sb("gth3", [128, 8])
        gsf = sb("gsf", [128, 8])
        lam8 = sb("lam8", [128, 4]); lam16 = sb("lam16", [128, 4]); lamt = sb("lamt", [128, 4])
        wabd = sb("wabd", [128, 4, 128], BF16)
        wibd = sb("wibd", [128, 4, 128], BF16)
        hstate = sb("hstate", [128, 4])
        slots = [sb(f"slab{i}", [128, 8, 512], BF16) for i in range(NS)]
        xin = sb("xin", [128, 4, D])
        xTs = [sb(f"xT{k}", [128, 8, T]) for k in range(2)]
        xsq = [sb(f"xsq{i}", [128, T], BF16) for i in range(4)]
        hbs = [sb(f"hb{k}", [128, 8, T], BF16) for k in range(2)]
        hid = sb("hid", [128, FC, T], BF16)
        sgb = [sb(f"sgb{i}", [128, T]) for i in range(2)]
        ntmp = [sb(f"ntmp{i}", [128, T]) for i in range(2)]
        rstds = [sb(f"rstd{k}", [128, T]) for k in range(2)]
        ub = sb("ub", [128, 4, T + 30], BF16)
        ux = sb("ux", [128, 4, T + 3])
        gy = sb("gy", [128, 4, T], BF16)
        cv = sb("cv", [128, 4, T])
        cvsq = [sb(f"cvsq{i}", [128, T], BF16) for i in range(2)]
        cvb = [sb(f"cvb{i}", [128, T], BF16) for i in range(2)]
        meanb = sb("meanb", [128, T])
        varb = sb("varb", [128, T])
        xr = [sb(f"xr{i}", [128, T]) for i in range(4)]
        xrb = [sb(f"xrb{i}", [128, T], BF16) for i in range(4)]
        abuf = [sb(f"abuf{i}", [128, T]) for i in range(2)]
        mbuf = [sb(f"mbuf{i}", [128, T]) for i in range(2)]
        btb = [sb(f"btb{i}", [128, T]) for i in range(2)]
        hs = [sb(f"hs{i}", [128, T]) for i in range(4)]
        oT = [sb(f"oT{i}", [128, T]) for i in range(3)]
        oo = [sb(f"oo{k}", [128, 4, 128]) for k in range(3)]
        ps = es.enter_context(nc.psum_tensor("ps", [128, 8, T], F32))

        R = lambda: Res()
        r_const = R()
        r_vecs = R(); r_modv = R(); r_cact = R(); r_derived = R(); r_hstate = [R() for _ in range(4)]
        r_gates = R()
        r_slot = [R() for _ in range(NS)]
        sem_slot = [newsem(f"s_slot{i}") for i in range(NS)]
        r_bank = [R() for _ in range(8)]
        r_xin = R(); sem_xin = newsem("s_xin")
        r_xTs = [[R() for _ in range(8)] for _ in range(2)]
        r_xsq = [R() for _ in range(4)]
        r_hbs = [[R() for _ in range(8)] for _ in range(2)]
        r_hid = [R() for _ in range(FC)]
        r_sgb = [R() for _ in range(2)]
        r_ntmp = [R() for _ in range(2)]
        r_rstds = [R(), R()]
        r_ub = [R() for _ in range(4)]
        r_ux = [R() for _ in range(4)]
        r_gy = [R() for _ in range(4)]
        r_cv = [R() for _ in range(4)]
        r_cvsq = [R() for _ in range(2)]
        r_cvb = [R() for _ in range(2)]
        r_meanb = R(); r_varb = R()
        r_xr = [R() for _ in range(4)]; r_xrb = [R() for _ in range(4)]
        r_abuf = [R() for _ in range(2)]
        r_mbuf = [R() for _ in range(2)]; r_btb = [R() for _ in range(2)]
        r_hs = [R() for _ in range(4)]
        r_oT = [R() for _ in range(3)]
        r_oo = [R() for _ in range(3)]; sem_oo = [newsem(f"s_oo{k}") for k in range(3)]
        sem_const = newsem("s_const")
        r_scr = {}
        sem_wb = [newsem(f"s_wb{i}") for i in range(NS)]
        sem_slot_pl = [newsem(f"s_slotp{i}") for i in range(NS)]

        bank_ctr = [0]

        held = set()

        def next_bank(hold=False):
            while True:
                b = bank_ctr[0] % 8
                bank_ctr[0] += 1
                if b not in held:
                    break
            if hold:
                held.add(b)
            return b

        rot = {}

        def nxt(name, n):
            v = rot.get(name, 0)
            rot[name] = v + 1
            return v % n

        def dma1(eng, out_ap, in_ap, reads, writes, sem):
            return emit(eng, lambda h: [h.dma_start(out=out_ap, in_=in_ap)], reads=reads, writes=writes, dma=sem)

        dma1(SP, vecs[:], vecs_d[:, :], [], [r_vecs], sem_const)
        dma1(SP, ident[:], ident_d[:, :], [], [r_const], newsem("s_ident"))
        def load_x(i):
            src = x_d[i * T:(i + 1) * T, :].rearrange("(b p) d -> p b d", p=128)
            dma1(PL, xin[:], src, [], [r_xin], sem_xin)
        load_x(0)
        sem_g = newsem("s_gates")
        dma1(PL, wabd[:], wabd_d[:, :].rearrange("p (c n) -> p c n", c=4), [], [r_gates], sem_g)
        dma1(PL, wibd[:], wibd_d[:, :].rearrange("p (c n) -> p c n", c=4), [], [r_gates], sem_g)

        emit(DVE, lambda h: [h.memset(ones_bf[:], 1.0)], writes=[r_const])
        emit(DVE, lambda h: [h.memset(hstate[:], 0.0)], writes=r_hstate)
        emit(DVE, lambda h: [h.memset(ub[:, :, 0:30], 0.0)], writes=r_ub)
        emit(DVE, lambda h: [h.memset(ux[:, :, 0:3], 0.0)], writes=r_ux)
        emit(ACT, lambda h: [h.activation(out=cact[:], in_=vecs[:, V_C:V_C + 8], func=AF.Silu)],
             reads=[r_vecs], writes=[r_cact])
        emit(ACT, lambda h: [h.activation(out=lamt[:], in_=vecs[:, V_LAM:V_LAM + 4], func=AF.Exp, scale=-1.0)],
             reads=[r_vecs], writes=[r_derived])
        emit(ACT, lambda h: [h.activation(out=lamt[:], in_=lamt[:], func=AF.Ln, bias=1.0)],
             reads=[r_derived], writes=[r_derived])
        emit(DVE, lambda h: [h.tensor_scalar(out=lam8[:], in0=lamt[:], scalar1=-8.0, scalar2=None, op0=ALU.mult)],
             reads=[r_derived], writes=[r_derived])
        emit(DVE, lambda h: [h.tensor_scalar(out=lam16[:], in0=lamt[:], scalar1=-16.0, scalar2=None, op0=ALU.mult)],
             reads=[r_derived], writes=[r_derived])

        kview = lambda ap: ap.rearrange("(kc p) n -> p kc n", p=128)
        plan = []
        st = {"issued": 0, "used": 0, "dry": True}

        cur_tile = [0]
        pos_ctr = {}

        def issue_slab():
            k = st["issued"]
            if k >= len(plan):
                return
            st["issued"] += 1
            si = k % NS
            cname, pfn, tile, pos = plan[k]

            def mk(pieces):
                return lambda h: [h.dma_start(out=o, in_=i) for o, i in pieces]
            if cname is None:
                pieces = pfn(si, 0)
                emit(SP, mk(pieces), writes=[r_slot[si]], dma=sem_slot[si], ndma=len(pieces))
            elif tile == 0:
                pieces = pfn(si, 1)
                emit(PL, mk(pieces), writes=[r_slot[si]], dma=sem_slot_pl[si], ndma=len(pieces))
                back = [(d, o) for o, d in pfn(si, 0)]
                r_scr[pos] = Res()
                emit(SP, mk(back), reads=[r_slot[si]], writes=[r_scr[pos]], dma=sem_wb[si], ndma=len(back))
            else:
                pieces = pfn(si, 0)
                emit(SP, mk(pieces), reads=[r_scr[pos]], writes=[r_slot[si]], dma=sem_slot[si], ndma=len(pieces))

        def next_slab(cname, pfn):
            if st["dry"]:
                t = cur_tile[0]
                pos = None
                if cname is not None:
                    pos = pos_ctr.get(t, 0)
                    pos_ctr[t] = pos + 1
                plan.append((cname, pfn, t, pos))
                return 0
            k = st["used"]
            st["used"] += 1
            assert k < st["issued"] and plan[k][0] == cname
            return k % NS

        def release_slab():
            if not st["dry"]:
                issue_slab()

        def E(eng, fn, reads=(), writes=()):
            if st["dry"]:
                return None
            return emit(eng, fn, reads=reads, writes=writes)

        def d_mod(g):
            src = wmod_d if g < 36 else wfmod_d
            gg = g if g < 36 else g - 36
            return (None, lambda si, f: [(slots[si][:].bitcast(F32), kview(src[:, gg * 256:(gg + 1) * 256]))])

        WT = {"w1i": (b1i, w1i_d), "w1o": (b1o, w1o_d), "win": (bin_, win_d), "wout": (bout, wout_d),
              "w2i": (b2i, w2i_d), "w2o": (b2o, w2o_d), "dcw": (bdc, dcw_d)}

        def d_ffn_in(nm, wi_unused, j):
            return (nm, lambda si, f: [
                (slots[si][:, :, 0:256], kview(WT[nm][f][:, 256 * j:256 * j + 256])),
                (slots[si][:, :, 256:512], kview(WT[nm][f][:, DFF + 256 * j:DFF + 256 * j + 256]))])

        def d_ffn_out(nm, wo_unused, ch, k0, nk):
            return (nm, lambda si, f: [(slots[si][:, 0:nk, :], kview(WT[nm][f][k0 * 128:(k0 + nk) * 128, ch * 512:(ch + 1) * 512]))])

        def d_cols(nm, w_unused, c0):
            return (nm, lambda si, f: [(slots[si][:, :, :], kview(WT[nm][f][:, c0:c0 + 512]))])

        def d_glu(half):
            return ("win", lambda si, f: [
                (slots[si][:, :, 0:256], kview(WT["win"][f][:, 256 * half:256 * half + 256])),
                (slots[si][:, :, 256:512], kview(WT["win"][f][:, 512 + 256 * half:512 + 256 * half + 256]))])

        def d_conv(c):
            return ("dcw", lambda si, f: [
                (slots[si][:].rearrange("p k n -> p (k n)")[:, 0:NCONV * 128], WT["dcw"][f][c * 128:(c + 1) * 128, :])])

        def mm_group(bank, pairs, reads, first=True, last=True):
            n = len(pairs)

            def fn(h):
                out = []
                for idx, (l, r) in enumerate(pairs):
                    out.append(h.matmul(ps[:, bank, :], lhsT=l, rhs=r,
                                        start=(first and idx == 0), stop=(last and idx == n - 1)))
                return out
            return E(PE, fn, reads=reads, writes=[r_bank[bank]])

        sh1 = modv[:, 0:8]; sh2 = modv[:, 24:32]; gt2 = modv[:, 40:48]; sh3 = modv[:, 48:56]; shf = modv[:, 72:80]

        def mod_part(g0, g1):
            bank = next_bank(hold=True)
            for g in range(g0, g1):
                si = next_slab(*d_mod(g))
                sf = slots[si][:].bitcast(F32)

                def fn(h, g=g, sf=sf):
                    out = []
                    for sub in range(2):
                        j = 2 * g + sub
                        for kc in range(8):
                            out.append(h.matmul(ps[:, bank, j:j + 1], lhsT=sf[:, kc, sub * 128:(sub + 1) * 128],
                                                rhs=cact[:, kc:kc + 1], start=(kc == 0), stop=(kc == 7)))
                    return out
                E(PE, fn, reads=[r_slot[si], r_cact], writes=[r_bank[bank]])
                release_slab()
                yield 'M'
            c0, c1 = 2 * g0, 2 * g1
            E(DVE, lambda h: [h.tensor_tensor(out=modv[:, c0:c1], in0=ps[:, bank, c0:c1], in1=vecs[:, V_BMOD + c0:V_BMOD + c1], op=ALU.add)],
              reads=[r_bank[bank], r_vecs], writes=[r_modv])
            held.discard(bank)

        def gs_op(dst, sc_col, g_col):
            E(DVE, lambda h: [h.scalar_tensor_tensor(out=dst[:], in0=modv[:, sc_col:sc_col + 8], scalar=1.0,
                                                     in1=vecs[:, g_col:g_col + 8], op0=ALU.add, op1=ALU.mult)],
              reads=[r_modv, r_vecs], writes=[r_derived])

        def half_op(dst, col):
            E(DVE, lambda h: [h.tensor_scalar(out=dst[:], in0=modv[:, col:col + 8], scalar1=0.5, scalar2=None, op0=ALU.mult)],
              reads=[r_modv], writes=[r_derived])

        def rsqrt_chain(buf, res, bank, scale):
            E(ACT, lambda h: [h.activation(out=buf[:], in_=ps[:, bank, :], func=AF.Sqrt, bias=EPS, scale=scale)],
              reads=[r_bank[bank]], writes=[res])
            E(DVE, lambda h: [h.reciprocal(out=buf[:], in_=buf[:])], reads=[res], writes=[res])

        def compute_xsq_and_stats(s):
            xT = xTs[s]; r_xT = r_xTs[s]; rstd = rstds[s]; r_rstd = r_rstds[s]
            yield 'e'
            sbank = next_bank(hold=True)
            pend = []

            def flush():
                for (q, fc) in pend:
                    mm_group(sbank, [(ones_bf[:], xsq[q][:])], [r_xsq[q], r_const], first=(fc == 0), last=(fc == 7))
                del pend[:]
            for fc in range(8):
                if fc % 2 == 0:
                    flush()
                q = nxt("xsq", 4)
                E(ACT, lambda h, q=q, fc=fc: [h.activation(out=xsq[q][:], in_=xT[:, fc, :], func=AF.Square)],
                  reads=[r_xT[fc]], writes=[r_xsq[q]])
                pend.append((q, fc))
                if fc % 2 == 1:
                    yield 'e'
            flush()
            rsqrt_chain(rstd, r_rstd, sbank, 1.0 / D)
            held.discard(sbank)

        def norm_apply(s, gs, sh):
            xT = xTs[s]; r_xT = r_xTs[s]; rstd = rstds[s]; r_rstd = r_rstds[s]; hb = hbs[s]; r_hb = r_hbs[s]
            for fc in range(8):
                if fc % 2 == 0:
                    yield 'e'
                q = nxt("ntmp", 2)
                E(DVE, lambda h, q=q, fc=fc: [h.scalar_tensor_tensor(
                    out=ntmp[q][:], in0=xT[:, fc, :], scalar=gs[:, fc:fc + 1], in1=rstd[:], op0=ALU.mult, op1=ALU.mult)],
                    reads=[r_xT[fc], r_rstd, r_derived], writes=[r_ntmp[q]])
                E(ACT, lambda h, q=q, fc=fc: [h.activation(out=hb[:, fc, :], in_=ntmp[q][:], func=AF.Identity,
                                                           bias=sh[:, fc:fc + 1], scale=1.0)],
                  reads=[r_ntmp[q], r_modv], writes=[r_hb[fc]])

        def ffn(s, nm_i, wi, nm_o, wo, gth, hook=None):
            xT = xTs[s]; r_xT = r_xTs[s]; hb = hbs[s]; r_hb = r_hbs[s]
            for j in range(11):
                yield 'M'
                si = next_slab(*d_ffn_in(nm_i, wi, j))
                for sub in range(2):
                    cg = 2 * j + sub
                    bg = next_bank()
                    mm_group(bg, [(slots[si][:, kc, sub * 128:(sub + 1) * 128], hb[:, kc, :]) for kc in range(8)],
                             [r_slot[si]] + r_hb)
                    bu = next_bank()
                    mm_group(bu, [(slots[si][:, kc, 256 + sub * 128:256 + (sub + 1) * 128], hb[:, kc, :]) for kc in range(8)],
                             [r_slot[si]] + r_hb)
                    q = nxt("sgb", 2)
                    E(ACT, lambda h, q=q, bg=bg: [h.activation(out=sgb[q][:], in_=ps[:, bg, :], func=AF.Silu)],
                      reads=[r_bank[bg]], writes=[r_sgb[q]])
                    E(DVE, lambda h, q=q, bu=bu, cg=cg: [h.tensor_tensor(out=hid[:, cg, :], in0=ps[:, bu, :], in1=sgb[q][:], op=ALU.mult)],
                      reads=[r_bank[bu], r_sgb[q]], writes=[r_hid[cg]])
                release_slab()
            if hook is not None:
                yield from hook()
            for ch in range(2):
                yield 'M'
                banks = [next_bank(hold=True) for _ in range(4)]
                for (k0, nk) in ((0, 8), (8, 8), (16, 6)):
                    si = next_slab(*d_ffn_out(nm_o, wo, ch, k0, nk))
                    for oc in range(4):
                        mm_group(banks[oc],
                                 [(slots[si][:, kl, oc * 128:(oc + 1) * 128], hid[:, k0 + kl, :]) for kl in range(nk)],
                                 [r_slot[si]] + r_hid[k0:k0 + nk], first=(k0 == 0), last=(k0 == 16))
                    release_slab()
                for oc in range(4):
                    fc = 4 * ch + oc
                    E(DVE, lambda h, fc=fc, b=banks[oc]: [h.scalar_tensor_tensor(
                        out=xT[:, fc, :], in0=ps[:, b, :], scalar=gth[:, fc:fc + 1], in1=xT[:, fc, :], op0=ALU.mult, op1=ALU.add)],
                        reads=[r_bank[banks[oc]], r_derived, r_modv], writes=[r_xT[fc]])
                for b in banks:
                    held.discard(b)

        mix_done = [0]

        def mixer(s, hook=None):
            xT = xTs[s]; r_xT = r_xTs[s]; hb = hbs[s]; r_hb = r_hbs[s]
            yield 'MIX'
            si = next_slab(*d_cols("win", bin_, 1024))
            for c in range(4):
                b = next_bank()
                mm_group(b, [(slots[si][:, kc, c * 128:(c + 1) * 128], hb[:, kc, :]) for kc in range(8)], [r_slot[si]] + r_hb)
                E(ACT, lambda h, c=c, b=b: [h.activation(out=ux[:, c, 3:3 + T], in_=ps[:, b, :], func=AF.Copy)],
                  reads=[r_bank[b]], writes=[r_ux[c]])
            release_slab()
            for c in range(4):
                E(DVE, lambda h, c=c: [h.tensor_scalar(
                    out=xr[c][:], in0=ux[:, c, 0:T], scalar1=vecs[:, V_RCW + 4 * c:V_RCW + 4 * c + 1],
                    scalar2=vecs[:, V_RCB + c:V_RCB + c + 1], op0=ALU.mult, op1=ALU.add)],
                    reads=[r_ux[c], r_vecs], writes=[r_xr[c]])
                for k in range(1, 4):
                    E(DVE, lambda h, c=c, k=k: [h.scalar_tensor_tensor(
                        out=xr[c][:], in0=ux[:, c, k:k + T], scalar=vecs[:, V_RCW + 4 * c + k:V_RCW + 4 * c + k + 1],
                        in1=xr[c][:], op0=ALU.mult, op1=ALU.add)],
                        reads=[r_ux[c], r_vecs, r_xr[c]], writes=[r_xr[c]])
                E(DVE, lambda h, c=c: [h.tensor_copy(out=ux[:, c, 0:3], in_=ux[:, c, T:T + 3])], reads=[r_ux[c]], writes=[r_ux[c]])
                E(DVE, lambda h, c=c: [h.tensor_copy(out=xrb[c][:], in_=xr[c][:])],
                  reads=[r_xr[c]], writes=[r_xrb[c]])
            for half in range(2):
                yield 'M'
                si = next_slab(*d_glu(half))
                for sub in range(2):
                    c = 2 * half + sub
                    bv = next_bank()
                    mm_group(bv, [(slots[si][:, kc, sub * 128:(sub + 1) * 128], hb[:, kc, :]) for kc in range(8)], [r_slot[si]] + r_hb)
                    bg = next_bank()
                    mm_group(bg, [(slots[si][:, kc, 256 + sub * 128:256 + (sub + 1) * 128], hb[:, kc, :]) for kc in range(8)], [r_slot[si]] + r_hb)
                    q = nxt("sgb", 2)
                    E(ACT, lambda h, q=q, bg=bg: [h.activation(out=sgb[q][:], in_=ps[:, bg, :], func=AF.Sigmoid)],
                      reads=[r_bank[bg]], writes=[r_sgb[q]])
                    E(DVE, lambda h, q=q, bv=bv, c=c: [h.tensor_tensor(out=ub[:, c, 30:30 + T], in0=ps[:, bv, :], in1=sgb[q][:], op=ALU.mult)],
                      reads=[r_bank[bv], r_sgb[q]], writes=[r_ub[c]])
                release_slab()

            hs_of = {}

            def rnn_pair_a(c0):
                cs = (c0, c0 + 1)
                qx = {c: c for c in cs}
                br = {}; bi = {}
                for c in cs:
                    br[c] = next_bank()
                    mm_group(br[c], [(wabd[:, c, :], xrb[c][:])], [r_gates, r_xrb[c]])
                    bi[c] = next_bank()
                    mm_group(bi[c], [(wibd[:, c, :], xrb[c][:])], [r_gates, r_xrb[c]])
                qx = {c: c % 2 for c in cs}
                for c in cs:
                    q = qx[c]
                    E(ACT, lambda h, c=c, q=q: [h.activation(out=abuf[q][:], in_=ps[:, br[c], :], func=AF.Sigmoid,
                                                           bias=vecs[:, V_BA + c:V_BA + c + 1], scale=1.0)],
                      reads=[r_bank[br[c]], r_vecs], writes=[r_abuf[q]])
                    E(ACT, lambda h, c=c, q=q: [h.activation(out=btb[q][:], in_=ps[:, bi[c], :], func=AF.Sigmoid,
                                                           bias=vecs[:, V_BI + c:V_BI + c + 1], scale=1.0)],
                      reads=[r_bank[bi[c]], r_vecs], writes=[r_btb[q]])
                for c in cs:
                    q = qx[c]
                    E(ACT, lambda h, c=c, q=q: [h.activation(out=mbuf[q][:], in_=abuf[q][:], func=AF.Exp, scale=lam16[:, c:c + 1])],
                      reads=[r_abuf[q], r_derived], writes=[r_mbuf[q]])
                    E(ACT, lambda h, c=c, q=q: [h.activation(out=abuf[q][:], in_=abuf[q][:], func=AF.Exp, scale=lam8[:, c:c + 1])],
                      reads=[r_abuf[q], r_derived], writes=[r_abuf[q]])
                for c in cs:
                    q = qx[c]
                    E(DVE, lambda h, q=q: [h.tensor_scalar(out=mbuf[q][:], in0=mbuf[q][:], scalar1=1.0, scalar2=-1.0, op0=ALU.min, op1=ALU.mult)],
                      reads=[r_mbuf[q]], writes=[r_mbuf[q]])
                    E(DVE, lambda h, q=q, c=c: [h.tensor_tensor(out=btb[q][:], in0=btb[q][:], in1=xr[c][:], op=ALU.mult)],
                      reads=[r_btb[q], r_xr[c]], writes=[r_btb[q]])
                for c in cs:
                    q = qx[c]
                    E(ACT, lambda h, q=q: [h.activation(out=mbuf[q][:], in_=mbuf[q][:], func=AF.Sqrt, bias=1.0, scale=1.0)],
                      reads=[r_mbuf[q]], writes=[r_mbuf[q]])
                for c in cs:
                    q = qx[c]
                    E(DVE, lambda h, q=q: [h.tensor_tensor(out=btb[q][:], in0=btb[q][:], in1=mbuf[q][:], op=ALU.mult)],
                      reads=[r_btb[q], r_mbuf[q]], writes=[r_btb[q]])
                    E(DVE, lambda h, c=c, q=q: [h.tensor_tensor_scan(out=hs[c][:], data0=abuf[q][:], data1=btb[q][:],
                                                                     initial=hstate[:, c:c + 1], op0=ALU.mult, op1=ALU.add)],
                      reads=[r_abuf[q], r_btb[q], r_hstate[c]], writes=[r_hs[c]])
                    E(DVE, lambda h, c=c: [h.tensor_copy(out=hstate[:, c:c + 1], in_=hs[c][:, T - 1:T])],
                      reads=[r_hs[c]], writes=[r_hstate[c]])

            def rnn_b(c):
                E(DVE, lambda h, c=c: [h.tensor_tensor(out=hb[:, 4 + c, :], in0=hs[c][:], in1=gy[:, c, :], op=ALU.mult)],
                  reads=[r_hs[c], r_gy[c]], writes=[r_hb[4 + c]])

            def conv_chunk(c):
                si = next_slab(*d_conv(c))
                dflat = slots[si][:].rearrange("p k n -> p (k n)")
                b = next_bank()
                mm_group(b, [(dflat[:, k * 128:(k + 1) * 128], ub[:, c, k:k + T]) for k in range(NCONV)], [r_slot[si], r_ub[c]])
                release_slab()
                E(ACT, lambda h: [h.activation(out=cv[:, c, :], in_=ps[:, b, :], func=AF.Identity,
                                               bias=vecs[:, V_CONVB + c:V_CONVB + c + 1], scale=1.0)],
                  reads=[r_bank[b], r_vecs], writes=[r_cv[c]])
                E(DVE, lambda h: [h.tensor_copy(out=ub[:, c, 0:30], in_=ub[:, c, T:T + 30])], reads=[r_ub[c]], writes=[r_ub[c]])

            yield 'M'
            rnn_pair_a(0)
            for c in range(4):
                yield 'M'
                if c == 2:
                    rnn_pair_a(2)
                conv_chunk(c)

            def ln_stats_and_chain():
                bs = next_bank(hold=True)
                bq = next_bank(hold=True)
                for c in range(4):
                    q = nxt("cvb", 2)
                    E(DVE, lambda h, c=c, q=q: [h.tensor_copy(out=cvb[q][:], in_=cv[:, c, :])], reads=[r_cv[c]], writes=[r_cvb[q]])
                    E(ACT, lambda h, c=c, q=q: [h.activation(out=cvsq[q][:], in_=cv[:, c, :], func=AF.Square)], reads=[r_cv[c]], writes=[r_cvsq[q]])
                    mm_group(bs, [(ones_bf[:], cvb[q][:])], [r_cvb[q], r_const], first=(c == 0), last=(c == 3))
                    mm_group(bq, [(ones_bf[:], cvsq[q][:])], [r_cvsq[q], r_const], first=(c == 0), last=(c == 3))
                ln_chain(bs, bq)

            def ln_chain(bs, bq):
                E(ACT, lambda h: [h.activation(out=meanb[:], in_=ps[:, bs, :], func=AF.Copy, scale=1.0 / 512)],
                  reads=[r_bank[bs]], writes=[r_meanb])
                E(ACT, lambda h: [h.activation(out=varb[:], in_=ps[:, bs, :], func=AF.Square, scale=1.0 / 512)],
                  reads=[r_bank[bs]], writes=[r_varb])
                E(DVE, lambda h: [h.scalar_tensor_tensor(out=varb[:], in0=ps[:, bq, :], scalar=1.0 / 512, in1=varb[:],
                                                         op0=ALU.mult, op1=ALU.subtract)],
                  reads=[r_bank[bq], r_varb], writes=[r_varb])
                E(ACT, lambda h: [h.activation(out=varb[:], in_=varb[:], func=AF.Sqrt, bias=EPS, scale=1.0)],
                  reads=[r_varb], writes=[r_varb])
                E(DVE, lambda h: [h.reciprocal(out=varb[:], in_=varb[:])], reads=[r_varb], writes=[r_varb])
                held.discard(bs); held.discard(bq)
            yield 'M'
            si = next_slab(*d_cols("win", bin_, 1536))
            for c in range(4):
                b = next_bank()
                mm_group(b, [(slots[si][:, kc, c * 128:(c + 1) * 128], hb[:, kc, :]) for kc in range(8)], [r_slot[si]] + r_hb)
                E(ACT, lambda h, c=c, b=b: [h.activation(out=gy[:, c, :], in_=ps[:, b, :], func=AF.Gelu_apprx_tanh)],
                  reads=[r_bank[b]], writes=[r_gy[c]])
            release_slab()
            if hook is not None:
                yield from hook()
            yield 'e'
            ln_stats_and_chain()
            yield 'e'
            for c in range(4):
                rnn_b(c)
            for c in range(4):
                if c % 2 == 0:
                    yield 'e'
                q = nxt("ntmp", 2)
                E(DVE, lambda h, c=c, q=q: [h.tensor_tensor(out=ntmp[q][:], in0=cv[:, c, :], in1=meanb[:], op=ALU.subtract)],
                  reads=[r_cv[c], r_meanb], writes=[r_ntmp[q]])
                E(DVE, lambda h, q=q: [h.tensor_tensor(out=ntmp[q][:], in0=ntmp[q][:], in1=varb[:], op=ALU.mult)],
                  reads=[r_ntmp[q], r_varb], writes=[r_ntmp[q]])
                E(ACT, lambda h, c=c, q=q: [h.activation(out=hb[:, c, :], in_=ntmp[q][:], func=AF.Silu,
                                                       bias=vecs[:, V_LNB + c:V_LNB + c + 1], scale=vecs[:, V_LNG + c:V_LNG + c + 1])],
                  reads=[r_ntmp[q], r_vecs], writes=[r_hb[c]])
            for ch in range(2):
                yield 'M'
                si = next_slab(*d_cols("wout", bout, ch * 512))
                for oc in range(4):
                    fc = 4 * ch + oc
                    b = next_bank()
                    mm_group(b, [(slots[si][:, kc, oc * 128:(oc + 1) * 128], hb[:, kc, :]) for kc in range(8)], [r_slot[si]] + r_hb)
                    E(DVE, lambda h, fc=fc, b=b: [h.scalar_tensor_tensor(
                        out=xT[:, fc, :], in0=ps[:, b, :], scalar=gt2[:, fc:fc + 1], in1=xT[:, fc, :], op0=ALU.mult, op1=ALU.add)],
                        reads=[r_bank[b], r_modv], writes=[r_xT[fc]])
                release_slab()
            mix_done[0] += 1

        def tile_gen(i):
            s = i % 2
            xT = xTs[s]; r_xT = r_xTs[s]; rstd = rstds[s]; r_rstd = r_rstds[s]
            first = (i == 0)
            for fc in range(8):
                if fc % 2 == 0:
                    yield 'e'
                b = next_bank()

                def fn(h, fc=fc, b=b):
                    return [h.transpose(out=ps[:, b, blk * 128:(blk + 1) * 128], in_=xin[:, blk, fc * 128:(fc + 1) * 128],
                                        identity=ident[:]) for blk in range(4)]
                E(PE, fn, reads=[r_xin, r_const], writes=[r_bank[b]])
                E(DVE, lambda h, fc=fc, b=b: [h.tensor_copy(out=xT[:, fc, :], in_=ps[:, b, :])],
                  reads=[r_bank[b]], writes=[r_xT[fc]])
            if i + 1 < n_tiles and not st["dry"]:
                load_x(i + 1)
            if first:
                yield from mod_part(0, 8)
                gs_op(gs1, 8, V_G1)
            yield 'S'
            yield from compute_xsq_and_stats(s)
            yield from norm_apply(s, gs1, sh1)

            def hook1():
                yield from mod_part(8, 12)
                half_op(gth1, 16)
            yield from ffn(s, "w1i", b1i, "w1o", b1o, gth1, hook=hook1 if first else None)
            if first:
                yield from mod_part(12, 20)
                gs_op(gs2, 32, V_GM)
            yield from compute_xsq_and_stats(s)
            yield from norm_apply(s, gs2, sh2)
            yield from mixer(s, hook=(lambda: mod_part(20, 24)) if first else None)
            if first:
                yield from mod_part(24, 32)
                gs_op(gs3, 56, V_G2)
            yield from compute_xsq_and_stats(s)
            yield from norm_apply(s, gs3, sh3)

            def hook2():
                yield from mod_part(32, 36)
                half_op(gth3, 64)
            yield from ffn(s, "w2i", b2i, "w2o", b2o, gth3, hook=hook2 if first else None)
            if first:
                yield from mod_part(36, 44)
                gs_op(gsf, 80, V_GF)
            yield from compute_xsq_and_stats(s)
            def out_block(fc, qo):
                b = next_bank()

                def fn(h, qo=qo, b=b):
                    return [h.transpose(out=ps[:, b, blk * 128:(blk + 1) * 128], in_=oT[qo][:, blk * 128:(blk + 1) * 128],
                                        identity=ident[:]) for blk in range(4)]
                E(PE, fn, reads=[r_oT[qo], r_const], writes=[r_bank[b]])
                k = nxt("oo", 3)
                if fc % 2 == 0:
                    E(DVE, lambda h, k=k, b=b: [h.tensor_copy(out=oo[k][:], in_=ps[:, b, :].rearrange("p (k n) -> p k n", k=4))],
                      reads=[r_bank[b]], writes=[r_oo[k]])
                else:
                    E(ACT, lambda h, k=k, b=b: [h.activation(out=oo[k][:], in_=ps[:, b, :].rearrange("p (k n) -> p k n", k=4), func=AF.Copy)],
                      reads=[r_bank[b]], writes=[r_oo[k]])
                if not st["dry"]:
                    dst = y_d[i * T:(i + 1) * T, fc * 128:(fc + 1) * 128].rearrange("(b p) d -> p b d", p=128)
                    dma1(PL, dst, oo[k][:], [r_oo[k]], [], sem_oo[k])

            prev = None
            for fc in range(8):
                yield 'e'
                q = nxt("ntmp", 2)
                qo = nxt("oT", 3)
                E(DVE, lambda h, q=q, fc=fc: [h.scalar_tensor_tensor(
                    out=ntmp[q][:], in0=xT[:, fc, :], scalar=gsf[:, fc:fc + 1], in1=rstd[:], op0=ALU.mult, op1=ALU.mult)],
                    reads=[r_xT[fc], r_rstd, r_derived], writes=[r_ntmp[q]])
                E(ACT, lambda h, q=q, qo=qo, fc=fc: [h.activation(out=oT[qo][:], in_=ntmp[q][:], func=AF.Identity,
                                                                 bias=shf[:, fc:fc + 1], scale=1.0)],
                  reads=[r_ntmp[q], r_modv], writes=[r_oT[qo]])
                if prev is not None:
                    out_block(*prev)
                prev = (fc, qo)
            yield 'e'
            out_block(*prev)

        def program():
            gens = []
            nxt_tile = [0]

            def step(e):
                cur_tile[0] = e[3]
                e[1] = next(e[0], None)
                while e[1] == 'S':
                    e[2] = True
                    e[1] = next(e[0], None)

            def maybe_spawn():
                while len(gens) < 2 and nxt_tile[0] < n_tiles and (not gens or gens[-1][2]):
                    e = [tile_gen(nxt_tile[0]), None, False, nxt_tile[0]]
                    nxt_tile[0] += 1
                    step(e)
                    gens.append(e)

            def can_own(e):
                return e[1] == 'M' or (e[1] == 'MIX' and mix_done[0] == e[3])

            mix_done[0] = 0
            owner = None
            maybe_spawn()
            while gens:
                if owner is not None and owner[1] in ('M', 'MIX') and owner in gens:
                    step(owner)
                    for o in gens:
                        if o is not owner:
                            for _ in range(2):
                                if o[1] == 'e':
                                    step(o)
                else:
                    owner = None
                    for e in gens:
                        if can_own(e):
                            owner = e
                            break
                    if owner is None:
                        progressed = False
                        for e in gens:
                            if e[1] == 'e':
                                step(e)
                                progressed = True
                        assert progressed, [(e[1], e[3]) for e in gens]
                gens[:] = [e for e in gens if e[1] is not None]
                maybe_spawn()

        program()
        st["dry"] = False
        bank_ctr[0] = 0
        rot.clear()
        held.clear()
        for _ in range(NS):
            issue_slab()
        program()

        final_vals = [(sm.h, sm.val) for sm in sem_oo]
        assert st["used"] == len(plan), (st, len(plan))

        with nc.Block() as block:
            @block.sync
            def _(h):
                replay(SP, h)

            @block.gpsimd
            def _(h):
                replay(PL, h)
                for smh, v in final_vals:
                    if v:
                        h.wait_ge(smh, v)

            @block.tensor
            def _(h):
                replay(PE, h)

            @block.scalar
            def _(h):
                replay(ACT, h)

            @block.vector
            def _(h):
                replay(DVE, h)
    return nc


def prep_inputs(inputs, n_tiles=SEQ // T, cores=N_CORES):
    f = lambda a: np.ascontiguousarray(np.asarray(a, dtype=np.float32))
    col = lambda v, n: f(v).reshape(n, 128).T
    x = f(inputs["x"]); c = f(inputs["c"])
    shared = np.zeros((128, NV), np.float32)
    shared[:, V_G1:V_G1 + 8] = col(inputs["g_ffn1"][0], 8)
    shared[:, V_GM:V_GM + 8] = col(inputs["g_mix"][0], 8)
    shared[:, V_G2:V_G2 + 8] = col(inputs["g_ffn2"][0], 8)
    shared[:, V_GF:V_GF + 8] = col(inputs["g_final"], 8)
    shared[:, V_BMOD:V_BMOD + 72] = col(inputs["b_mod"][0], 72)
    shared[:, V_BMOD + 72:V_BMOD + 88] = col(inputs["b_fmod"], 16)
    shared[:, V_CONVB:V_CONVB + 4] = col(inputs["conv_b"][0], 4)
    shared[:, V_LNG:V_LNG + 4] = col(inputs["ln_g"][0], 4)
    shared[:, V_LNB:V_LNB + 4] = col(inputs["ln_b"][0], 4)
    shared[:, V_RCB:V_RCB + 4] = col(inputs["rnn_conv_b"][0], 4)
    shared[:, V_BA:V_BA + 4] = col(inputs["b_a"][0], 4)
    shared[:, V_BI:V_BI + 4] = col(inputs["b_i"][0], 4)
    shared[:, V_LAM:V_LAM + 4] = col(inputs["lru_lambda"][0], 4)
    rcw = f(inputs["rnn_conv_w"][0])
    for cc in range(4):
        for k in range(4):
            shared[:, V_RCW + 4 * cc + k] = rcw[k, cc * 128:(cc + 1) * 128]
    cw = f(inputs["conv_w"][0])
    dcw = np.zeros((4, 128, NCONV, 128), np.float32)
    ar = np.arange(128)
    for cc in range(4):
        for k in range(NCONV):
            dcw[cc, ar, k, ar] = cw[k, cc * 128:(cc + 1) * 128]
    dcw = dcw.reshape(512, NCONV * 128)
    def bd(w):
        w = f(w)
        o = np.zeros((128, 4, 128), np.float32)
        for cc in range(4):
            o[0:64, cc, 0:64] = w[2 * cc]
            o[64:128, cc, 64:128] = w[2 * cc + 1]
        return o.reshape(128, 512)
    wabd = bd(inputs["w_a"][0]); wibd = bd(inputs["w_i"][0])
    common = {
        "ident": np.eye(128, dtype=np.float32), "wabd": wabd, "wibd": wibd,
        "w_mod": f(inputs["w_mod"][0]), "w_fmod": f(inputs["w_fmod"]),
        "w1i": f(inputs["w_ffn1_in"][0]), "w1o": f(inputs["w_ffn1_out"][0]),
        "win": f(inputs["w_in"][0]), "wout": f(inputs["w_out"][0]),
        "w2i": f(inputs["w_ffn2_in"][0]), "w2o": f(inputs["w_ffn2_out"][0]),
        "dcw": dcw,
    }
    in_maps = []
    S = n_tiles * T
    for b in range(cores):
        v = shared.copy()
        v[:, V_C:V_C + 8] = c[b].reshape(8, 128).T
        m = dict(common)
        m["vecs"] = v
        m["x"] = np.ascontiguousarray(x[b, :S, :])
        in_maps.append(m)
    return in_maps


_NC_CACHE = {}


def kernel(**inputs):
    n_tiles = SEQ // T
    if n_tiles not in _NC_CACHE:
        _NC_CACHE[n_tiles] = build_program(n_tiles)
    nc = _NC_CACHE[n_tiles]
    in_maps = prep_inputs(inputs, n_tiles, N_CORES)
    res = run_bass_kernel_spmd(nc, in_maps, core_ids=list(range(N_CORES)))
    out = np.stack([np.asarray(r["y"], dtype=np.float32) for r in res.results], axis=0)
    return out
```

```python
import numpy as np
import concourse.bass as bass
import concourse.mybir as mybir
from concourse.bass_utils import run_bass_kernel_spmd

F32 = mybir.dt.float32
BF16 = mybir.dt.bfloat16
AF = mybir.ActivationFunctionType
ALU = mybir.AluOpType

D = 1024
DFF = 2816
SEQ = 4096
T = 512
KC = 8
FC = 22
NCONV = 31
EPS = 1e-6
NS = 4
N_CORES = 8

V_G1, V_GM, V_G2, V_GF = 0, 8, 16, 24
V_BMOD = 32
V_CONVB = 120
V_LNG = 124
V_LNB = 128
V_RCB = 132
V_BA = 136
V_BI = 140
V_LAM = 144
V_RCW = 148
V_C = 164
NV = 172


class Sem:
    def __init__(self, h):
        self.h = h
        self.val = 0


class Res:
    __slots__ = ("w", "r")

    def __init__(self):
        self.w = None
        self.r = {}


class Eng:
    def __init__(self, name, sem, self_sync):
        self.name = name
        self.sem = sem
        self.prog = []
        self.seen = {}
        self.self_sync = self_sync


def emit(eng, fn, reads=(), writes=(), dma=None, ndma=1):
    waits = {}

    def need(tok):
        if tok is None:
            return
        s, v = tok
        if waits.get(s, 0) < v:
            waits[s] = v

    for r in reads:
        need(r.w)
    for w in writes:
        need(w.w)
        for s, v in w.r.items():
            need((s, v))
    wl = []
    for s, v in waits.items():
        if s is eng.sem and not eng.self_sync:
            continue
        if eng.seen.get(s, 0) >= v:
            continue
        eng.seen[s] = v
        wl.append((s, v))
    if dma is None:
        eng.sem.val += 1
        tok = (eng.sem, eng.sem.val)
        eng.prog.append((wl, fn, eng.sem, 1, False))
    else:
        dma.val += 16 * ndma
        tok = (dma, dma.val)
        eng.prog.append((wl, fn, dma, 16, True))
    for r in reads:
        if r.r.get(tok[0], 0) < tok[1]:
            r.r[tok[0]] = tok[1]
    for w in writes:
        w.w = tok
        w.r = {}
    return tok


def replay(eng, h):
    for wl, fn, sem, inc, is_dma in eng.prog:
        for s, v in wl:
            h.wait_ge(s.h, v)
        ins = fn(h)
        if is_dma:
            for i in ins:
                i.then_inc(sem.h, 16)
        else:
            ins[-1].then_inc(sem.h, 1)


def build_program(n_tiles=SEQ // T):
    S = n_tiles * T
    nc = bass.Bass("TRN2", target_bir_lowering=False)
    dt_in = lambda name, shape: nc.dram_tensor(name, shape, F32, kind="ExternalInput").ap()
    x_d = dt_in("x", [D, S])
    vecs_d = dt_in("vecs", [128, NV])
    ident_d = dt_in("ident", [128, 128])
    wabd_d = dt_in("wabd", [128, 512])
    wibd_d = dt_in("wibd", [128, 512])
    wmod_d = dt_in("w_mod", [D, 9 * D])
    wfmod_d = dt_in("w_fmod", [D, 2 * D])
    w1i_d = dt_in("w1i", [D, 2 * DFF])
    w1o_d = dt_in("w1o", [DFF, D])
    win_d = dt_in("win", [D, 2 * D])
    wout_d = dt_in("wout", [D, D])
    w2i_d = dt_in("w2i", [D, 2 * DFF])
    w2o_d = dt_in("w2o", [DFF, D])
    dcw_d = dt_in("dcw", [512, NCONV * 128])
    y_d = nc.dram_tensor("y", [D, S], F32, kind="ExternalOutput").ap()
    dt_sc = lambda name, shape: nc.dram_tensor(name, shape, BF16, kind="Internal").ap()
    b1i = dt_sc("b1i", [D, 2 * DFF])
    b1o = dt_sc("b1o", [DFF, D])
    bin_ = dt_sc("bin", [D, 2 * D])
    bout = dt_sc("bout", [D, D])
    b2i = dt_sc("b2i", [D, 2 * DFF])
    b2o = dt_sc("b2o", [DFF, D])
    bdc = dt_sc("bdc", [512, NCONV * 128])

    from contextlib import ExitStack
    es = ExitStack()
    with es:
        sb = lambda name, shape, dt=F32: es.enter_context(nc.sbuf_tensor("sb_" + name, shape, dt))
        newsem = lambda name: Sem(es.enter_context(nc.semaphore(name)))

        PE = Eng("pe", newsem("s_pe"), False)
        ACT = Eng("act", newsem("s_act"), True)
        DVE = Eng("dve", newsem("s_dve"), True)
        SP = Eng("sp", newsem("s_sp"), False)
        PL = Eng("pool", newsem("s_pool"), False)

        ident = sb("ident", [128, 128])
        ones_bf = sb("ones_bf", [128, 128], BF16)
        vecs = sb("vecs", [128, NV])
        modv = sb("modv", [128, 88])
        cact = sb("cact", [128, 8])
        gs1 = sb("gs1", [128, 8]); gth1 = sb("gth1", [128, 8])
        gs2 = sb("gs2", [128, 8])
        gs3 = sb("gs3", [128, 8]); gth3 = sb("gth3", [128, 8])
        gsf = sb("gsf", [128, 8])
        lam8 = sb("lam8", [128, 4]); lam16 = sb("lam16", [128, 4]); lamt = sb("lamt", [128, 4])
        wabd = sb("wabd", [128, 4, 128], BF16)
        wibd = sb("wibd", [128, 4, 128], BF16)
        hstate = sb("hstate", [128, 4])
        slots = [sb(f"slab{i}", [128, 8, 512], BF16) for i in range(NS)]
        xTs = [sb(f"xT{k}", [128, 8, T]) for k in range(2)]
        xsq = [sb(f"xsq{i}", [128, T], BF16) for i in range(4)]
        hbs = [sb(f"hb{k}", [128, 8, T], BF16) for k in range(2)]
        hid = sb("hid", [128, FC, T], BF16)
        sgb = [sb(f"sgb{i}", [128, T]) for i in range(2)]
        ntmp = [sb(f"ntmp{i}", [128, T]) for i in range(2)]
        rstds = [sb(f"rstd{k}", [128, T]) for k in range(2)]
        ub = sb("ub", [128, 4, T + 30], BF16)
        ux = sb("ux", [128, 4, T + 3])
        gy = sb("gy", [128, 4, T], BF16)
        cv = sb("cv", [128, 4, T])
        cvsq = [sb(f"cvsq{i}", [128, T], BF16) for i in range(2)]
        cvb = [sb(f"cvb{i}", [128, T], BF16) for i in range(2)]
        meanb = sb("meanb", [128, T])
        varb = sb("varb", [128, T])
        xr = [sb(f"xr{i}", [128, T]) for i in range(4)]
        xrb = [sb(f"xrb{i}", [128, T], BF16) for i in range(4)]
        abuf = [sb(f"abuf{i}", [128, T]) for i in range(2)]
        mbuf = [sb(f"mbuf{i}", [128, T]) for i in range(2)]
        btb = [sb(f"btb{i}", [128, T]) for i in range(2)]
        hs = [sb(f"hs{i}", [128, T]) for i in range(4)]
        oT = [sb(f"oT{i}", [128, T]) for i in range(3)]
        ps = es.enter_context(nc.psum_tensor("ps", [128, 8, T], F32))

        R = lambda: Res()
        r_const = R()
        r_vecs = R(); r_modv = R(); r_cact = R(); r_derived = R(); r_hstate = [R() for _ in range(4)]
        r_gates = R()
        r_slot = [R() for _ in range(NS)]
        sem_slot = [newsem(f"s_slot{i}") for i in range(NS)]
        r_bank = [R() for _ in range(8)]
        sem_xin = [newsem("s_xin0"), newsem("s_xin1")]
        r_xTs = [[R() for _ in range(8)] for _ in range(2)]
        r_xsq = [R() for _ in range(4)]
        r_hbs = [[R() for _ in range(8)] for _ in range(2)]
        r_hid = [R() for _ in range(FC)]
        r_sgb = [R() for _ in range(2)]
        r_ntmp = [R() for _ in range(2)]
        r_rstds = [R(), R()]
        r_ub = [R() for _ in range(4)]
        r_ux = [R() for _ in range(4)]
        r_gy = [R() for _ in range(4)]
        r_cv = [R() for _ in range(4)]
        r_cvsq = [R() for _ in range(2)]
        r_cvb = [R() for _ in range(2)]
        r_meanb = R(); r_varb = R()
        r_xr = [R() for _ in range(4)]; r_xrb = [R() for _ in range(4)]
        r_abuf = [R() for _ in range(2)]
        r_mbuf = [R() for _ in range(2)]; r_btb = [R() for _ in range(2)]
        r_hs = [R() for _ in range(4)]
        r_oT = [R() for _ in range(3)]
        sem_oo = [newsem(f"s_oo{k}") for k in range(3)]
        sem_const = newsem("s_const")
        r_scr = {}
        sem_wb = [newsem(f"s_wb{i}") for i in range(NS)]
        sem_slot_pl = [newsem(f"s_slotp{i}") for i in range(NS)]

        bank_ctr = [0]

        held = set()

        def next_bank(hold=False):
            while True:
                b = bank_ctr[0] % 8
                bank_ctr[0] += 1
                if b not in held:
                    break
            if hold:
                held.add(b)
            return b

        rot = {}

        def nxt(name, n):
            v = rot.get(name, 0)
            rot[name] = v + 1
            return v % n

        def dma1(eng, out_ap, in_ap, reads, writes, sem):
            return emit(eng, lambda h: [h.dma_start(out=out_ap, in_=in_ap)], reads=reads, writes=writes, dma=sem)

        dma1(SP, vecs[:], vecs_d[:, :], [], [r_vecs], sem_const)
        dma1(SP, ident[:], ident_d[:, :], [], [r_const], newsem("s_ident"))
        def load_x(i):
            src = x_d[:, i * T:(i + 1) * T].rearrange("(fc p) t -> p fc t", p=128)
            dma1(PL, xTs[i % 2][:], src, [], r_xTs[i % 2], sem_xin[i % 2])
        load_x(0)
        sem_g = newsem("s_gates")
        dma1(PL, wabd[:], wabd_d[:, :].rearrange("p (c n) -> p c n", c=4), [], [r_gates], sem_g)
        dma1(PL, wibd[:], wibd_d[:, :].rearrange("p (c n) -> p c n", c=4), [], [r_gates], sem_g)

        emit(DVE, lambda h: [h.memset(ones_bf[:], 1.0)], writes=[r_const])
        emit(DVE, lambda h: [h.memset(hstate[:], 0.0)], writes=r_hstate)
        emit(DVE, lambda h: [h.memset(ub[:, :, 0:30], 0.0)], writes=r_ub)
        emit(DVE, lambda h: [h.memset(ux[:, :, 0:3], 0.0)], writes=r_ux)
        emit(ACT, lambda h: [h.activation(out=cact[:], in_=vecs[:, V_C:V_C + 8], func=AF.Silu)],
             reads=[r_vecs], writes=[r_cact])
        emit(ACT, lambda h: [h.activation(out=lamt[:], in_=vecs[:, V_LAM:V_LAM + 4], func=AF.Exp, scale=-1.0)],
             reads=[r_vecs], writes=[r_derived])
        emit(ACT, lambda h: [h.activation(out=lamt[:], in_=lamt[:], func=AF.Ln, bias=1.0)],
             reads=[r_derived], writes=[r_derived])
        emit(DVE, lambda h: [h.tensor_scalar(out=lam8[:], in0=lamt[:], scalar1=-8.0, scalar2=None, op0=ALU.mult)],
             reads=[r_derived], writes=[r_derived])
        emit(DVE, lambda h: [h.tensor_scalar(out=lam16[:], in0=lamt[:], scalar1=-16.0, scalar2=None, op0=ALU.mult)],
             reads=[r_derived], writes=[r_derived])

        kview = lambda ap: ap.rearrange("(kc p) n -> p kc n", p=128)
        plan = []
        st = {"issued": 0, "used": 0, "dry": True}

        cur_tile = [0]
        pos_ctr = {}

        def issue_slab():
            k = st["issued"]
            if k >= len(plan):
                return
            st["issued"] += 1
            si = k % NS
            cname, pfn, tile, pos = plan[k]

            def mk(pieces):
                return lambda h: [h.dma_start(out=o, in_=i) for o, i in pieces]
            if cname is None:
                pieces = pfn(si, 0)
                emit(SP, mk(pieces), writes=[r_slot[si]], dma=sem_slot[si], ndma=len(pieces))
            elif tile == 0:
                pieces = pfn(si, 1)
                emit(PL, mk(pieces), writes=[r_slot[si]], dma=sem_slot_pl[si], ndma=len(pieces))
                back = [(d, o) for o, d in pfn(si, 0)]
                r_scr[pos] = Res()
                emit(SP, mk(back), reads=[r_slot[si]], writes=[r_scr[pos]], dma=sem_wb[si], ndma=len(back))
            else:
                pieces = pfn(si, 0)
                emit(SP, mk(pieces), reads=[r_scr[pos]], writes=[r_slot[si]], dma=sem_slot[si], ndma=len(pieces))

        def next_slab(cname, pfn):
            if st["dry"]:
                t = cur_tile[0]
                pos = None
                if cname is not None:
                    pos = pos_ctr.get(t, 0)
                    pos_ctr[t] = pos + 1
                plan.append((cname, pfn, t, pos))
                return 0
            k = st["used"]
            st["used"] += 1
            assert k < st["issued"] and plan[k][0] == cname
            return k % NS

        def release_slab():
            if not st["dry"]:
                issue_slab()

        def E(eng, fn, reads=(), writes=()):
            if st["dry"]:
                return None
            return emit(eng, fn, reads=reads, writes=writes)

        def d_mod(g):
            src = wmod_d if g < 36 else wfmod_d
            gg = g if g < 36 else g - 36
            return (None, lambda si, f: [(slots[si][:].bitcast(F32), kview(src[:, gg * 256:(gg + 1) * 256]))])

        WT = {"w1i": (b1i, w1i_d), "w1o": (b1o, w1o_d), "win": (bin_, win_d), "wout": (bout, wout_d),
              "w2i": (b2i, w2i_d), "w2o": (b2o, w2o_d), "dcw": (bdc, dcw_d)}

        def d_ffn_in(nm, wi_unused, j):
            return (nm, lambda si, f: [
                (slots[si][:, :, 0:256], kview(WT[nm][f][:, 256 * j:256 * j + 256])),
                (slots[si][:, :, 256:512], kview(WT[nm][f][:, DFF + 256 * j:DFF + 256 * j + 256]))])

        def d_ffn_out(nm, wo_unused, ch, k0, nk):
            return (nm, lambda si, f: [(slots[si][:, 0:nk, :], kview(WT[nm][f][k0 * 128:(k0 + nk) * 128, ch * 512:(ch + 1) * 512]))])

        def d_cols(nm, w_unused, c0):
            return (nm, lambda si, f: [(slots[si][:, :, :], kview(WT[nm][f][:, c0:c0 + 512]))])

        def d_glu(half):
            return ("win", lambda si, f: [
                (slots[si][:, :, 0:256], kview(WT["win"][f][:, 256 * half:256 * half + 256])),
                (slots[si][:, :, 256:512], kview(WT["win"][f][:, 512 + 256 * half:512 + 256 * half + 256]))])

        def d_conv(c):
            return ("dcw", lambda si, f: [
                (slots[si][:].rearrange("p k n -> p (k n)")[:, 0:NCONV * 128], WT["dcw"][f][c * 128:(c + 1) * 128, :])])

        def mm_group(bank, pairs, reads, first=True, last=True):
            n = len(pairs)

            def fn(h):
                out = []
                for idx, (l, r) in enumerate(pairs):
                    out.append(h.matmul(ps[:, bank, :], lhsT=l, rhs=r,
                                        start=(first and idx == 0), stop=(last and idx == n - 1)))
                return out
            return E(PE, fn, reads=reads, writes=[r_bank[bank]])

        sh1 = modv[:, 0:8]; sh2 = modv[:, 24:32]; gt2 = modv[:, 40:48]; sh3 = modv[:, 48:56]; shf = modv[:, 72:80]

        def mod_part(g0, g1):
            bank = next_bank(hold=True)
            for g in range(g0, g1):
                si = next_slab(*d_mod(g))
                sf = slots[si][:].bitcast(F32)

                def fn(h, g=g, sf=sf):
                    out = []
                    for sub in range(2):
                        j = 2 * g + sub
                        for kc in range(8):
                            out.append(h.matmul(ps[:, bank, j:j + 1], lhsT=sf[:, kc, sub * 128:(sub + 1) * 128],
                                                rhs=cact[:, kc:kc + 1], start=(kc == 0), stop=(kc == 7)))
                    return out
                E(PE, fn, reads=[r_slot[si], r_cact], writes=[r_bank[bank]])
                release_slab()
                yield 'M'
            c0, c1 = 2 * g0, 2 * g1
            E(DVE, lambda h: [h.tensor_tensor(out=modv[:, c0:c1], in0=ps[:, bank, c0:c1], in1=vecs[:, V_BMOD + c0:V_BMOD + c1], op=ALU.add)],
              reads=[r_bank[bank], r_vecs], writes=[r_modv])
            held.discard(bank)

        def gs_op(dst, sc_col, g_col):
            E(DVE, lambda h: [h.scalar_tensor_tensor(out=dst[:], in0=modv[:, sc_col:sc_col + 8], scalar=1.0,
                                                     in1=vecs[:, g_col:g_col + 8], op0=ALU.add, op1=ALU.mult)],
              reads=[r_modv, r_vecs], writes=[r_derived])

        def half_op(dst, col):
            E(DVE, lambda h: [h.tensor_scalar(out=dst[:], in0=modv[:, col:col + 8], scalar1=0.5, scalar2=None, op0=ALU.mult)],
              reads=[r_modv], writes=[r_derived])

        def rsqrt_chain(buf, res, bank, scale):
            E(ACT, lambda h: [h.activation(out=buf[:], in_=ps[:, bank, :], func=AF.Sqrt, bias=EPS, scale=scale)],
              reads=[r_bank[bank]], writes=[res])
            E(DVE, lambda h: [h.reciprocal(out=buf[:], in_=buf[:])], reads=[res], writes=[res])

        def compute_xsq_and_stats(s):
            xT = xTs[s]; r_xT = r_xTs[s]; rstd = rstds[s]; r_rstd = r_rstds[s]
            yield 'e'
            sbank = next_bank(hold=True)
            pend = []

            def flush():
                for (q, fc) in pend:
                    mm_group(sbank, [(ones_bf[:], xsq[q][:])], [r_xsq[q], r_const], first=(fc == 0), last=(fc == 7))
                del pend[:]
            for fc in range(8):
                if fc % 2 == 0:
                    flush()
                q = nxt("xsq", 4)
                E(ACT, lambda h, q=q, fc=fc: [h.activation(out=xsq[q][:], in_=xT[:, fc, :], func=AF.Square)],
                  reads=[r_xT[fc]], writes=[r_xsq[q]])
                pend.append((q, fc))
                if fc % 2 == 1:
                    yield 'e'
            flush()
            rsqrt_chain(rstd, r_rstd, sbank, 1.0 / D)
            held.discard(sbank)

        def norm_apply(s, gs, sh):
            xT = xTs[s]; r_xT = r_xTs[s]; rstd = rstds[s]; r_rstd = r_rstds[s]; hb = hbs[s]; r_hb = r_hbs[s]
            for fc in range(8):
                if fc % 2 == 0:
                    yield 'e'
                q = nxt("ntmp", 2)
                E(DVE, lambda h, q=q, fc=fc: [h.scalar_tensor_tensor(
                    out=ntmp[q][:], in0=xT[:, fc, :], scalar=gs[:, fc:fc + 1], in1=rstd[:], op0=ALU.mult, op1=ALU.mult)],
                    reads=[r_xT[fc], r_rstd, r_derived], writes=[r_ntmp[q]])
                E(ACT, lambda h, q=q, fc=fc: [h.activation(out=hb[:, fc, :], in_=ntmp[q][:], func=AF.Identity,
                                                           bias=sh[:, fc:fc + 1], scale=1.0)],
                  reads=[r_ntmp[q], r_modv], writes=[r_hb[fc]])

        def ffn(s, nm_i, wi, nm_o, wo, gth, hook=None):
            xT = xTs[s]; r_xT = r_xTs[s]; hb = hbs[s]; r_hb = r_hbs[s]
            for j in range(11):
                yield 'M'
                si = next_slab(*d_ffn_in(nm_i, wi, j))
                for sub in range(2):
                    cg = 2 * j + sub
                    bg = next_bank()
                    mm_group(bg, [(slots[si][:, kc, sub * 128:(sub + 1) * 128], hb[:, kc, :]) for kc in range(8)],
                             [r_slot[si]] + r_hb)
                    bu = next_bank()
                    mm_group(bu, [(slots[si][:, kc, 256 + sub * 128:256 + (sub + 1) * 128], hb[:, kc, :]) for kc in range(8)],
                             [r_slot[si]] + r_hb)
                    q = nxt("sgb", 2)
                    E(ACT, lambda h, q=q, bg=bg: [h.activation(out=sgb[q][:], in_=ps[:, bg, :], func=AF.Silu)],
                      reads=[r_bank[bg]], writes=[r_sgb[q]])
                    E(DVE, lambda h, q=q, bu=bu, cg=cg: [h.tensor_tensor(out=hid[:, cg, :], in0=ps[:, bu, :], in1=sgb[q][:], op=ALU.mult)],
                      reads=[r_bank[bu], r_sgb[q]], writes=[r_hid[cg]])
                release_slab()
            if hook is not None:
                yield from hook()
            for ch in range(2):
                yield 'M'
                banks = [next_bank(hold=True) for _ in range(4)]
                for (k0, nk) in ((0, 8), (8, 8), (16, 6)):
                    si = next_slab(*d_ffn_out(nm_o, wo, ch, k0, nk))
                    for oc in range(4):
                        mm_group(banks[oc],
                                 [(slots[si][:, kl, oc * 128:(oc + 1) * 128], hid[:, k0 + kl, :]) for kl in range(nk)],
                                 [r_slot[si]] + r_hid[k0:k0 + nk], first=(k0 == 0), last=(k0 == 16))
                    release_slab()
                for oc in range(4):
                    fc = 4 * ch + oc
                    E(DVE, lambda h, fc=fc, b=banks[oc]: [h.scalar_tensor_tensor(
                        out=xT[:, fc, :], in0=ps[:, b, :], scalar=gth[:, fc:fc + 1], in1=xT[:, fc, :], op0=ALU.mult, op1=ALU.add)],
                        reads=[r_bank[banks[oc]], r_derived, r_modv], writes=[r_xT[fc]])
                for b in banks:
                    held.discard(b)

        mix_done = [0]

        def mixer(s, hook=None):
            xT = xTs[s]; r_xT = r_xTs[s]; hb = hbs[s]; r_hb = r_hbs[s]
            yield 'MIX'
            si = next_slab(*d_cols("win", bin_, 1024))
            for c in range(4):
                b = next_bank()
                mm_group(b, [(slots[si][:, kc, c * 128:(c + 1) * 128], hb[:, kc, :]) for kc in range(8)], [r_slot[si]] + r_hb)
                E(ACT, lambda h, c=c, b=b: [h.activation(out=ux[:, c, 3:3 + T], in_=ps[:, b, :], func=AF.Copy)],
                  reads=[r_bank[b]], writes=[r_ux[c]])
            release_slab()
            for c in range(4):
                E(DVE, lambda h, c=c: [h.tensor_scalar(
                    out=xr[c][:], in0=ux[:, c, 0:T], scalar1=vecs[:, V_RCW + 4 * c:V_RCW + 4 * c + 1],
                    scalar2=vecs[:, V_RCB + c:V_RCB + c + 1], op0=ALU.mult, op1=ALU.add)],
                    reads=[r_ux[c], r_vecs], writes=[r_xr[c]])
                for k in range(1, 4):
                    E(DVE, lambda h, c=c, k=k: [h.scalar_tensor_tensor(
                        out=xr[c][:], in0=ux[:, c, k:k + T], scalar=vecs[:, V_RCW + 4 * c + k:V_RCW + 4 * c + k + 1],
                        in1=xr[c][:], op0=ALU.mult, op1=ALU.add)],
                        reads=[r_ux[c], r_vecs, r_xr[c]], writes=[r_xr[c]])
                E(DVE, lambda h, c=c: [h.tensor_copy(out=ux[:, c, 0:3], in_=ux[:, c, T:T + 3])], reads=[r_ux[c]], writes=[r_ux[c]])
                E(DVE, lambda h, c=c: [h.tensor_copy(out=xrb[c][:], in_=xr[c][:])],
                  reads=[r_xr[c]], writes=[r_xrb[c]])
            for half in range(2):
                yield 'M'
                si = next_slab(*d_glu(half))
                for sub in range(2):
                    c = 2 * half + sub
                    bv = next_bank()
                    mm_group(bv, [(slots[si][:, kc, sub * 128:(sub + 1) * 128], hb[:, kc, :]) for kc in range(8)], [r_slot[si]] + r_hb)
                    bg = next_bank()
                    mm_group(bg, [(slots[si][:, kc, 256 + sub * 128:256 + (sub + 1) * 128], hb[:, kc, :]) for kc in range(8)], [r_slot[si]] + r_hb)
                    q = nxt("sgb", 2)
                    E(ACT, lambda h, q=q, bg=bg: [h.activation(out=sgb[q][:], in_=ps[:, bg, :], func=AF.Sigmoid)],
                      reads=[r_bank[bg]], writes=[r_sgb[q]])
                    E(DVE, lambda h, q=q, bv=bv, c=c: [h.tensor_tensor(out=ub[:, c, 30:30 + T], in0=ps[:, bv, :], in1=sgb[q][:], op=ALU.mult)],
                      reads=[r_bank[bv], r_sgb[q]], writes=[r_ub[c]])
                release_slab()

            hs_of = {}

            def rnn_pair_a(c0):
                cs = (c0, c0 + 1)
                qx = {c: c for c in cs}
                br = {}; bi = {}
                for c in cs:
                    br[c] = next_bank()
                    mm_group(br[c], [(wabd[:, c, :], xrb[c][:])], [r_gates, r_xrb[c]])
                    bi[c] = next_bank()
                    mm_group(bi[c], [(wibd[:, c, :], xrb[c][:])], [r_gates, r_xrb[c]])
                qx = {c: c % 2 for c in cs}
                for c in cs:
                    q = qx[c]
                    E(ACT, lambda h, c=c, q=q: [h.activation(out=abuf[q][:], in_=ps[:, br[c], :], func=AF.Sigmoid,
                                                           bias=vecs[:, V_BA + c:V_BA + c + 1], scale=1.0)],
                      reads=[r_bank[br[c]], r_vecs], writes=[r_abuf[q]])
                    E(ACT, lambda h, c=c, q=q: [h.activation(out=btb[q][:], in_=ps[:, bi[c], :], func=AF.Sigmoid,
                                                           bias=vecs[:, V_BI + c:V_BI + c + 1], scale=1.0)],
                      reads=[r_bank[bi[c]], r_vecs], writes=[r_btb[q]])
                for c in cs:
                    q = qx[c]
                    E(ACT, lambda h, c=c, q=q: [h.activation(out=mbuf[q][:], in_=abuf[q][:], func=AF.Exp, scale=lam16[:, c:c + 1])],
                      reads=[r_abuf[q], r_derived], writes=[r_mbuf[q]])
                    E(ACT, lambda h, c=c, q=q: [h.activation(out=abuf[q][:], in_=abuf[q][:], func=AF.Exp, scale=lam8[:, c:c + 1])],
                      reads=[r_abuf[q], r_derived], writes=[r_abuf[q]])
                for c in cs:
                    q = qx[c]
                    E(DVE, lambda h, q=q: [h.tensor_scalar(out=mbuf[q][:], in0=mbuf[q][:], scalar1=1.0, scalar2=-1.0, op0=ALU.min, op1=ALU.mult)],
                      reads=[r_mbuf[q]], writes=[r_mbuf[q]])
                    E(DVE, lambda h, q=q, c=c: [h.tensor_tensor(out=btb[q][:], in0=btb[q][:], in1=xr[c][:], op=ALU.mult)],
                      reads=[r_btb[q], r_xr[c]], writes=[r_btb[q]])
                for c in cs:
                    q = qx[c]
                    E(ACT, lambda h, q=q: [h.activation(out=mbuf[q][:], in_=mbuf[q][:], func=AF.Sqrt, bias=1.0, scale=1.0)],
                      reads=[r_mbuf[q]], writes=[r_mbuf[q]])
                for c in cs:
                    q = qx[c]
                    E(DVE, lambda h, q=q: [h.tensor_tensor(out=btb[q][:], in0=btb[q][:], in1=mbuf[q][:], op=ALU.mult)],
                      reads=[r_btb[q], r_mbuf[q]], writes=[r_btb[q]])
                    E(DVE, lambda h, c=c, q=q: [h.tensor_tensor_scan(out=hs[c][:], data0=abuf[q][:], data1=btb[q][:],
                                                                     initial=hstate[:, c:c + 1], op0=ALU.mult, op1=ALU.add)],
                      reads=[r_abuf[q], r_btb[q], r_hstate[c]], writes=[r_hs[c]])
                    E(DVE, lambda h, c=c: [h.tensor_copy(out=hstate[:, c:c + 1], in_=hs[c][:, T - 1:T])],
                      reads=[r_hs[c]], writes=[r_hstate[c]])

            def rnn_b(c):
                E(DVE, lambda h, c=c: [h.tensor_tensor(out=hb[:, 4 + c, :], in0=hs[c][:], in1=gy[:, c, :], op=ALU.mult)],
                  reads=[r_hs[c], r_gy[c]], writes=[r_hb[4 + c]])

            def conv_chunk(c):
                si = next_slab(*d_conv(c))
                dflat = slots[si][:].rearrange("p k n -> p (k n)")
                b = next_bank()
                mm_group(b, [(dflat[:, k * 128:(k + 1) * 128], ub[:, c, k:k + T]) for k in range(NCONV)], [r_slot[si], r_ub[c]])
                release_slab()
                E(ACT, lambda h: [h.activation(out=cv[:, c, :], in_=ps[:, b, :], func=AF.Identity,
                                               bias=vecs[:, V_CONVB + c:V_CONVB + c + 1], scale=1.0)],
                  reads=[r_bank[b], r_vecs], writes=[r_cv[c]])
                E(DVE, lambda h: [h.tensor_copy(out=ub[:, c, 0:30], in_=ub[:, c, T:T + 30])], reads=[r_ub[c]], writes=[r_ub[c]])

            yield 'M'
            rnn_pair_a(0)
            for c in range(4):
                yield 'M'
                if c == 2:
                    rnn_pair_a(2)
                conv_chunk(c)

            def ln_stats_and_chain():
                bs = next_bank(hold=True)
                bq = next_bank(hold=True)
                for c in range(4):
                    q = nxt("cvb", 2)
                    E(DVE, lambda h, c=c, q=q: [h.tensor_copy(out=cvb[q][:], in_=cv[:, c, :])], reads=[r_cv[c]], writes=[r_cvb[q]])
                    E(ACT, lambda h, c=c, q=q: [h.activation(out=cvsq[q][:], in_=cv[:, c, :], func=AF.Square)], reads=[r_cv[c]], writes=[r_cvsq[q]])
                    mm_group(bs, [(ones_bf[:], cvb[q][:])], [r_cvb[q], r_const], first=(c == 0), last=(c == 3))
                    mm_group(bq, [(ones_bf[:], cvsq[q][:])], [r_cvsq[q], r_const], first=(c == 0), last=(c == 3))
                ln_chain(bs, bq)

            def ln_chain(bs, bq):
                E(ACT, lambda h: [h.activation(out=meanb[:], in_=ps[:, bs, :], func=AF.Copy, scale=1.0 / 512)],
                  reads=[r_bank[bs]], writes=[r_meanb])
                E(ACT, lambda h: [h.activation(out=varb[:], in_=ps[:, bs, :], func=AF.Square, scale=1.0 / 512)],
                  reads=[r_bank[bs]], writes=[r_varb])
                E(DVE, lambda h: [h.scalar_tensor_tensor(out=varb[:], in0=ps[:, bq, :], scalar=1.0 / 512, in1=varb[:],
                                                         op0=ALU.mult, op1=ALU.subtract)],
                  reads=[r_bank[bq], r_varb], writes=[r_varb])
                E(ACT, lambda h: [h.activation(out=varb[:], in_=varb[:], func=AF.Sqrt, bias=EPS, scale=1.0)],
                  reads=[r_varb], writes=[r_varb])
                E(DVE, lambda h: [h.reciprocal(out=varb[:], in_=varb[:])], reads=[r_varb], writes=[r_varb])
                held.discard(bs); held.discard(bq)
            yield 'M'
            si = next_slab(*d_cols("win", bin_, 1536))
            for c in range(4):
                b = next_bank()
                mm_group(b, [(slots[si][:, kc, c * 128:(c + 1) * 128], hb[:, kc, :]) for kc in range(8)], [r_slot[si]] + r_hb)
                E(ACT, lambda h, c=c, b=b: [h.activation(out=gy[:, c, :], in_=ps[:, b, :], func=AF.Gelu_apprx_tanh)],
                  reads=[r_bank[b]], writes=[r_gy[c]])
            release_slab()
            if hook is not None:
                yield from hook()
            yield 'e'
            ln_stats_and_chain()
            yield 'e'
            for c in range(4):
                rnn_b(c)
            for c in range(4):
                if c % 2 == 0:
                    yield 'e'
                q = nxt("ntmp", 2)
                E(DVE, lambda h, c=c, q=q: [h.tensor_tensor(out=ntmp[q][:], in0=cv[:, c, :], in1=meanb[:], op=ALU.subtract)],
                  reads=[r_cv[c], r_meanb], writes=[r_ntmp[q]])
                E(DVE, lambda h, q=q: [h.tensor_tensor(out=ntmp[q][:], in0=ntmp[q][:], in1=varb[:], op=ALU.mult)],
                  reads=[r_ntmp[q], r_varb], writes=[r_ntmp[q]])
                E(ACT, lambda h, c=c, q=q: [h.activation(out=hb[:, c, :], in_=ntmp[q][:], func=AF.Silu,
                                                       bias=vecs[:, V_LNB + c:V_LNB + c + 1], scale=vecs[:, V_LNG + c:V_LNG + c + 1])],
                  reads=[r_ntmp[q], r_vecs], writes=[r_hb[c]])
            for ch in range(2):
                yield 'M'
                si = next_slab(*d_cols("wout", bout, ch * 512))
                for oc in range(4):
                    fc = 4 * ch + oc
                    b = next_bank()
                    mm_group(b, [(slots[si][:, kc, oc * 128:(oc + 1) * 128], hb[:, kc, :]) for kc in range(8)], [r_slot[si]] + r_hb)
                    E(DVE, lambda h, fc=fc, b=b: [h.scalar_tensor_tensor(
                        out=xT[:, fc, :], in0=ps[:, b, :], scalar=gt2[:, fc:fc + 1], in1=xT[:, fc, :], op0=ALU.mult, op1=ALU.add)],
                        reads=[r_bank[b], r_modv], writes=[r_xT[fc]])
                release_slab()
            mix_done[0] += 1

        def tile_gen(i):
            s = i % 2
            xT = xTs[s]; r_xT = r_xTs[s]; rstd = rstds[s]; r_rstd = r_rstds[s]
            first = (i == 0)
            yield 'e'
            if i > 0 and not st["dry"]:
                load_x(i)
            if first:
                yield from mod_part(0, 8)
                gs_op(gs1, 8, V_G1)
            yield 'S'
            yield from compute_xsq_and_stats(s)
            yield from norm_apply(s, gs1, sh1)

            def hook1():
                yield from mod_part(8, 12)
                half_op(gth1, 16)
            yield from ffn(s, "w1i", b1i, "w1o", b1o, gth1, hook=hook1 if first else None)
            if first:
                yield from mod_part(12, 20)
                gs_op(gs2, 32, V_GM)
            yield from compute_xsq_and_stats(s)
            yield from norm_apply(s, gs2, sh2)
            yield from mixer(s, hook=(lambda: mod_part(20, 24)) if first else None)
            if first:
                yield from mod_part(24, 32)
                gs_op(gs3, 56, V_G2)
            yield from compute_xsq_and_stats(s)
            yield from norm_apply(s, gs3, sh3)

            def hook2():
                yield from mod_part(32, 36)
                half_op(gth3, 64)
            yield from ffn(s, "w2i", b2i, "w2o", b2o, gth3, hook=hook2 if first else None)
            if first:
                yield from mod_part(36, 44)
                gs_op(gsf, 80, V_GF)
            yield from compute_xsq_and_stats(s)
            for fc in range(8):
                yield 'e'
                q = nxt("ntmp", 2)
                qo = nxt("oT", 3)
                E(DVE, lambda h, q=q, fc=fc: [h.scalar_tensor_tensor(
                    out=ntmp[q][:], in0=xT[:, fc, :], scalar=gsf[:, fc:fc + 1], in1=rstd[:], op0=ALU.mult, op1=ALU.mult)],
                    reads=[r_xT[fc], r_rstd, r_derived], writes=[r_ntmp[q]])
                E(ACT, lambda h, q=q, qo=qo, fc=fc: [h.activation(out=oT[qo][:], in_=ntmp[q][:], func=AF.Identity,
                                                                 bias=shf[:, fc:fc + 1], scale=1.0)],
                  reads=[r_ntmp[q], r_modv], writes=[r_oT[qo]])
                if not st["dry"]:
                    dma1(PL, y_d[fc * 128:(fc + 1) * 128, i * T:(i + 1) * T], oT[qo][:], [r_oT[qo]], [], sem_oo[qo])

        def program():
            gens = []
            nxt_tile = [0]

            def step(e):
                cur_tile[0] = e[3]
                e[1] = next(e[0], None)
                while e[1] == 'S':
                    e[2] = True
                    e[1] = next(e[0], None)

            def maybe_spawn():
                while len(gens) < 2 and nxt_tile[0] < n_tiles and (not gens or gens[-1][2]):
                    e = [tile_gen(nxt_tile[0]), None, False, nxt_tile[0]]
                    nxt_tile[0] += 1
                    step(e)
                    gens.append(e)

            def can_own(e):
                return e[1] == 'M' or (e[1] == 'MIX' and mix_done[0] == e[3])

            mix_done[0] = 0
            owner = None
            maybe_spawn()
            while gens:
                if owner is not None and owner[1] in ('M', 'MIX') and owner in gens:
                    step(owner)
                    for o in gens:
                        if o is not owner:
                            for _ in range(2):
                                if o[1] == 'e':
                                    step(o)
                else:
                    owner = None
                    for e in gens:
                        if can_own(e):
                            owner = e
                            break
                    if owner is None:
                        progressed = False
                        for e in gens:
                            if e[1] == 'e':
                                step(e)
                                progressed = True
                        assert progressed, [(e[1], e[3]) for e in gens]
                gens[:] = [e for e in gens if e[1] is not None]
                maybe_spawn()

        program()
        st["dry"] = False
        bank_ctr[0] = 0
        rot.clear()
        held.clear()
        for _ in range(NS):
            issue_slab()
        program()

        final_vals = [(sm.h, sm.val) for sm in sem_oo]
        assert st["used"] == len(plan), (st, len(plan))

        with nc.Block() as block:
            @block.sync
            def _(h):
                replay(SP, h)

            @block.gpsimd
            def _(h):
                replay(PL, h)
                for smh, v in final_vals:
                    if v:
                        h.wait_ge(smh, v)

            @block.tensor
            def _(h):
                replay(PE, h)

            @block.scalar
            def _(h):
                replay(ACT, h)

            @block.vector
            def _(h):
                replay(DVE, h)
    return nc


def prep_inputs(inputs, n_tiles=SEQ // T, cores=N_CORES):
    f = lambda a: np.ascontiguousarray(np.asarray(a, dtype=np.float32))
    col = lambda v, n: f(v).reshape(n, 128).T
    x = f(inputs["x"]); c = f(inputs["c"])
    shared = np.zeros((128, NV), np.float32)
    shared[:, V_G1:V_G1 + 8] = col(inputs["g_ffn1"][0], 8)
    shared[:, V_GM:V_GM + 8] = col(inputs["g_mix"][0], 8)
    shared[:, V_G2:V_G2 + 8] = col(inputs["g_ffn2"][0], 8)
    shared[:, V_GF:V_GF + 8] = col(inputs["g_final"], 8)
    shared[:, V_BMOD:V_BMOD + 72] = col(inputs["b_mod"][0], 72)
    shared[:, V_BMOD + 72:V_BMOD + 88] = col(inputs["b_fmod"], 16)
    shared[:, V_CONVB:V_CONVB + 4] = col(inputs["conv_b"][0], 4)
    shared[:, V_LNG:V_LNG + 4] = col(inputs["ln_g"][0], 4)
    shared[:, V_LNB:V_LNB + 4] = col(inputs["ln_b"][0], 4)
    shared[:, V_RCB:V_RCB + 4] = col(inputs["rnn_conv_b"][0], 4)
    shared[:, V_BA:V_BA + 4] = col(inputs["b_a"][0], 4)
    shared[:, V_BI:V_BI + 4] = col(inputs["b_i"][0], 4)
    shared[:, V_LAM:V_LAM + 4] = col(inputs["lru_lambda"][0], 4)
    rcw = f(inputs["rnn_conv_w"][0])
    for cc in range(4):
        for k in range(4):
            shared[:, V_RCW + 4 * cc + k] = rcw[k, cc * 128:(cc + 1) * 128]
    cw = f(inputs["conv_w"][0])
    dcw = np.zeros((4, 128, NCONV, 128), np.float32)
    ar = np.arange(128)
    for cc in range(4):
        for k in range(NCONV):
            dcw[cc, ar, k, ar] = cw[k, cc * 128:(cc + 1) * 128]
    dcw = dcw.reshape(512, NCONV * 128)
    def bd(w):
        w = f(w)
        o = np.zeros((128, 4, 128), np.float32)
        for cc in range(4):
            o[0:64, cc, 0:64] = w[2 * cc]
            o[64:128, cc, 64:128] = w[2 * cc + 1]
        return o.reshape(128, 512)
    wabd = bd(inputs["w_a"][0]); wibd = bd(inputs["w_i"][0])
    common = {
        "ident": np.eye(128, dtype=np.float32), "wabd": wabd, "wibd": wibd,
        "w_mod": f(inputs["w_mod"][0]), "w_fmod": f(inputs["w_fmod"]),
        "w1i": f(inputs["w_ffn1_in"][0]), "w1o": f(inputs["w_ffn1_out"][0]),
        "win": f(inputs["w_in"][0]), "wout": f(inputs["w_out"][0]),
        "w2i": f(inputs["w_ffn2_in"][0]), "w2o": f(inputs["w_ffn2_out"][0]),
        "dcw": dcw,
    }
    in_maps = []
    S = n_tiles * T
    for b in range(cores):
        v = shared.copy()
        v[:, V_C:V_C + 8] = c[b].reshape(8, 128).T
        m = dict(common)
        m["vecs"] = v
        m["x"] = np.ascontiguousarray(x[b, :S, :].T)
        in_maps.append(m)
    return in_maps


_NC_CACHE = {}


def kernel(**inputs):
    n_tiles = SEQ // T
    if n_tiles not in _NC_CACHE:
        _NC_CACHE[n_tiles] = build_program(n_tiles)
    nc = _NC_CACHE[n_tiles]
    in_maps = prep_inputs(inputs, n_tiles, N_CORES)
    res = run_bass_kernel_spmd(nc, in_maps, core_ids=list(range(N_CORES)))
    out = np.stack([np.ascontiguousarray(np.asarray(r["y"], dtype=np.float32).T) for r in res.results], axis=0)
    return out
```
